# Optimizing a Trainium2 kernel written in Bass

```python
import jax, jax.numpy as jnp
from jax import lax
import numpy as np

D_MODEL = 1024
BATCH = 16
SEQ = 2048
DEPTH = 2
DEC_BATCH = 32
DEC_SEQ = 32
PAST_LEN = 2048

CHUNK = 64
N_EVEN = (DEPTH + 1) // 2
N_ODD = DEPTH // 2
EPS = 1e-6
FFN_SCALE = 0.5
D_FF = 2816
H_A = 8
DK_A = D_MODEL // H_A
DV_A = D_MODEL // H_A
HGRN_BLOCK = 16
H_B = 8
DH_B = D_MODEL // H_B
Q_BLOCK = 128
GMLP_LEN = 128
D_C = 2 * D_MODEL
G_C = 8
DG_C = D_C // G_C
EVEN_SIZES = (H_A * DK_A, H_A * DK_A, H_A * DV_A, H_A * DV_A, H_B * DH_B, H_B * DH_B, H_B * DH_B)
D_IN_EVEN = 2 * H_A * DK_A + 2 * H_A * DV_A + 3 * H_B * DH_B
D_MIX_EVEN = H_A * DV_A + H_B * DH_B

kernel_name = 'hybrid_hgrn2_stickbreak_gmlp_stream_step'


def _split_points(sizes):
    pts, acc = [], 0
    for s in sizes[:-1]:
        acc += s
        pts.append(acc)
    return pts


def _rms(x, g):
    xf = x.astype(jnp.float32)
    y = xf * lax.rsqrt(jnp.mean(jnp.square(xf), axis=-1, keepdims=True) + EPS)
    return (y * g.astype(jnp.float32)).astype(x.dtype)


def _layernorm(x, g, b):
    xf = x.astype(jnp.float32)
    xc = xf - jnp.mean(xf, axis=-1, keepdims=True)
    y = xc * lax.rsqrt(jnp.mean(jnp.square(xc), axis=-1, keepdims=True) + EPS)
    return (y * g.astype(jnp.float32) + b.astype(jnp.float32)).astype(x.dtype)


def _swiglu(x, w_gate, w_up, w_down):
    return (jax.nn.silu(x @ w_gate) * (x @ w_up)) @ w_down


def _hgrn_lower_bounds(lb_param):
    p = jax.nn.softmax(lb_param.astype(jnp.float32), axis=0)
    return jnp.cumsum(p, axis=0)[:-1]


def _hgrn2(q, k, v, log_f, s0):
    B, T, H, _ = q.shape
    L = HGRN_BLOCK
    n_blk = -(-T // L)
    pad = n_blk * L - T

    def prep(a):
        a = jnp.pad(a, ((0, 0), (0, pad), (0, 0), (0, 0)))
        return jnp.moveaxis(a.reshape(B, n_blk, L, H, a.shape[-1]), 1, 0)

    causal = jnp.tril(jnp.ones((L, L), dtype=bool))

    def step(S, blk):
        qb, kb, vb, lfb = blk
        c = jnp.cumsum(lfb, axis=1)
        diff = c[:, :, None] - c[:, None, :]
        decay = jnp.exp(jnp.where(causal[None, :, :, None, None], diff, -jnp.inf))
        scores = jnp.einsum('bthc,btshc,bshc->bhts', qb, decay, kb)
        o = jnp.einsum('bhts,bshd->bthd', scores, vb) + jnp.einsum('bthc,bhcd->bthd', qb * jnp.exp(c), S)
        c_last = c[:, -1]
        k_dec = kb * jnp.exp(c_last[:, None] - c)
        S_new = jnp.exp(c_last)[..., None] * S + jnp.einsum('bshc,bshd->bhcd', k_dec, vb)
        return S_new, o

    S_fin, o = lax.scan(step, s0, (prep(q), prep(k), prep(v), prep(log_f)))
    o = jnp.moveaxis(o, 0, 1).reshape(B, n_blk * L, H, v.shape[-1])[:, :T]
    return o, S_fin


def _stick_breaking(q, q_pos, k, v, k_pos):
    z = jnp.einsum('bqhd,bkhd->bhqk', q, k, preferred_element_type=jnp.float32) * (DH_B ** -0.5)
    mask = (k_pos[None, :] < q_pos[:, None])[None, None]
    log_stay = jnp.where(mask, jax.nn.log_sigmoid(-z), 0.0)
    log_pass = lax.cumsum(log_stay, axis=3, reverse=True) - log_stay
    w = jnp.where(mask, jnp.exp(jax.nn.log_sigmoid(z) + log_pass), 0.0)
    return jnp.einsum('bhqk,bkhd->bqhd', w.astype(v.dtype), v, preferred_element_type=jnp.float32)


def _mixer_ab(h, w_in, lb, g_out, g_q, g_k, w_out, s0, k_past, v_past):
    B, T, _ = h.shape
    f32 = jnp.float32
    qa, fa, va, ga, qb, kb, vb = jnp.split(h @ w_in, _split_points(EVEN_SIZES), axis=-1)
    heads_a = lambda a: a.reshape(B, T, H_A, -1)
    f = lb + (1.0 - lb) * jax.nn.sigmoid(fa.astype(f32))
    if s0 is None:
        s0 = jnp.zeros((B, H_A, DK_A, DV_A), f32)
    o_a, s_new = _hgrn2(heads_a(jax.nn.silu(qa).astype(f32)), heads_a(1.0 - f),
                        heads_a(va.astype(f32)), heads_a(jnp.log(f)), s0.astype(f32))
    o_a = _rms(o_a, g_out) * jax.nn.sigmoid(heads_a(ga.astype(f32)))
    qb = _rms(qb.reshape(B, T, H_B, DH_B), g_q)
    kb = _rms(kb.reshape(B, T, H_B, DH_B), g_k)
    vb = vb.reshape(B, T, H_B, DH_B)
    if k_past is None:
        nq = T // Q_BLOCK
        q_blocks = jnp.moveaxis(qb.reshape(B, nq, Q_BLOCK, H_B, DH_B), 1, 0)
        q_pos = jnp.arange(T).reshape(nq, Q_BLOCK)
        k_pos = jnp.arange(T)
        o_b = lax.map(lambda a: _stick_breaking(a[0], a[1], kb, vb, k_pos), (q_blocks, q_pos))
        o_b = jnp.moveaxis(o_b, 0, 1).reshape(B, T, H_B, DH_B)
    else:
        P = k_past.shape[1]
        k_all = jnp.concatenate([k_past.astype(kb.dtype), kb], axis=1)
        v_all = jnp.concatenate([v_past.astype(vb.dtype), vb], axis=1)
        o_b = _stick_breaking(qb, P + jnp.arange(T), k_all, v_all, jnp.arange(P + T))
    mix = jnp.concatenate([o_a.reshape(B, T, -1).astype(h.dtype), o_b.reshape(B, T, -1).astype(h.dtype)], axis=-1)
    return mix @ w_out, s_new.astype(h.dtype), kb, vb


def _mixer_c(h, w_in, ln_g, ln_b, w_s, b_s, w_out):
    B, T, _ = h.shape
    u, v = jnp.split(jax.nn.gelu(h @ w_in), 2, axis=-1)
    v = _layernorm(v, ln_g, ln_b)
    L = min(T, GMLP_LEN)
    n = T // L
    pos = jnp.arange(L)
    mask = (pos[None, :] // CHUNK) <= (pos[:, None] // CHUNK)
    ws = jnp.where(mask[None], w_s[:, :L, :L], 0.0).astype(v.dtype)
    vg = v.reshape(B, n, L, G_C, DG_C)
    mixed = jnp.einsum('gij,bnjgc->bnigc', ws, vg) + b_s[:, :L].T[None, None, :, :, None].astype(v.dtype)
    return (u * mixed.reshape(B, T, D_C)) @ w_out, v


def _trunk(x, p, hgrn_state, sb_k, sb_v):
    lbs = _hgrn_lower_bounds(p['ab_lb'])
    new_k, new_v, new_s, new_cv = [], [], [], []
    for l in range(DEPTH):
        x = x + FFN_SCALE * _swiglu(_rms(x, p['ffn1_norm'][l]), p['ffn1_w_gate'][l], p['ffn1_w_up'][l], p['ffn1_w_down'][l])
        h = _rms(x, p['mix_norm'][l])
        if l % 2 == 0:
            e = l // 2
            y, s, k, v = _mixer_ab(h, p['ab_w_in'][e], lbs[e], p['ab_g_out'][e], p['ab_g_q'][e], p['ab_g_k'][e],
                                   p['ab_w_out'][e],
                                   None if hgrn_state is None else hgrn_state[e],
                                   None if sb_k is None else sb_k[e],
                                   None if sb_v is None else sb_v[e])
            new_s.append(s)
            new_k.append(k)
            new_v.append(v)
        else:
            o = l // 2
            y, cv = _mixer_c(h, p['c_w_in'][o], p['c_ln_g'][o], p['c_ln_b'][o], p['c_w_s'][o], p['c_b_s'][o], p['c_w_out'][o])
            new_cv.append(cv)
        x = x + y
        x = x + FFN_SCALE * _swiglu(_rms(x, p['ffn2_norm'][l]), p['ffn2_w_gate'][l], p['ffn2_w_up'][l], p['ffn2_w_down'][l])
    return x, new_k, new_v, new_s, new_cv


def setup_inputs(seed: int = 0) -> dict:
    key = jax.random.key(seed)
    ks = iter(jax.random.split(key, 40))
    nrm = lambda shape, scale: jax.random.normal(next(ks), shape, jnp.float32) * scale
    gain = lambda shape: 1.0 + nrm(shape, 0.02)
    return {
        'x_prompt': nrm((BATCH, SEQ, D_MODEL), 1.0),
        'x_sample': nrm((DEC_BATCH, DEC_SEQ, D_MODEL), 1.0),
        'cache_sb_k': nrm((N_EVEN, DEC_BATCH, PAST_LEN, H_B, DH_B), 1.0),
        'cache_sb_v': nrm((N_EVEN, DEC_BATCH, PAST_LEN, H_B, DH_B), 1.0),
        'state_hgrn': nrm((N_EVEN, DEC_BATCH, H_A, DK_A, DV_A), 0.5),
        'ffn1_norm': gain((DEPTH, D_MODEL)),
        'ffn1_w_gate': nrm((DEPTH, D_MODEL, D_FF), D_MODEL ** -0.5),
        'ffn1_w_up': nrm((DEPTH, D_MODEL, D_FF), D_MODEL ** -0.5),
        'ffn1_w_down': nrm((DEPTH, D_FF, D_MODEL), D_FF ** -0.5),
        'mix_norm': gain((DEPTH, D_MODEL)),
        'ffn2_norm': gain((DEPTH, D_MODEL)),
        'ffn2_w_gate': nrm((DEPTH, D_MODEL, D_FF), D_MODEL ** -0.5),
        'ffn2_w_up': nrm((DEPTH, D_MODEL, D_FF), D_MODEL ** -0.5),
        'ffn2_w_down': nrm((DEPTH, D_FF, D_MODEL), D_FF ** -0.5),
        'ab_w_in': nrm((N_EVEN, D_MODEL, D_IN_EVEN), D_MODEL ** -0.5),
        'ab_lb': nrm((N_EVEN + 1, H_A * DK_A), 0.1),
        'ab_g_out': gain((N_EVEN, H_A, DV_A)),
        'ab_g_q': gain((N_EVEN, DH_B)),
        'ab_g_k': gain((N_EVEN, DH_B)),
        'ab_w_out': nrm((N_EVEN, D_MIX_EVEN, D_MODEL), D_MIX_EVEN ** -0.5),
        'c_w_in': nrm((N_ODD, D_MODEL, 2 * D_C), D_MODEL ** -0.5),
        'c_ln_g': gain((N_ODD, D_C)),
        'c_ln_b': nrm((N_ODD, D_C), 0.02),
        'c_w_s': nrm((N_ODD, G_C, GMLP_LEN, GMLP_LEN), GMLP_LEN ** -0.5),
        'c_b_s': 1.0 + nrm((N_ODD, G_C, GMLP_LEN), 0.1),
        'c_w_out': nrm((N_ODD, D_C, D_MODEL), D_C ** -0.5),
    }


def reference(x_prompt, x_sample, cache_sb_k, cache_sb_v, state_hgrn, ffn1_norm, ffn1_w_gate, ffn1_w_up,
              ffn1_w_down, mix_norm, ffn2_norm, ffn2_w_gate, ffn2_w_up, ffn2_w_down, ab_w_in, ab_lb, ab_g_out,
              ab_g_q, ab_g_k, ab_w_out, c_w_in, c_ln_g, c_ln_b, c_w_s, c_b_s, c_w_out):
    p = {'ffn1_norm': ffn1_norm, 'ffn1_w_gate': ffn1_w_gate, 'ffn1_w_up': ffn1_w_up, 'ffn1_w_down': ffn1_w_down,
         'mix_norm': mix_norm, 'ffn2_norm': ffn2_norm, 'ffn2_w_gate': ffn2_w_gate, 'ffn2_w_up': ffn2_w_up,
         'ffn2_w_down': ffn2_w_down, 'ab_w_in': ab_w_in, 'ab_lb': ab_lb, 'ab_g_out': ab_g_out, 'ab_g_q': ab_g_q,
         'ab_g_k': ab_g_k, 'ab_w_out': ab_w_out, 'c_w_in': c_w_in, 'c_ln_g': c_ln_g, 'c_ln_b': c_ln_b,
         'c_w_s': c_w_s, 'c_b_s': c_b_s, 'c_w_out': c_w_out}
    y_prompt, pk, pv, ps, _ = _trunk(x_prompt, p, None, None, None)
    y_sample, sk, sv, ss, scv = _trunk(x_sample, p, state_hgrn, cache_sb_k, cache_sb_v)
    return (y_prompt, y_sample, jnp.stack(pk), jnp.stack(pv), jnp.stack(ps),
            jnp.stack(sk), jnp.stack(sv), jnp.stack(ss), jnp.stack(scv))
```

```python
import contextlib
import numpy as np
import concourse.bass as bass
import concourse.mybir as mybir
from concourse.bass_utils import run_bass_kernel_spmd

import os as _os
DBG = set(_os.environ.get("KDBG", "ffn1,norm,ab,c,ffn2,prompt,sample").split(","))
F32 = mybir.dt.float32
F32R = mybir.dt.float32r
BF16 = mybir.dt.bfloat16
AF = mybir.ActivationFunctionType
ALU = mybir.AluOpType

D = 1024
KC = 8
DFF = 2816
H = 8
DH = 128
DC = 2048
EPS = 1e-6
ENGS = ("pe", "act", "dve", "pool", "sp")
DMA_K = 8


class Buf:
    __slots__ = ("name", "w", "r", "excl")

    def __init__(self, name="", excl=False):
        self.name = name
        self.w = None
        self.r = {}
        self.excl = excl


class Prog:
    def __init__(self, nc):
        self.nc = nc
        self.st = contextlib.ExitStack()
        self.ops = {e: [] for e in ENGS}
        self.cnt = {e: 0 for e in ENGS}
        self.seen = {e: {} for e in ENGS}
        self.sems = {}
        for e in ENGS:
            self.sems[e] = self.st.enter_context(nc.semaphore("s_" + e))
        self.dma_i = {}
        for q in ("sp", "act", "pool"):
            self.dma_i[q] = 0
            for k in range(DMA_K):
                self.sems[("d", q, k)] = self.st.enter_context(nc.semaphore(f"d_{q}_{k}"))
        self.n_ins = 0
        self.n_wait = 0
        self.marks = []

    def mark(self, label):
        self.marks.append((label, dict(self.cnt)))

    def sb(self, name, shape, dt=F32, stack=None):
        return (stack or self.st).enter_context(self.nc.sbuf_tensor(name, list(shape), dt))

    def ps(self, name, shape, dt=F32):
        return self.st.enter_context(self.nc.psum_tensor(name, list(shape), dt))

    def _collect(self, eng, reads, writes, extra=()):
        waits = {}
        seen = self.seen[eng]

        def need(k, v):
            if k == eng and eng == "pe":
                return
            if seen.get(k, 0) < v and waits.get(k, 0) < v:
                waits[k] = v

        for b in reads:
            if b.w is not None:
                need(*b.w)
            if b.excl:
                for k, v in b.r.items():
                    if k != eng:
                        need(k, v)
        for b in writes:
            if b.w is not None:
                need(*b.w)
            for k, v in b.r.items():
                need(k, v)
        for k, v in extra:
            need(k, v)
        for k, v in waits.items():
            seen[k] = v
        return [(self.sems[k], v) for k, v in waits.items()]

    def op(self, eng, fn, reads=(), writes=()):
        waits = self._collect(eng, reads, writes)
        self.cnt[eng] += 1
        tok = (eng, self.cnt[eng])
        sem = self.sems[eng]
        self.n_wait += len(waits)
        self.n_ins += 1

        def run(e, waits=waits, fn=fn, sem=sem):
            for s, v in waits:
                e.wait_ge(s, v)
            fn(e).then_inc(sem, 1)

        self.ops[eng].append(run)
        for b in writes:
            b.w = tok
            b.r = {}
        for b in reads:
            b.r[eng] = tok[1]
        return tok

    def dma(self, q, out, in_, reads=(), writes=(), **kw):
        i = self.dma_i[q]
        self.dma_i[q] = i + 1
        slot, gen = i % DMA_K, i // DMA_K
        key = ("d", q, slot)
        extra = [(key, 16 * gen)] if gen > 0 else []
        waits = self._collect(q, reads, writes, extra)
        sem = self.sems[key]
        tok = (key, 16 * (gen + 1))
        self.n_wait += len(waits)
        self.n_ins += 1

        def run(e, waits=waits, sem=sem, out=out, in_=in_, kw=kw):
            for s, v in waits:
                e.wait_ge(s, v)
            e.dma_start(out=out, in_=in_, **kw).then_inc(sem, 16)

        self.ops[q].append(run)
        for b in writes:
            b.w = tok
            b.r = {}
        for b in reads:
            b.r[key] = tok[1]
        return tok

    def _all_tokens(self, dma_queues=("sp", "act", "pool")):
        toks = []
        for q in dma_queues:
            n = self.dma_i[q]
            for slot in range(min(n, DMA_K)):
                gens = (n - 1 - slot) // DMA_K + 1
                toks.append((("d", q, slot), 16 * gens))
        for e in ("pe", "act", "dve", "pool"):
            if self.cnt[e] > 0:
                toks.append((e, self.cnt[e]))
        return toks

    def barrier(self, engines=("pe", "act", "dve", "pool"), dma_queues=("act", "pool")):
        toks = self._all_tokens(dma_queues)
        for eng in engines:
            seen = self.seen[eng]
            waits = []
            for k, v in toks:
                if k == eng:
                    continue
                if seen.get(k, 0) < v:
                    seen[k] = v
                    waits.append((self.sems[k], v))

            def run(e, waits=waits):
                for s, v in waits:
                    e.wait_ge(s, v)

            self.ops[eng].append(run)

    def finish(self):
        fin = [(self.sems[k], v) for k, v in self._all_tokens()]

        def run_fin(e, fin=fin):
            for s, v in fin:
                e.wait_ge(s, v)

        self.ops["sp"].append(run_fin)
        nc, ops = self.nc, self.ops
        with nc.Block() as block:
            @block.tensor
            def _(e):
                for f in ops["pe"]:
                    f(e)

            @block.scalar
            def _(e):
                for f in ops["act"]:
                    f(e)

            @block.vector
            def _(e):
                for f in ops["dve"]:
                    f(e)

            @block.gpsimd
            def _(e):
                for f in ops["pool"]:
                    f(e)

            @block.sync
            def _(e):
                for f in ops["sp"]:
                    f(e)
        self.st.close()


class Slot:
    __slots__ = ("t", "b", "ring", "i")

    def __init__(self, t, b, ring, i):
        self.t, self.b, self.ring, self.i = t, b, ring, i

    def free(self):
        self.ring.freelist.append(self.i)


class Ring:
    def __init__(self, tensors, name, excl=False):
        self.name = name
        self.slots = [Slot(t, Buf(f"{name}{i}", excl), self, i) for i, t in enumerate(tensors)]
        self.freelist = list(range(len(tensors)))

    def alloc(self):
        assert self.freelist, f"ring {self.name} exhausted"
        return self.slots[self.freelist.pop(0)]

    def extend(self, tensors):
        for t in tensors:
            i = len(self.slots)
            self.slots.append(Slot(t, Buf(f"{self.name}{i}", self.slots[0].b.excl), self, i))
            self.freelist.append(i)

    def shrink(self, n):
        for _ in range(n):
            i = len(self.slots) - 1
            assert i in self.freelist, f"ring {self.name}: slot {i} still in use"
            self.freelist.remove(i)
            self.slots.pop()


C_ID, C_ONES, C_NTRI, C_NONES, C_HMP, C_HMS, C_RMS, C_SCS, C_GMP, C_SCP, C_AM, C_AMS = (
    0, 128, 256, 384, 512, 640, 768, 772, 900, 1028, 1540, 3588)
C_RES = 900
NCONST = 3588 + 128


def make_consts():
    c = np.zeros((128, NCONST), np.float32)
    i = np.arange(128)
    c[:, C_ID:C_ID + 128] = np.eye(128)
    c[:, C_ONES:C_ONES + 128] = 1.0
    c[:, C_NTRI:C_NTRI + 128] = -1.0 * (i[:, None] >= i[None, :])
    c[:, C_NONES:C_NONES + 128] = -1.0
    c[:, C_HMP:C_HMP + 128] = (i[:, None] // 64 == i[None, :] // 64) & (i[:, None] <= i[None, :])
    c[:, C_HMS:C_HMS + 128] = (i[:, None] // 32 == i[None, :] // 32) & (i[:, None] <= i[None, :])
    c[:, C_GMP:C_GMP + 128] = (i[None, :] // 64) <= (i[:, None] // 64)
    c[:, C_RMS:C_RMS + 4] = (i[:, None] // 32 == np.arange(4)[None, :])
    t = np.arange(512)
    c[:, C_SCP:C_SCP + 512] = (t % 64 != 0)[None, :]
    c[:, C_SCS:C_SCS + 128] = (np.arange(128) % 32 != 0)[None, :]
    for kd in range(4):
        c[:, C_AM + kd * 512:C_AM + (kd + 1) * 512] = (t[None, :] - 128 * kd) > i[:, None]
    for b in range(4):
        c[:, C_AMS + b * 32:C_AMS + (b + 1) * 32] = (i[:, None] // 32 == b) & ((i[:, None] % 32) < np.arange(32)[None, :])
    return c


class Builder:
    def __init__(self, NP, SEQ, NS, DS, PAST, n_wbf=6):
        self.NP, self.SEQ, self.NS, self.DS, self.PAST = NP, SEQ, NS, DS, PAST
        assert NS * DS == 128 and DS == 32
        nc = self.nc = bass.Bass("TRN2", target_bir_lowering=False)
        P = self.P = Prog(nc)

        def din(name, shape):
            return nc.dram_tensor(name, list(shape), F32, kind="ExternalInput").ap()

        def dout(name, shape):
            return nc.dram_tensor(name, list(shape), F32, kind="ExternalOutput").ap()

        self.i = dict(
            xp=din("xp", [NP, SEQ, D]), xs=din("xs", [128, D]),
            ck=din("ck", [NS, PAST, H, DH]), cv=din("cv", [NS, PAST, H, DH]), sh=din("sh", [NS, H, 128, 128]),
            consts=din("consts", [128, NCONST]),
            ffn1_norm=din("ffn1_norm", [2, D]), mix_norm=din("mix_norm", [2, D]), ffn2_norm=din("ffn2_norm", [2, D]),
            ffn1_w_gate=din("ffn1_w_gate", [2, D, DFF]), ffn1_w_up=din("ffn1_w_up", [2, D, DFF]),
            ffn1_w_down=din("ffn1_w_down", [2, DFF, D]),
            ffn2_w_gate=din("ffn2_w_gate", [2, D, DFF]), ffn2_w_up=din("ffn2_w_up", [2, D, DFF]),
            ffn2_w_down=din("ffn2_w_down", [2, DFF, D]),
            ab_w_in=din("ab_w_in", [1, D, 7 * D]), ab_lb=din("ab_lb", [2, D]), ab_g_out=din("ab_g_out", [1, H, 128]),
            ab_g_q=din("ab_g_q", [1, 128]), ab_g_k=din("ab_g_k", [1, 128]), ab_w_out=din("ab_w_out", [1, 2 * D, D]),
            c_w_in=din("c_w_in", [1, D, 2 * DC]), c_ln_g=din("c_ln_g", [1, DC]), c_ln_b=din("c_ln_b", [1, DC]),
            c_w_s=din("c_w_s", [1, 8, 128, 128]), c_b_s=din("c_b_s", [1, 8, 128]), c_w_out=din("c_w_out", [1, DC, D]),
        )
        self.o = dict(
            yp=dout("yp", [NP, SEQ, D]), ys=dout("ys", [128, D]),
            kp=dout("kp", [NP, SEQ, H, DH]), vp=dout("vp", [NP, SEQ, H, DH]), hp=dout("hp", [NP, H, 128, 128]),
            ks=dout("ks", [128, H, DH]), vs=dout("vs", [128, H, DH]), hs=dout("hs", [NS, H, 128, 128]),
            gv=dout("gv", [128, DC]),
        )
        self.banks = Ring([P.ps(f"bk{i}", [128, 512]) for i in range(8)], "bk", excl=True)
        self.wst = Ring([P.sb(f"wst{i}", [128, 2048]) for i in range(2)], "wst")
        self.wst_parts = [[Buf() for _ in range(4)] for _ in range(2)]
        self.wbf = Ring([P.sb(f"wbf{i}", [128, 2048], BF16) for i in range(n_wbf)], "wbf")
        self.F = Ring([P.sb(f"F{i}", [128, 512]) for i in range(6)], "F")
        self.Hh = Ring([P.sb(f"H{i}", [128, 512], BF16) for i in range(12)], "H")
        self.cA = P.sb("cA", [128, C_RES])
        self.scanb = P.sb("scanb", [128, 512], BF16)
        self.ntri = P.sb("ntri", [128, 256], F32R)
        self.FR = Ring([P.sb(f"FR{i}", [128, 512], F32R) for i in range(3)], "FR")
        self.amask = P.sb("amask", [128, 4, 512], BF16)
        self.amask_s = P.sb("amask_s", [128, 4, 32], BF16)
        self.identb = P.sb("identb", [128, 128], BF16)
        self.onesb = P.sb("onesb", [128, 128], BF16)
        self.prm = P.sb("prm", [128, 104])
        self.lbp = P.sb("lbp", [128, 3, 8])
        self.gqk = P.sb("gqk", [128, 2, 128])
        self.cb = Buf("consts")
        self.setup_consts()

    def mm(self, out, lhsT, rhs, start, stop, rd, wr, **kw):
        self.P.op("pe", lambda e: e.matmul(out, lhsT=lhsT, rhs=rhs, start=start, stop=stop, **kw), reads=rd, writes=wr)

    def tp(self, out, in_, bf, rd, wr):
        ident = self.identb[:] if bf else self.cA[:, C_ID:C_ID + 128]
        pin = in_.partition_size()
        ident = ident[0:pin, 0:pin]
        self.P.op("pe", lambda e: e.transpose(out=out, in_=in_, identity=ident), reads=list(rd) + [self.cb], writes=wr)

    def act(self, out, in_, func, rd, wr, **kw):
        self.P.op("act", lambda e: e.activation(out=out, in_=in_, func=func, **kw), reads=rd, writes=wr)

    def tt(self, eng, out, in0, in1, op, rd, wr):
        self.P.op(eng, lambda e: e.tensor_tensor(out=out, in0=in0, in1=in1, op=op), reads=rd, writes=wr)

    def ts(self, eng, out, in0, s1, s2, op0, op1, rd, wr):
        self.P.op(eng, lambda e: e.tensor_scalar(out=out, in0=in0, scalar1=s1, scalar2=s2, op0=op0, op1=op1), reads=rd, writes=wr)

    def stt(self, eng, out, in0, scalar, in1, op0, op1, rd, wr):
        self.P.op(eng, lambda e: e.scalar_tensor_tensor(out=out, in0=in0, scalar=scalar, in1=in1, op0=op0, op1=op1),
                  reads=rd, writes=wr)

    def cp(self, eng, out, in_, rd, wr):
        if eng == "act":
            self.P.op(eng, lambda e: e.activation(out=out, in_=in_, func=AF.Copy), reads=rd, writes=wr)
        else:
            self.P.op(eng, lambda e: e.tensor_copy(out=out, in_=in_), reads=rd, writes=wr)

    def ts1(self, eng, out, in_, scalar, op, rd, wr):
        self.P.op(eng, lambda e: e.tensor_single_scalar(out=out, in_=in_, scalar=scalar, op=op), reads=rd, writes=wr)

    def const(self, off, n=128, rows=128):
        return self.cA[0:rows, off:off + n]

    def wload(self, srcs, total, cast_eng="act"):
        if cast_eng == "alt":
            cast_eng = "act"
        st = self.wst.alloc()
        parts = self.wst_parts[st.i]
        for pi, (off, n, shape_fn, src) in enumerate(srcs):
            dst = st.t[:, off:off + n]
            if shape_fn is not None:
                dst = shape_fn(dst)
            self.P.dma("sp", dst, src, writes=[parts[pi]])
        wb = self.wbf.alloc()
        self.cp(cast_eng, wb.t[:, 0:total], st.t[:, 0:total], rd=parts, wr=[wb.b])
        st.free()
        return wb

    def setup_consts(self):
        P, I = self.P, self.i
        cb = self.cb
        P.dma("sp", self.cA[:], I["consts"][:, 0:C_RES], writes=[cb])
        st = self.wst.alloc()
        tb = Buf()
        P.dma("sp", st.t[:, 0:2048], I["consts"][:, C_AM:C_AM + 2048], writes=[tb])
        self.ts("dve", self.amask[:].rearrange("p a b -> p (a b)"), st.t[:, 0:2048], 30000.0, -30000.0, ALU.mult, ALU.add,
                rd=[tb], wr=[cb])
        st2 = self.wst.alloc()
        tb2 = Buf()
        P.dma("sp", st2.t[:, 0:128], I["consts"][:, C_AMS:C_AMS + 128], writes=[tb2])
        P.dma("sp", st2.t[:, 128:640], I["consts"][:, C_SCP:C_SCP + 512], writes=[tb2])
        self.cp("dve", self.scanb[:], st2.t[:, 128:640], rd=[tb2], wr=[cb])
        self.ts("dve", self.amask_s[:].rearrange("p a b -> p (a b)"), st2.t[:, 0:128], 30000.0, -30000.0, ALU.mult, ALU.add,
                rd=[tb2], wr=[cb])
        self.cp("dve", self.ntri[:], self.cA[:, C_NTRI:C_NTRI + 256], rd=[cb], wr=[cb])
        self.cp("dve", self.identb[:], self.cA[:, C_ID:C_ID + 128], rd=[cb], wr=[cb])
        self.cp("dve", self.onesb[:], self.cA[:, C_ONES:C_ONES + 128], rd=[cb], wr=[cb])
        tmpstk = contextlib.ExitStack()
        if "nos2" in DBG:
            st.free(); st2.free(); return
        rowt = P.sb("prmrows", [128, 128], stack=tmpstk)
        rows = rowt[0:104, 0:128]
        tb3 = Buf()

        def ld(r0, n, src):
            P.dma("sp", rowt[r0:r0 + n, 0:128], src, writes=[tb3])

        ld(0, 16, I["ffn1_norm"].rearrange("l (c k) -> (l c) k", k=128))
        ld(16, 16, I["mix_norm"].rearrange("l (c k) -> (l c) k", k=128))
        ld(32, 16, I["ffn2_norm"].rearrange("l (c k) -> (l c) k", k=128))
        ld(48, 16, I["ab_lb"].rearrange("l (c k) -> (l c) k", k=128))
        ld(64, 8, I["ab_g_out"][0])
        ld(72, 16, I["c_ln_g"].rearrange("l (c k) -> (l c) k", k=128))
        ld(88, 16, I["c_ln_b"].rearrange("l (c k) -> (l c) k", k=128))
        bk = self.banks.alloc()
        self.tp(bk.t[:, 0:104], rows, False, rd=[tb3], wr=[bk.b])
        self.cp("dve", self.prm[:], bk.t[:, 0:104], rd=[bk.b], wr=[cb])
        bk.free()
        self.tt("dve", self.lbp[:, 0, :], self.prm[:, 48:56], self.prm[:, 56:64], ALU.subtract, rd=[cb], wr=[cb])
        self.act(self.lbp[:, 0, :], self.lbp[:, 0, :], AF.Sigmoid, rd=[cb], wr=[cb])
        self.ts("dve", self.lbp[:, 1, :], self.lbp[:, 0, :], -1.0, 1.0, ALU.mult, ALU.add, rd=[cb], wr=[cb])
        self.ts("dve", self.lbp[:, 2, :], self.lbp[:, 0, :], 1.0, -1.0, ALU.mult, ALU.add, rd=[cb], wr=[cb])
        if "nos3" in DBG:
            st.free(); st2.free(); return
        P.dma("sp", self.gqk[:, 0, :], I["ab_g_q"][0:1, :].partition_broadcast(128), reads=[cb], writes=[cb])
        P.dma("sp", self.gqk[:, 1, :], I["ab_g_k"][0:1, :].partition_broadcast(128), reads=[cb], writes=[cb])
        self.ts1("dve", self.gqk[:, 0, :], self.gqk[:, 0, :], float(DH ** -0.5), ALU.mult, rd=[cb], wr=[cb])
        st.free()
        st2.free()
        P.barrier(engines=ENGS, dma_queues=("sp", "act", "pool"))
        tmpstk.close()

    def gmlp_consts(self, sample, wsT, BT, stack, tmps=None):
        P, I = self.P, self.i
        cb = self.cb
        sfx = "s" if sample else "p"
        if tmps is not None:
            wsf, wsTf, bsb = tmps
        else:
            wsf = P.sb("gc_wsf" + sfx, [128, 8, 128], stack=stack)
            wsTf = P.sb("gc_wsTf" + sfx, [128, 8, 128], stack=stack)
            bsb = P.sb("gc_bsb" + sfx, [128, 8, 128], stack=stack)
        b1, b2, b3 = Buf(), Buf(), Buf()
        if not sample:
            P.dma("sp", wsf[:], I["c_w_s"][0].rearrange("g i j -> i g j"), writes=[b1])
            gm = bsb[:, 0, :]
            P.dma("sp", gm, I["consts"][:, C_GMP:C_GMP + 128], writes=[b3])
            for g in range(8):
                self.tt("dve", wsf[:, g, :], wsf[:, g, :], gm, ALU.mult, rd=[b1, b3], wr=[b1, b3])
            P.dma("sp", bsb[:].rearrange("p g i -> p (g i)"),
                  I["c_b_s"][0:1].rearrange("o g i -> o (g i)").partition_broadcast(128), writes=[b3])
        else:
            P.op("pool", lambda e: e.memset(wsf[:], 0.0), writes=[b1])
            for b in range(4):
                P.dma("sp", wsf[b * 32:(b + 1) * 32, :, b * 32:(b + 1) * 32],
                      I["c_w_s"][0, :, 0:32, 0:32].rearrange("g i j -> i g j"), reads=[b1], writes=[b1])
                P.dma("sp", bsb[:, :, b * 32:(b + 1) * 32],
                      I["c_b_s"][0:1, :, 0:32].partition_broadcast(128), writes=[b3])
        if "g1" in DBG:
            return
        for g in range(8):
            bk = self.banks.alloc()
            self.tp(bk.t[:, 0:128], wsf[:, g, :], False, rd=[b1], wr=[bk.b])
            self.cp("dve", wsTf[:, g, :], bk.t[:, 0:128], rd=[bk.b], wr=[b2])
            self.cp("act", wsT[:, g, :], bk.t[:, 0:128], rd=[bk.b], wr=[cb])
            bk.free()
        if "g2" in DBG:
            return
        for g in range(8):
            bk = self.banks.alloc()
            self.mm(bk.t[:, 0:128], self.const(C_ONES), wsTf[:, g, :], True, True, rd=[b2, cb], wr=[bk.b])
            for cc in (2 * g, 2 * g + 1):
                self.stt("dve", BT[:, cc, :], bk.t[:, 0:128], self.prm[:, 88 + cc:89 + cc], bsb[:, g, :],
                         ALU.mult, ALU.add, rd=[bk.b, b3, cb], wr=[cb])
            bk.free()

    def load_x(self, ctx, src, xin_ring):
        P = self.P
        T = ctx["T"]
        for ti in range(T // 128):
            g = (ti * 128) // ctx["GN"]
            xin = xin_ring.alloc()
            P.dma("sp", xin.t[:, :], src[ti * 128:(ti + 1) * 128, :], writes=[xin.b])
            for half in range(2):
                bk = self.banks.alloc()
                for j in range(4):
                    c = half * 4 + j
                    self.tp(bk.t[:, j * 128:(j + 1) * 128], xin.t[:, c * 128:(c + 1) * 128], False, rd=[xin.b], wr=[bk.b])
                self.cp("act" if half == 0 else "dve",
                        ctx["xT"][:, half * 4:half * 4 + 4, ti * 128:(ti + 1) * 128],
                        bk.t[:, 0:512].rearrange("p (c t) -> p c t", c=4),
                        rd=[bk.b], wr=[ctx["xb"][g][half * 4 + j] for j in range(4)])
                bk.free()
            xin.free()

    def store_y(self, ctx, dst, ybuf_ring):
        P = self.P
        T = ctx["T"]
        for ti in range(T // 128):
            g = (ti * 128) // ctx["GN"]
            yb = ybuf_ring.alloc()
            for half in range(2):
                bk = self.banks.alloc()
                for j in range(4):
                    c = half * 4 + j
                    self.tp(bk.t[:, j * 128:(j + 1) * 128], ctx["xT"][:, c, ti * 128:(ti + 1) * 128], False,
                            rd=[ctx["xb"][g][c]], wr=[bk.b])
                self.cp("act" if half == 0 else "dve", yb.t[:, half * 512:(half + 1) * 512], bk.t[:, 0:512],
                        rd=[bk.b], wr=[yb.b])
                bk.free()
            P.dma("act", dst[ti * 128:(ti + 1) * 128, :], yb.t[:, :], reads=[yb.b])
            yb.free()

    def rstd_bc(self, src_bank, n, dim, explog=False):
        f = self.F.alloc()
        if explog:
            self.act(f.t[:, 0:n], src_bank.t[:, 0:n], AF.Ln, rd=[src_bank.b], wr=[f.b], scale=1.0 / dim, bias=EPS)
            self.act(f.t[:, 0:n], f.t[:, 0:n], AF.Exp, rd=[f.b], wr=[f.b], scale=-0.5)
        else:
            self.act(f.t[:, 0:n], src_bank.t[:, 0:n], AF.Sqrt, rd=[src_bank.b], wr=[f.b], scale=1.0 / dim, bias=EPS)
            self.P.op("dve", lambda e: e.reciprocal(out=f.t[:, 0:n], in_=f.t[:, 0:n]), reads=[f.b], writes=[f.b])
        return f

    def rmsnorm(self, ctx, gcol0):
        xT, hT = ctx["xT"], ctx["hT"]
        for g, (t0, n) in enumerate(ctx["groups"]):
            bk = self.banks.alloc()
            for c in range(KC):
                sq = self.Hh.alloc()
                self.act(sq.t[:, 0:n], xT[:, c, t0:t0 + n], AF.Square, rd=[ctx["xb"][g][c]], wr=[sq.b])
                self.mm(bk.t[:, 0:n], self.onesb[:], sq.t[:, 0:n], c == 0, c == KC - 1, rd=[sq.b, self.cb], wr=[bk.b])
                sq.free()
            rs = self.rstd_bc(bk, n, D)
            bk.free()
            for c in range(KC):
                self.stt("dve", hT[:, c, t0:t0 + n], xT[:, c, t0:t0 + n],
                         self.prm[:, gcol0 + c:gcol0 + c + 1], rs.t[:, 0:n], ALU.mult, ALU.mult,
                         rd=[ctx["xb"][g][c], rs.b, self.cb], wr=[ctx["hb"][g]])
            rs.free()

    def ffn(self, ctx, Wg, Wu, Wd):
        xT, hT = ctx["xT"], ctx["hT"]
        pending = None

        def down(wd, a, g, t0, n, free_wd):
            for d in range(KC):
                bk = self.banks.alloc()
                for j in range(2):
                    self.mm(bk.t[:, 0:n], wd.t[:, j * 1024 + d * 128: j * 1024 + (d + 1) * 128], a[j].t[:, 0:n],
                            j == 0, j == 1, rd=[wd.b, a[j].b], wr=[bk.b])
                self.stt("dve", xT[:, d, t0:t0 + n], bk.t[:, 0:n], 0.5, xT[:, d, t0:t0 + n], ALU.mult, ALU.add,
                         rd=[bk.b, ctx["xb"][g][d]], wr=[ctx["xb"][g][d]])
                bk.free()
            for j in range(2):
                a[j].free()
            if free_wd:
                wd.free()

        npieces = DFF // 256
        for p in range(npieces):
            f0 = p * 256
            view = lambda ap: ap.rearrange("p (c f) -> p c f", c=KC)
            wg = self.wload([(0, 2048, view, Wg.rearrange("(c k) f -> k c f", k=128)[:, :, f0:f0 + 256])], 2048, "alt")
            wu = self.wload([(0, 2048, view, Wu.rearrange("(c k) f -> k c f", k=128)[:, :, f0:f0 + 256])], 2048, "alt")
            viewd = lambda ap: ap.rearrange("p (j d) -> p j d", j=2)
            wd = self.wload([(0, 2048, viewd, Wd[f0:f0 + 256, :].rearrange("(j f) d -> f j d", f=128))], 2048, "alt")
            for g, (t0, n) in enumerate(ctx["groups"]):
                a = []
                for j in range(2):
                    bg = self.banks.alloc()
                    bu = self.banks.alloc()
                    for c in range(KC):
                        self.mm(bg.t[:, 0:n], wg.t[:, c * 256 + j * 128:c * 256 + (j + 1) * 128], hT[:, c, t0:t0 + n],
                                c == 0, c == KC - 1, rd=[wg.b, ctx["hb"][g]], wr=[bg.b])
                    for c in range(KC):
                        self.mm(bu.t[:, 0:n], wu.t[:, c * 256 + j * 128:c * 256 + (j + 1) * 128], hT[:, c, t0:t0 + n],
                                c == 0, c == KC - 1, rd=[wu.b, ctx["hb"][g]], wr=[bu.b])
                    sg = self.F.alloc()
                    self.act(sg.t[:, 0:n], bg.t[:, 0:n], AF.Silu, rd=[bg.b], wr=[sg.b])
                    bg.free()
                    aj = self.Hh.alloc()
                    self.tt("dve", aj.t[:, 0:n], sg.t[:, 0:n], bu.t[:, 0:n], ALU.mult, rd=[sg.b, bu.b], wr=[aj.b])
                    sg.free()
                    bu.free()
                    a.append(aj)
                if pending is not None:
                    down(*pending)
                last_g = g == len(ctx["groups"]) - 1
                pending = (wd, a, g, t0, n, last_g)
            wg.free()
            wu.free()
        down(*pending)

    def mixer_c(self, ctx, sample, wsT, BT, ar):
        P, I = self.P, self.i
        xT, hT = ctx["xT"], ctx["hT"]
        Win, Wout = I["c_w_in"][0], I["c_w_out"][0]
        view = lambda ap: ap.rearrange("p (c f) -> p c f", c=KC)
        viewd = lambda ap: ap.rearrange("p (j d) -> p j d", j=2)
        vn, vnb, stats, stb = ar["vn"], ar["vnb"], ar["stats"], ar["stb"]
        for g, (t0, n) in enumerate(ctx["groups"]):
            TG = n // 128
            P.op("pool", lambda e: e.memset(stats[:], 0.0), writes=[stb] + list(vnb))
            for u in range(8):
                wv = self.wload([(0, 2048, view, Win.rearrange("(c k) f -> k c f", k=128)[:, :, DC + u * 256:DC + (u + 1) * 256])], 2048, "alt")
                for t in range(TG):
                    bk = self.banks.alloc()
                    for c in range(KC):
                        self.mm(bk.t[:, 0:256], hT[:, c, t0 + t * 128:t0 + (t + 1) * 128], wv.t[:, c * 256:(c + 1) * 256],
                                c == 0, c == KC - 1, rd=[wv.b, ctx["hb"][g]], wr=[bk.b])
                    self.act(vn[:, t, u * 256:(u + 1) * 256], bk.t[:, 0:256], AF.Gelu_apprx_tanh, rd=[bk.b], wr=[vnb[t]],
                             accum_out=stats[:, t, 0, u:u + 1])
                    bk.free()
                    junk = self.F.alloc()
                    self.act(junk.t[:, 0:256], vn[:, t, u * 256:(u + 1) * 256], AF.Square, rd=[vnb[t]], wr=[junk.b, stb],
                             accum_out=stats[:, t, 1, u:u + 1])
                    junk.free()
                wv.free()
            mv = ar["mv"]
            for t in range(TG):
                P.op("dve", lambda e, t=t: e.reduce_sum(out=mv[:, t, 0:2], in_=stats[:, t, :, :], axis=mybir.AxisListType.X),
                     reads=[stb, vnb[t]], writes=[stb])
                self.ts1("dve", mv[:, t, 0:2], mv[:, t, 0:2], 1.0 / DC, ALU.mult, rd=[stb], wr=[stb])
                self.tt("dve", mv[:, t, 2:3], mv[:, t, 0:1], mv[:, t, 0:1], ALU.mult, rd=[stb], wr=[stb])
                self.tt("dve", mv[:, t, 2:3], mv[:, t, 1:2], mv[:, t, 2:3], ALU.subtract, rd=[stb], wr=[stb])
                self.act(mv[:, t, 2:3], mv[:, t, 2:3], AF.Sqrt, rd=[stb], wr=[stb], bias=EPS)
                P.op("dve", lambda e, t=t: e.reciprocal(out=mv[:, t, 3:4], in_=mv[:, t, 2:3]), reads=[stb], writes=[stb])
                self.ts("dve", vn[:, t, :], vn[:, t, :], mv[:, t, 0:1], mv[:, t, 3:4], ALU.subtract, ALU.mult,
                        rd=[stb, vnb[t]], wr=[vnb[t]])
                if sample:
                    vo = ar["vout"]
                    self.tt("dve", vo[:, :], vn[:, t, :], ar["lng"][:, :], ALU.mult, rd=[vnb[t], self.cb], wr=[ar["voutb"]])
                    self.tt("pool", vo[:, :], vo[:, :], ar["lnb"][:, :], ALU.add, rd=[self.cb], wr=[ar["voutb"]])
                    P.dma("act", self.o["gv"][:, :], vo[:, :], reads=[ar["voutb"]])
                    self.cp("pool", ar["vhb"][:, t, :], vn[:, t, :], rd=[vnb[t]], wr=[ar["vhbb"]])
            vh = ar["vhb"] if sample else vn
            vhrd = [ar["vhbb"]] if sample else [vnb[t] for t in range(TG)]
            pending = None

            def down(wo, a, t0=t0, n=n, g=g):
                for d in range(KC):
                    bk = self.banks.alloc()
                    for j in range(2):
                        self.mm(bk.t[:, 0:n], wo.t[:, j * 1024 + d * 128:j * 1024 + (d + 1) * 128], a[j].t[:, 0:n],
                                j == 0, j == 1, rd=[wo.b, a[j].b], wr=[bk.b])
                    self.stt("dve", xT[:, d, t0:t0 + n], bk.t[:, 0:n], 1.0, xT[:, d, t0:t0 + n], ALU.mult, ALU.add,
                             rd=[bk.b, ctx["xb"][g][d]], wr=[ctx["xb"][g][d]])
                    bk.free()
                for j in range(2):
                    a[j].free()
                wo.free()

            for p in range(8):
                wu = self.wload([(0, 2048, view, Win.rearrange("(c k) f -> k c f", k=128)[:, :, p * 256:(p + 1) * 256])], 2048, "alt")
                wo = self.wload([(0, 2048, viewd, Wout[p * 256:(p + 1) * 256, :].rearrange("(j f) d -> f j d", f=128))], 2048, "alt")
                a = []
                for j in range(2):
                    cc = 2 * p + j
                    bu = self.banks.alloc()
                    for c in range(KC):
                        self.mm(bu.t[:, 0:n], wu.t[:, c * 256 + j * 128:c * 256 + (j + 1) * 128], hT[:, c, t0:t0 + n],
                                c == 0, c == KC - 1, rd=[wu.b, ctx["hb"][g]], wr=[bu.b])
                    ug = self.F.alloc()
                    self.act(ug.t[:, 0:n], bu.t[:, 0:n], AF.Gelu_apprx_tanh, rd=[bu.b], wr=[ug.b])
                    bu.free()
                    bm = self.banks.alloc()
                    for t in range(TG):
                        self.mm(bm.t[:, t * 128:(t + 1) * 128], vh[:, t, cc * 128:(cc + 1) * 128], wsT[:, p, :],
                                True, True, rd=vhrd + [self.cb], wr=[bm.b])
                    t1 = self.F.alloc()
                    for t in range(TG):
                        self.stt("dve", t1.t[:, t * 128:(t + 1) * 128], bm.t[:, t * 128:(t + 1) * 128],
                                 self.prm[:, 72 + cc:73 + cc], BT[:, cc, :], ALU.mult, ALU.add,
                                 rd=[bm.b, self.cb], wr=[t1.b])
                    bm.free()
                    aj = self.Hh.alloc()
                    self.tt("dve", aj.t[:, 0:n], t1.t[:, 0:n], ug.t[:, 0:n], ALU.mult, rd=[t1.b, ug.b], wr=[aj.b])
                    t1.free()
                    ug.free()
                    a.append(aj)
                wu.free()
                if pending is not None:
                    down(*pending)
                pending = (wo, a)
            down(*pending)

    def attn_pairs(self, pairs, R, O, filler=None):
        cb = self.cb
        K = len(pairs)
        st = [dict() for _ in pairs]

        def scores(bank, p, last_stop):
            zs, QTv, qb_, avs, S, N, mask = p
            for j, (c0, c1, KTv, kb_) in enumerate(zs):
                self.mm(bank.t[0:S, c0:c1], KTv, QTv[:, c0:c1], j == 0, last_stop and mask is None and j == len(zs) - 1,
                        rd=[kb_, qb_], wr=[bank.b])
            if mask is not None:
                self.mm(bank.t[0:S, 0:N], self.identb[0:S, 0:S], mask, False, last_stop, rd=[cb], wr=[bank.b])

        def A(p, s):
            zs, QTv, qb_, avs, S, N, mask = p
            z = self.banks.alloc()
            scores(z, p, True)
            X = self.F.alloc()
            self.act(X.t[0:S, 0:N], z.t[0:S, 0:N], AF.Exp, rd=[z.b], wr=[X.b])
            z.free()
            E = self.FR.alloc()
            self.act(E.t[0:S, 0:N], X.t[0:S, 0:N], AF.Ln, rd=[X.b], wr=[E.b], bias=1.0)
            X.free()
            s["E"] = E

        def B(p, s, first, last):
            zs, QTv, qb_, avs, S, N, mask = p
            E = s["E"]
            L = self.banks.alloc()
            scores(L, p, False)
            self.mm(L.t[0:S, 0:N], self.ntri[0:S, 0:S], E.t[0:S, 0:N], False, first, rd=[E.b, cb], wr=[L.b])
            if not first:
                self.mm(L.t[0:S, 0:N], self.ntri[:, 128:128 + S], R.t[:, 0:N], False, True, rd=[R.b, cb], wr=[L.b])
            W = self.Hh.alloc()
            self.act(W.t[0:S, 0:N], L.t[0:S, 0:N], AF.Exp, rd=[L.b], wr=[W.b])
            L.free()
            s["W"] = W
            if not last:
                Rv, Ev = R.t[0:S, 0:N], E.t[0:S, 0:N]
                if first:
                    self.cp("dve", Rv, Ev, rd=[E.b], wr=[R.b])
                else:
                    self.tt("dve", Rv, Rv, Ev, ALU.add, rd=[E.b, R.b], wr=[R.b])
            E.free()

        def C(p, s, first, last):
            zs, QTv, qb_, avs, S, N, mask = p
            W = s["W"]
            for j, (c0, c1, Vv, vb_) in enumerate(avs):
                self.mm(O.t[:, c0:c1], Vv, W.t[0:S, c0:c1], first and j == 0, last and j == len(avs) - 1,
                        rd=[vb_, W.b], wr=[O.b])
            W.free()

        for i in range(K + 2):
            if i < K:
                A(pairs[i], st[i])
            if 0 <= i - 1 < K:
                B(pairs[i - 1], st[i - 1], i - 1 == 0, i - 1 == K - 1)
            if 0 <= i - 2 < K:
                C(pairs[i - 2], st[i - 2], i - 2 == 0, i - 2 == K - 1)
            if filler is not None:
                next(filler, None)

    def mixer_ab(self, ctx, sample, ar, seq_out):
        P, I, O_ = self.P, self.i, self.o
        cb = self.cb
        xT, hT = ctx["xT"], ctx["hT"]
        Win, Wout = I["ab_w_in"][0], I["ab_w_out"][0]
        Wv = Win.rearrange("(c k) (t h n) -> k c t h n", k=128, t=7, h=H)
        CH = 32 if sample else 64
        cpt = 128 // CH
        hmask = self.const(C_HMS if sample else C_HMP)
        scanm = self.const(C_SCS, 128) if sample else self.scanb[:, :]
        KT, Vb, kvb = ar["KT"], ar["Vb"], ar["kvb"]
        groups = ctx["groups"]
        NGp = len(groups)
        iters = [(h, g) for h in range(H) for g in range(NGp)]
        NI = len(iters)
        st = [dict() for _ in iters]
        wts = {}
        v2 = lambda ap: ap.rearrange("p (c t n) -> p c t n", c=KC, t=2)[:, :, 0, :]
        v2b = lambda ap: ap.rearrange("p (c t n) -> p c t n", c=KC, t=2)[:, :, 1, :]
        v1 = lambda ap: ap.rearrange("p (c n) -> p c n", c=KC)

        def load_head(h):
            blk = lambda t: Wv[:, :, t, h, :]
            w1 = self.wload([(0, 2048, v2, blk(0)), (0, 2048, v2b, blk(1))], 2048)
            w2 = self.wload([(0, 1024, v1, blk(3))], 1024)
            w3 = self.wload([(0, 2048, v2, blk(2)), (0, 2048, v2b, blk(4))], 2048)
            w4 = self.wload([(0, 2048, v2, blk(5)), (0, 2048, v2b, blk(6))], 2048)
            wo = self.wload([(0, 1024, None, Wout[h * 128:(h + 1) * 128, :]),
                             (1024, 1024, None, Wout[D + h * 128:D + (h + 1) * 128, :])], 2048)
            wts[h] = (w1, w2, w3, w4, wo)

        def P1a(n):
            h, g = iters[n]
            s = st[n]
            t0, nn = groups[g]
            TG = nn // 128
            par = n % 2
            if h not in wts:
                load_head(h)
            w1, w2, w3, w4, wo = wts[h]
            hb = ctx["hb"][g]
            bq, bf_, bg_ = self.banks.alloc(), self.banks.alloc(), self.banks.alloc()
            for c in range(KC):
                self.mm(bq.t[:, 0:nn], w1.t[:, c * 256:c * 256 + 128], hT[:, c, t0:t0 + nn], c == 0, c == KC - 1,
                        rd=[w1.b, hb], wr=[bq.b])
            yield
            for c in range(KC):
                self.mm(bf_.t[:, 0:nn], w1.t[:, c * 256 + 128:c * 256 + 256], hT[:, c, t0:t0 + nn], c == 0, c == KC - 1,
                        rd=[w1.b, hb], wr=[bf_.b])
            for c in range(KC):
                self.mm(bg_.t[:, 0:nn], w2.t[:, c * 128:(c + 1) * 128], hT[:, c, t0:t0 + nn], c == 0, c == KC - 1,
                        rd=[w2.b, hb], wr=[bg_.b])
            yield
            T1, T2, T3, T4, T5 = [self.F.alloc() for _ in range(5)]
            def sigm(T):
                self.ts1("dve", T.t[:, 0:nn], T.t[:, 0:nn], 1.0, ALU.add, rd=[T.b], wr=[T.b])
                P.op("dve", lambda e, T=T: e.reciprocal(out=T.t[:, 0:nn], in_=T.t[:, 0:nn]), reads=[T.b], writes=[T.b])

            self.act(T1.t[:, 0:nn], bf_.t[:, 0:nn], AF.Exp, rd=[bf_.b], wr=[T1.b], scale=-1.0)
            bf_.free()
            self.act(T5.t[:, 0:nn], bg_.t[:, 0:nn], AF.Exp, rd=[bg_.b], wr=[T5.b], scale=-1.0)
            bg_.free()
            sigm(T1)
            self.act(T2.t[:, 0:nn], T1.t[:, 0:nn], AF.Ln, rd=[T1.b, cb], wr=[T2.b],
                     scale=self.lbp[:, 1, h:h + 1], bias=self.lbp[:, 0, h:h + 1])
            P.op("dve", lambda e, T3=T3, T2=T2, nn=nn: e.tensor_tensor_scan(
                out=T3.t[:, 0:nn], data0=scanm[:, 0:nn], data1=T2.t[:, 0:nn], initial=0.0, op0=ALU.mult, op1=ALU.add),
                reads=[T2.b, cb], writes=[T3.b])
            self.act(T4.t[:, 0:nn], T3.t[:, 0:nn], AF.Exp, rd=[T3.b], wr=[T4.b])
            self.act(T2.t[:, 0:nn], T3.t[:, 0:nn], AF.Exp, rd=[T3.b, T2.b], wr=[T2.b], scale=-1.0)
            self.act(T3.t[:, 0:nn], bq.t[:, 0:nn], AF.Exp, rd=[bq.b, T3.b], wr=[T3.b], scale=-1.0)
            sigm(T5)
            self.ts("dve", T1.t[:, 0:nn], T1.t[:, 0:nn], self.lbp[:, 2, h:h + 1], self.lbp[:, 1, h:h + 1],
                    ALU.mult, ALU.add, rd=[T1.b, cb], wr=[T1.b])
            kt = self.Hh.alloc()
            self.tt("dve", kt.t[:, 0:nn], T1.t[:, 0:nn], T2.t[:, 0:nn], ALU.mult, rd=[T1.b, T2.b], wr=[kt.b])
            sigm(T3)
            self.tt("dve", T3.t[:, 0:nn], T3.t[:, 0:nn], bq.t[:, 0:nn], ALU.mult, rd=[T3.b, bq.b], wr=[T3.b])
            bq.free()
            qt = self.Hh.alloc()
            self.tt("dve", qt.t[:, 0:nn], T3.t[:, 0:nn], T4.t[:, 0:nn], ALU.mult, rd=[T3.b, T4.b], wr=[qt.b])
            nch = nn // CH
            ecl = ar["ecl"][:, par, :]
            eclb = ar["eclb"][par]
            self.cp("pool", ecl[:, 0:nch], T4.t[:, CH - 1:nn:CH], rd=[T4.b], wr=[eclb])
            T1.free(); T2.free(); T3.free(); T4.free()
            va, Kst, Vst = ar["va"][par], ar["Kst"], ar["Vst"]
            vab = ar["vab"][par]
            qk = []
            for t in range(TG):
                yield
                bt = self.banks.alloc()
                cols = slice(t0 + t * 128, t0 + (t + 1) * 128)
                for c in range(KC):
                    self.mm(bt.t[:, 0:256], hT[:, c, cols], w3.t[:, c * 256:(c + 1) * 256], c == 0, c == KC - 1,
                            rd=[w3.b, hb], wr=[bt.b])
                for c in range(KC):
                    self.mm(bt.t[:, 256:512], hT[:, c, cols], w4.t[:, c * 256:(c + 1) * 256], c == 0, c == KC - 1,
                            rd=[w4.b, hb], wr=[bt.b])
                self.cp("act", va[:, t, :], bt.t[:, 0:128], rd=[bt.b], wr=[vab[t]])
                ss = ar["ss"][:, t, :]
                ssb = ar["ssb"][t]
                junk = self.Hh.alloc()
                self.act(junk.t[:, 0:128], bt.t[:, 128:256], AF.Square, rd=[bt.b], wr=[junk.b, ssb], accum_out=ss[:, 0:1])
                self.act(junk.t[:, 128:256], bt.t[:, 256:384], AF.Square, rd=[bt.b], wr=[junk.b, ssb], accum_out=ss[:, 1:2])
                junk.free()
                self.cp("act", Vst[:, t, :], bt.t[:, 384:512], rd=[bt.b], wr=[ar["Vstb"][t]])
                yield
                self.act(ss[:, 2:4], ss[:, 0:2], AF.Ln, rd=[ssb], wr=[ssb], scale=1.0 / DH, bias=EPS)
                self.act(ss[:, 2:4], ss[:, 2:4], AF.Exp, rd=[ssb], wr=[ssb], scale=-0.5)
                if t % 2 == 0:
                    qk.append(self.Hh.alloc())
                qs = qk[-1]
                o0 = (t % 2) * 256
                self.stt("dve", qs.t[:, o0:o0 + 128], bt.t[:, 128:256], ss[:, 2:3], self.gqk[:, 0, :], ALU.mult, ALU.mult,
                         rd=[bt.b, ssb, cb], wr=[qs.b])
                self.stt("dve", Kst[:, t, :], bt.t[:, 256:384], ss[:, 3:4], self.gqk[:, 1, :], ALU.mult, ALU.mult,
                         rd=[bt.b, ssb, cb], wr=[ar["Kstb"][t]])
                bt.free()
                self.cp("pool", qs.t[:, o0 + 128:o0 + 256], Kst[:, t, :], rd=[ar["Kstb"][t], qs.b], wr=[qs.b])
            s.update(kt=kt, qt=qt, T5=T5, qk=qk, par=par)
            if g == NGp - 1:
                for w in (w1, w2, w3, w4):
                    w.free()
                if NGp > 2 and h + 1 < H and "nopf" not in DBG:
                    load_head(h + 1)

        def P1b(n):
            h, g = iters[n]
            s = st[n]
            t0, nn = groups[g]
            TG = nn // 128
            kt = s["kt"]
            ktm, Kst, Vst = ar["ktm"], ar["Kst"], ar["Vst"]
            QT = self.Hh.alloc()
            for t in range(TG):
                gt = (t0 // 128) + t
                qs = s["qk"][t // 2]
                o0 = (t % 2) * 256
                self.cp("pool", Vb[:, gt, :], Vst[:, t, :], rd=[ar["Vstb"][t]], wr=[kvb[gt]])
                bx = self.banks.alloc()
                bxv = bx.t[:].bitcast(BF16)
                self.tp(bxv[:, 0:128], qs.t[:, o0:o0 + 128], True, rd=[qs.b], wr=[bx.b])
                self.tp(bxv[:, 128:256], qs.t[:, o0 + 128:o0 + 256], True, rd=[qs.b], wr=[bx.b])
                self.tp(bxv[:, 256:384], kt.t[:, t * 128:(t + 1) * 128], True, rd=[kt.b], wr=[bx.b])
                self.cp("act", QT.t[:, t * 128:(t + 1) * 128], bxv[:, 0:128], rd=[bx.b], wr=[QT.b])
                self.cp("dve", KT[:, gt * 128:(gt + 1) * 128], bxv[:, 128:256], rd=[bx.b], wr=[kvb[gt]])
                self.cp("dve", ktm[:, t, :], bxv[:, 256:384], rd=[bx.b], wr=[ar["ktmb"][t]])
                bx.free()
            for qs in s["qk"]:
                qs.free()
            s["QT"] = QT
            kdst = (O_["ks"] if sample else O_["kp"][seq_out])
            vdst = (O_["vs"] if sample else O_["vp"][seq_out])
            P.dma("act", kdst[t0:t0 + nn, h, :].rearrange("(t p) d -> p t d", p=128), Kst[:, 0:TG, :], reads=ar["Kstb"][0:TG])
            P.dma("act", vdst[t0:t0 + nn, h, :].rearrange("(t p) d -> p t d", p=128), Vst[:, 0:TG, :], reads=ar["Vstb"][0:TG])

        def P2a(n):
            h, g = iters[n]
            s = st[n]
            t0, nn = groups[g]
            par = s["par"]
            nch = nn // CH
            va, vab, ktm = ar["va"][par], ar["vab"][par], ar["ktm"]
            ecl, eclb = ar["ecl"][:, par, :], ar["eclb"][par]
            Sbf, Sbfb, S, Sb = ar["Sbf"], ar["Sbfb"], ar["S"], ar["Sb"]
            if sample:
                S0, S0b, S0bf, S0bfb = ar["S0"], ar["S0b"], ar["S0bf"], ar["S0bfb"]
                ktmm = ar["ktmm"]
                for b in range(4):
                    P.dma("sp", S0[:, b, :], I["sh"][b, h], writes=[S0b[b]])
                    self.cp("pool", S0bf[:, b, :], S0[:, b, :], rd=[S0b[b]], wr=[S0bfb[b]])
                    self.ts1("dve", ktmm[:, b, :], ktm[:, 0, :], self.cA[:, C_RMS + b:C_RMS + b + 1], ALU.mult,
                             rd=[ar["ktmb"][0], cb], wr=[ar["ktmmb"]])
            bDs = []
            if sample:
                dmap = lambda i: (i // 4, i % 4)
                nbD = (nch + 3) // 4
            else:
                dmap = lambda i: ((i % cpt) + cpt * ((i // cpt) // 4), (i // cpt) % 4)
                nbD = cpt * ((nch // cpt + 3) // 4)

            def scan_step(i):
                bD = bDs[dmap(i)[0]]
                dcols = slice(dmap(i)[1] * 128, (dmap(i)[1] + 1) * 128)
                if sample:
                    Sn = ar["Sn"]
                    self.tt("dve", Sn[:, i, :], bD.t[:, dcols], S0[:, i, :], ALU.add, rd=[bD.b, S0b[i]], wr=[ar["Snb"][i]])
                    self.ts1("dve", Sn[:, i, :], Sn[:, i, :], ecl[:, i:i + 1], ALU.mult, rd=[eclb, ar["Snb"][i]], wr=[ar["Snb"][i]])
                    P.dma("act", O_["hs"][i, h], Sn[:, i, :], reads=[ar["Snb"][i]])
                else:
                    gi = (t0 // CH) + i
                    cur, prv = gi % 2, (gi + 1) % 2
                    if gi == 0:
                        self.ts1("dve", S[:, cur, :], bD.t[:, dcols], ecl[:, i:i + 1], ALU.mult, rd=[bD.b, eclb], wr=[Sb[cur]])
                    else:
                        self.ts1("dve", S[:, cur, :], S[:, prv, :], ecl[:, i:i + 1], ALU.mult, rd=[Sb[prv], eclb], wr=[Sb[cur]])
                        self.stt("dve", S[:, cur, :], bD.t[:, dcols], ecl[:, i:i + 1], S[:, cur, :], ALU.mult, ALU.add,
                                 rd=[bD.b, eclb, Sb[cur]], wr=[Sb[cur]])
                    self.cp("dve", Sbf[:, i + 1, :], S[:, cur, :], rd=[Sb[cur]], wr=[Sbfb[i + 1]])
                    if gi == self.SEQ // CH - 1:
                        P.dma("act", O_["hp"][seq_out, h], S[:, cur, :], reads=[Sb[cur]])
            for _ in range(nbD):
                bDs.append(self.banks.alloc())
            for i in range(nch):
                bD = bDs[dmap(i)[0]]
                t, pb = i // cpt, (i % cpt) * CH
                dcols = slice(dmap(i)[1] * 128, (dmap(i)[1] + 1) * 128)
                if sample:
                    self.mm(bD.t[:, dcols], ktmm[:, i, :], va[:, 0, :], True, True, rd=[ar["ktmmb"], vab[0]], wr=[bD.b])
                else:
                    self.mm(bD.t[:, dcols], ktm[pb:pb + CH, t, :], va[pb:pb + CH, t, :], True, True,
                            rd=[ar["ktmb"][t], vab[t]], wr=[bD.b])
                if "nodf" in DBG:
                    scan_step(i)
            if "nodf" not in DBG:
                for i in range(nch):
                    scan_step(i)
            for bD in bDs:
                bD.free()

        def PA(n):
            h, g = iters[n]
            s = st[n]
            t0, nn = groups[g]
            TG = nn // 128
            kt, qt = s["kt"], s["qt"]
            bA = self.banks.alloc()
            ATm = self.Hh.alloc()
            for t in range(TG):
                cs = slice(t * 128, (t + 1) * 128)
                self.mm(bA.t[:, cs], kt.t[:, cs], qt.t[:, cs], True, True, rd=[kt.b, qt.b], wr=[bA.b])
            self.tt("dve", ATm.t[:, 0:nn].rearrange("p (t c) -> p t c", c=128), bA.t[:, 0:nn].rearrange("p (t c) -> p t c", c=128),
                    hmask.unsqueeze(1).to_broadcast([128, TG, 128]), ALU.mult, rd=[bA.b, cb], wr=[ATm.b])
            bA.free()
            kt.free()
            s["ATm"] = ATm

        def PO(n):
            h, g = iters[n]
            s = st[n]
            t0, nn = groups[g]
            TG = nn // 128
            par = s["par"]
            nch = nn // CH
            qt, T5, ATm = s["qt"], s["T5"], s["ATm"]
            va, vab = ar["va"][par], ar["vab"][par]
            Sbf, Sbfb = ar["Sbf"], ar["Sbfb"]
            bO = self.banks.alloc()
            for t in range(TG):
                cs = slice(t * 128, (t + 1) * 128)
                self.mm(bO.t[:, cs], va[:, t, :], ATm.t[:, cs], True, False, rd=[vab[t], ATm.b], wr=[bO.b])
                for ci in range(cpt):
                    i = t * cpt + ci
                    ccs = slice(i * CH, (i + 1) * CH)
                    lastmm = ci == cpt - 1
                    if sample:
                        self.mm(bO.t[:, ccs], ar["S0bf"][:, i, :], qt.t[:, ccs], False, lastmm,
                                rd=[ar["S0bfb"][i], qt.b], wr=[bO.b])
                    else:
                        gi = (t0 // CH) + i
                        if gi == 0:
                            continue
                        self.mm(bO.t[:, ccs], Sbf[:, i, :], qt.t[:, ccs], False, lastmm, rd=[Sbfb[i], qt.b], wr=[bO.b])
            ATm.free()
            qt.free()
            if not sample and g < NGp - 1:
                self.cp("dve", Sbf[:, 0, :], Sbf[:, nch, :], rd=[Sbfb[nch]], wr=[Sbfb[0]])
            sq = self.Hh.alloc()
            self.act(sq.t[:, 0:nn], bO.t[:, 0:nn], AF.Square, rd=[bO.b], wr=[sq.b])
            bS = self.banks.alloc()
            self.mm(bS.t[:, 0:nn], self.onesb[:], sq.t[:, 0:nn], True, True, rd=[sq.b, cb], wr=[bS.b])
            sq.free()
            rs = self.rstd_bc(bS, nn, DH, explog=True)
            bS.free()
            self.tt("dve", rs.t[:, 0:nn], bO.t[:, 0:nn], rs.t[:, 0:nn], ALU.mult, rd=[bO.b, rs.b], wr=[rs.b])
            bO.free()
            mixa = self.Hh.alloc()
            self.stt("dve", mixa.t[:, 0:nn], rs.t[:, 0:nn], self.prm[:, 64 + h:65 + h], T5.t[:, 0:nn], ALU.mult, ALU.mult,
                     rd=[rs.b, T5.b, cb], wr=[mixa.b])
            rs.free()
            T5.free()
            s["mixa"] = mixa

        def P3(n, filler=None):
            h, g = iters[n]
            s = st[n]
            t0, nn = groups[g]
            QT = s["QT"]
            mixb = self.Hh.alloc()
            if not sample:
                R = self.FR.alloc()
                Ob = self.banks.alloc()
                kts = list(range((t0 + nn) // 128 - 1, -1, -1))
                g0 = t0 // 128
                pairs = []
                for kbi in kts:
                    kd = kbi - g0
                    mask = self.amask[:, kd, 0:nn] if kd >= 0 else None
                    pairs.append(([(0, nn, KT[:, kbi * 128:(kbi + 1) * 128], kvb[kbi])], QT.t[:, 0:nn], QT.b,
                                  [(0, nn, Vb[:, kbi, :], kvb[kbi])], 128, nn, mask))
                self.attn_pairs(pairs, R, Ob, filler)
                self.cp("act", mixb.t[:, 0:nn], Ob.t[:, 0:nn], rd=[Ob.b], wr=[mixb.b])
                Ob.free()
                R.free()
            else:
                npt = self.PAST // 128
                vk = lambda ap: ap.rearrange("p (t d) -> p t d", d=128)
                vcs = []
                for b in range(4):
                    kc = self.wload([(0, npt * 128, vk, I["ck"][b, :, h, :].rearrange("(t p) d -> p t d", p=128))], npt * 128)
                    vcs.append(self.wload([(0, npt * 128, vk, I["cv"][b, :, h, :].rearrange("(t p) d -> p t d", p=128))], npt * 128))
                    KTp, KTpb = ar["KTp"][b], ar["KTpb"][b]
                    for t8 in range(0, npt, 8):
                        bx = self.banks.alloc()
                        bxv = bx.t[:].bitcast(BF16)
                        m = min(8, npt - t8)
                        for j in range(m):
                            self.tp(bxv[:, j * 128:(j + 1) * 128], kc.t[:, (t8 + j) * 128:(t8 + j + 1) * 128], True,
                                    rd=[kc.b], wr=[bx.b])
                        self.cp("act" if (t8 // 8) % 2 == 0 else "dve", KTp[:, t8 * 128:(t8 + m) * 128], bxv[:, 0:m * 128],
                                rd=[bx.b], wr=[KTpb])
                        bx.free()
                    kc.free()
                R = self.FR.alloc()
                Ob = self.banks.alloc()
                pairs = [([(0, 128, KT[:, 0:128], kvb[0])], QT.t[:, 0:128], QT.b, [(0, 128, Vb[:, 0, :], kvb[0])], 128, 128,
                          self.amask_s[:].rearrange("p a b -> p (a b)"))]
                for kbi in range(npt - 1, -1, -1):
                    zs = [(b * 32, (b + 1) * 32, ar["KTp"][b][:, kbi * 128:(kbi + 1) * 128], ar["KTpb"][b]) for b in range(4)]
                    avs = [(b * 32, (b + 1) * 32, vcs[b].t[:, kbi * 128:(kbi + 1) * 128], vcs[b].b) for b in range(4)]
                    pairs.append((zs, QT.t[:, 0:128], QT.b, avs, 128, 128, None))
                self.attn_pairs(pairs, R, Ob, filler)
                self.cp("act", mixb.t[:, 0:128], Ob.t[:, 0:128], rd=[Ob.b], wr=[mixb.b])
                Ob.free()
                R.free()
                for vc in vcs:
                    vc.free()
            QT.free()
            s["mixb"] = mixb

        def P4(n):
            h, g = iters[n]
            s = st[n]
            t0, nn = groups[g]
            wo = wts[h][4]
            mixa, mixb = s["mixa"], s["mixb"]
            for d in range(KC):
                bk = self.banks.alloc()
                self.mm(bk.t[:, 0:nn], wo.t[:, d * 128:(d + 1) * 128], mixa.t[:, 0:nn], True, False, rd=[wo.b, mixa.b], wr=[bk.b])
                self.mm(bk.t[:, 0:nn], wo.t[:, 1024 + d * 128:1024 + (d + 1) * 128], mixb.t[:, 0:nn], False, True,
                        rd=[wo.b, mixb.b], wr=[bk.b])
                self.stt("dve", xT[:, d, t0:t0 + nn], bk.t[:, 0:nn], 1.0, xT[:, d, t0:t0 + nn], ALU.mult, ALU.add,
                         rd=[bk.b, ctx["xb"][g][d]], wr=[ctx["xb"][g][d]])
                bk.free()
            mixa.free()
            mixb.free()
            if g == NGp - 1:
                wo.free()

        for _ in P1a(0):
            pass
        P1b(0)
        for n in range(NI):
            nxt = n + 1 < NI
            gen = P1a(n + 1) if nxt else iter(())
            if "fill" not in DBG:
                for _ in gen:
                    pass
            else:
                next(gen, None)
            P2a(n)
            PA(n)
            if n > 0 and "p4late" not in DBG:
                P4(n - 1)
            P3(n, gen)
            for _ in gen:
                pass
            PO(n)
            if nxt:
                P1b(n + 1)
            if n > 0 and "p4late" in DBG:
                P4(n - 1)
        P4(NI - 1)

    def run_pass(self, ctx, src, ydst, sample, seq_out, ar, wsT, BT, io_ring):
        I = self.i
        dbg = DBG
        mk = self.P.mark
        mk("load_x")
        self.load_x(ctx, src, io_ring)
        for l in range(2):
            if "ffn1" in dbg:
                mk(f"L{l}.norm1")
                self.rmsnorm(ctx, 0 + l * 8)
                mk(f"L{l}.ffn1")
                self.ffn(ctx, I["ffn1_w_gate"][l], I["ffn1_w_up"][l], I["ffn1_w_down"][l])
            if "norm" in dbg:
                mk(f"L{l}.normM")
                self.rmsnorm(ctx, 16 + l * 8)
            self.P.barrier()
            mk(f"L{l}.mixer")
            if l == 0:
                if "ab" in dbg:
                    self.mixer_ab(ctx, sample, ar, seq_out)
            else:
                if "c" in dbg:
                    self.mixer_c(ctx, sample, wsT, BT, ar)
            self.P.barrier()
            if "ffn2" in dbg:
                mk(f"L{l}.norm2")
                self.rmsnorm(ctx, 32 + l * 8)
                mk(f"L{l}.ffn2")
                self.ffn(ctx, I["ffn2_w_gate"][l], I["ffn2_w_up"][l], I["ffn2_w_down"][l])
        mk("store_y")
        self.store_y(ctx, ydst, io_ring)
        mk("end")

    def build(self):
        P = self.P
        SEQ, NP = self.SEQ, self.NP
        with contextlib.ExitStack() as stk:
            GN = min(512, SEQ)
            NG = SEQ // GN
            xT = P.sb("xT", [128, KC, SEQ], stack=stk)
            hT = P.sb("hT", [128, KC, SEQ], BF16, stack=stk)
            arena = P.sb("arena", [128, 4800], stack=stk)
            wsT = P.sb("wsT", [128, 8, 128], BF16, stack=stk)
            BT = P.sb("BT", [128, 16, 128], stack=stk)
            with contextlib.ExitStack() as tmp:
                if "nog" not in DBG:
                    tmps = [arena[:, k * 1024:(k + 1) * 1024].rearrange("p (g j) -> p g j", g=8) for k in range(3)]
                    self.gmlp_consts(False, wsT, BT, tmp, tmps)
                P.barrier()
            off = [0]

            def carve(nf32, dt=F32, shape=None):
                ap = arena[:, off[0]:off[0] + nf32]
                off[0] += nf32
                if dt == BF16:
                    ap = ap.bitcast(BF16)
                return ap

            TGm = GN // 128
            nchm = GN // 64
            ar = {}
            ar["KT"] = carve(SEQ // 2, BF16)
            ar["Vb"] = carve(SEQ // 2, BF16).rearrange("p (t d) -> p t d", d=128)
            ar["kvb"] = [Buf() for _ in range(SEQ // 128)]
            ar["va"] = [carve(TGm * 64, BF16).rearrange("p (t d) -> p t d", d=128) for _ in range(2)]
            ar["vab"] = [[Buf() for _ in range(TGm)] for _ in range(2)]
            ar["ktm"] = carve(TGm * 64, BF16).rearrange("p (t d) -> p t d", d=128); ar["ktmb"] = [Buf() for _ in range(TGm)]
            ar["Sbf"] = carve((nchm + 1) * 64, BF16).rearrange("p (t d) -> p t d", d=128)
            ar["Sbfb"] = [Buf() for _ in range(nchm + 1)]
            ar["Kst"] = carve(TGm * 128).rearrange("p (t d) -> p t d", d=128); ar["Kstb"] = [Buf() for _ in range(TGm)]
            ar["Vst"] = carve(TGm * 128).rearrange("p (t d) -> p t d", d=128); ar["Vstb"] = [Buf() for _ in range(TGm)]
            ar["S"] = carve(256).rearrange("p (t d) -> p t d", d=128); ar["Sb"] = [Buf(), Buf()]
            ar["ecl"] = carve(32).rearrange("p (a c) -> p a c", a=2); ar["eclb"] = [Buf(), Buf()]
            ar["ss"] = carve(4 * TGm).rearrange("p (t a) -> p t a", a=4); ar["ssb"] = [Buf() for _ in range(TGm)]
            ab_end = off[0]
            off[0] = 0
            ar["vn"] = carve(TGm * 1024, BF16).rearrange("p (t f) -> p t f", f=DC)
            ar["vnb"] = [Buf() for _ in range(TGm)]
            ar["stats"] = carve(TGm * 16).rearrange("p (t a u) -> p t a u", a=2, u=8); ar["stb"] = Buf()
            ar["mv"] = carve(TGm * 4).rearrange("p (t a) -> p t a", a=4)
            c_end = off[0]
            io_ring = Ring([arena[:, k * 1024:(k + 1) * 1024] for k in range(4)], "io")
            assert max(ab_end, c_end, 4096) <= 4800, (ab_end, c_end)
            for s in range(NP if "prompt" in DBG else 0):
                ctx = dict(xT=xT, hT=hT, T=SEQ, GN=GN, groups=[(g * GN, GN) for g in range(NG)],
                           xb=[[Buf() for _ in range(KC)] for _ in range(NG)], hb=[Buf() for _ in range(NG)])
                P.barrier(engines=ENGS, dma_queues=("sp", "act", "pool"))
                self.run_pass(ctx, self.i["xp"][s], self.o["yp"][s], False, s, ar, wsT, BT, io_ring)
        P.barrier(engines=ENGS, dma_queues=("sp", "act", "pool"))
        with contextlib.ExitStack() as stk:
            xT = P.sb("xTs", [128, KC, 128], stack=stk)
            hT = P.sb("hTs", [128, KC, 128], BF16, stack=stk)
            wsT = P.sb("wsTs", [128, 8, 128], BF16, stack=stk)
            BT = P.sb("BTs", [128, 16, 128], stack=stk)
            with contextlib.ExitStack() as tmp:
                if "nogs" not in DBG:
                    self.gmlp_consts(True, wsT, BT, tmp)
                P.barrier()
            ar = {}
            sbt = lambda name, shape, dt=F32: P.sb("s_" + name, shape, dt, stack=stk)
            self.wbf.extend([sbt(f"xwbf{i}", [128, 2048], BF16) for i in range(6)])
            self.wst.extend([sbt(f"xwst{i}", [128, 2048]) for i in range(2)])
            self.wst_parts += [[Buf() for _ in range(4)] for _ in range(2)]
            self.Hh.extend([sbt(f"xH{i}", [128, 512], BF16) for i in range(1)])
            ar["KT"] = sbt("KT", [128, 128], BF16)
            ar["Vb"] = sbt("Vb", [128, 1, 128], BF16); ar["kvb"] = [Buf()]
            ar["va"] = [sbt("va0", [128, 1, 128], BF16), sbt("va1", [128, 1, 128], BF16)]; ar["vab"] = [[Buf()], [Buf()]]
            ar["ktm"] = sbt("ktm", [128, 1, 128], BF16); ar["ktmb"] = [Buf()]
            ar["ktmm"] = sbt("ktmm", [128, 4, 128], BF16); ar["ktmmb"] = Buf()
            ar["Sbf"] = None; ar["Sbfb"] = None; ar["S"] = None; ar["Sb"] = None
            ar["Kst"] = sbt("Kst", [128, 1, 128]); ar["Kstb"] = [Buf()]
            ar["Vst"] = sbt("Vst", [128, 1, 128]); ar["Vstb"] = [Buf()]
            ar["ecl"] = sbt("ecl", [128, 2, 16]); ar["eclb"] = [Buf(), Buf()]
            ar["ss"] = sbt("ss", [128, 1, 4]); ar["ssb"] = [Buf()]
            ar["S0"] = sbt("S0", [128, 4, 128]); ar["S0b"] = [Buf() for _ in range(4)]
            ar["S0bf"] = sbt("S0bf", [128, 4, 128], BF16); ar["S0bfb"] = [Buf() for _ in range(4)]
            ar["Sn"] = sbt("Sn", [128, 4, 128]); ar["Snb"] = [Buf() for _ in range(4)]
            ar["KTp"] = [sbt(f"KTp{b}", [128, self.PAST], BF16) for b in range(4)]; ar["KTpb"] = [Buf() for _ in range(4)]
            ar["vn"] = sbt("vn", [128, 1, DC]); ar["vnb"] = [Buf()]
            ar["stats"] = sbt("stats", [128, 1, 2, 8]); ar["stb"] = Buf()
            ar["mv"] = sbt("mv", [128, 1, 4])
            ar["vout"] = sbt("vout", [128, DC]); ar["voutb"] = Buf()
            ar["vhb"] = sbt("vhb", [128, 1, DC], BF16); ar["vhbb"] = Buf()
            ar["lng"] = sbt("lng", [128, DC]); ar["lnb"] = sbt("lnb", [128, DC])
            P.dma("sp", ar["lng"][:, :], self.i["c_ln_g"][0:1, :].partition_broadcast(128), reads=[self.cb], writes=[self.cb])
            P.dma("sp", ar["lnb"][:, :], self.i["c_ln_b"][0:1, :].partition_broadcast(128), reads=[self.cb], writes=[self.cb])
            io_ring = Ring([sbt("io0", [128, 1024]), sbt("io1", [128, 1024])], "ios")
            ctx = dict(xT=xT, hT=hT, T=128, GN=128, groups=[(0, 128)],
                       xb=[[Buf() for _ in range(KC)]], hb=[Buf()])
            if "sample" in DBG:
                self.run_pass(ctx, self.i["xs"], self.o["ys"], True, None, ar, wsT, BT, io_ring)
            P.barrier()
        P.finish()
        if _os.environ.get("KMARKS"):
            import json
            json.dump(P.marks, open(_os.environ["KMARKS"], "w"))
        return self.nc


_CACHE = {}


def _get_nc(key):
    if key not in _CACHE:
        _CACHE[key] = Builder(*key).build()
    return _CACHE[key]


WEIGHT_KEYS = ["ffn1_norm", "ffn1_w_gate", "ffn1_w_up", "ffn1_w_down", "mix_norm", "ffn2_norm", "ffn2_w_gate",
               "ffn2_w_up", "ffn2_w_down", "ab_w_in", "ab_lb", "ab_g_out", "ab_g_q", "ab_g_k", "ab_w_out", "c_w_in",
               "c_ln_g", "c_ln_b", "c_w_s", "c_b_s", "c_w_out"]


def kernel(x_prompt, x_sample, cache_sb_k, cache_sb_v, state_hgrn, n_cores=8, **w):
    x_prompt = np.asarray(x_prompt, np.float32)
    x_sample = np.asarray(x_sample, np.float32)
    B, SEQ, _ = x_prompt.shape
    BS, DS, _ = x_sample.shape
    PAST = cache_sb_k.shape[2]
    NP, NS = B // n_cores, BS // n_cores
    nc = _get_nc((NP, SEQ, NS, DS, PAST))
    consts = make_consts()
    wts = {k: np.ascontiguousarray(np.asarray(w[k], np.float32)) for k in WEIGHT_KEYS}
    in_maps = []
    for i in range(n_cores):
        m = dict(wts)
        m["consts"] = consts
        m["xp"] = np.ascontiguousarray(x_prompt[i * NP:(i + 1) * NP])
        m["xs"] = np.ascontiguousarray(x_sample[i * NS:(i + 1) * NS]).reshape(NS * DS, D)
        m["ck"] = np.ascontiguousarray(np.asarray(cache_sb_k, np.float32)[0, i * NS:(i + 1) * NS])
        m["cv"] = np.ascontiguousarray(np.asarray(cache_sb_v, np.float32)[0, i * NS:(i + 1) * NS])
        m["sh"] = np.ascontiguousarray(np.asarray(state_hgrn, np.float32)[0, i * NS:(i + 1) * NS])
        in_maps.append(m)
    res = run_bass_kernel_spmd(nc, in_maps, core_ids=list(range(n_cores)))
    R = res.results
    cat = lambda k: np.concatenate([np.asarray(r[k]) for r in R], axis=0)
    y_prompt = cat("yp")
    y_sample = cat("ys").reshape(BS, DS, D)
    kp = cat("kp")[None]
    vp = cat("vp")[None]
    hp = cat("hp")[None]
    ks = cat("ks").reshape(BS, DS, H, DH)[None]
    vs = cat("vs").reshape(BS, DS, H, DH)[None]
    hs = cat("hs")[None]
    gv = cat("gv").reshape(BS, DS, DC)[None]
    return (y_prompt, y_sample, kp, vp, hp, ks, vs, hs, gv)
```

```python
import contextlib
import numpy as np
import concourse.bass as bass
import concourse.mybir as mybir
from concourse.bass_utils import run_bass_kernel_spmd

import os as _os
DBG = set(_os.environ.get("KDBG", "ffn1,norm,ab,c,ffn2,prompt,sample").split(","))
F32 = mybir.dt.float32
F32R = mybir.dt.float32r
BF16 = mybir.dt.bfloat16
AF = mybir.ActivationFunctionType
ALU = mybir.AluOpType

D = 1024
KC = 8
DFF = 2816
H = 8
DH = 128
DC = 2048
EPS = 1e-6
ENGS = ("pe", "act", "dve", "pool", "sp")
DMA_K = 8


class Buf:
    __slots__ = ("name", "w", "r", "excl")

    def __init__(self, name="", excl=False):
        self.name = name
        self.w = None
        self.r = {}
        self.excl = excl


class Prog:
    def __init__(self, nc):
        self.nc = nc
        self.st = contextlib.ExitStack()
        self.ops = {e: [] for e in ENGS}
        self.cnt = {e: 0 for e in ENGS}
        self.seen = {e: {} for e in ENGS}
        self.sems = {}
        for e in ENGS:
            self.sems[e] = self.st.enter_context(nc.semaphore("s_" + e))
        self.dma_i = {}
        for q in ("sp", "act", "pool"):
            self.dma_i[q] = 0
            for k in range(DMA_K):
                self.sems[("d", q, k)] = self.st.enter_context(nc.semaphore(f"d_{q}_{k}"))
        self.n_ins = 0
        self.n_wait = 0
        self.marks = []

    def mark(self, label):
        self.marks.append((label, dict(self.cnt)))

    def sb(self, name, shape, dt=F32, stack=None):
        return (stack or self.st).enter_context(self.nc.sbuf_tensor(name, list(shape), dt))

    def ps(self, name, shape, dt=F32):
        return self.st.enter_context(self.nc.psum_tensor(name, list(shape), dt))

    def _collect(self, eng, reads, writes, extra=()):
        waits = {}
        seen = self.seen[eng]

        def need(k, v):
            if k == eng and eng == "pe":
                return
            if seen.get(k, 0) < v and waits.get(k, 0) < v:
                waits[k] = v

        for b in reads:
            if b.w is not None:
                need(*b.w)
            if b.excl:
                for k, v in b.r.items():
                    if k != eng:
                        need(k, v)
        for b in writes:
            if b.w is not None:
                need(*b.w)
            for k, v in b.r.items():
                need(k, v)
        for k, v in extra:
            need(k, v)
        for k, v in waits.items():
            seen[k] = v
        return [(self.sems[k], v) for k, v in waits.items()]

    def op(self, eng, fn, reads=(), writes=()):
        waits = self._collect(eng, reads, writes)
        self.cnt[eng] += 1
        tok = (eng, self.cnt[eng])
        sem = self.sems[eng]
        self.n_wait += len(waits)
        self.n_ins += 1

        def run(e, waits=waits, fn=fn, sem=sem):
            for s, v in waits:
                e.wait_ge(s, v)
            fn(e).then_inc(sem, 1)

        self.ops[eng].append(run)
        for b in writes:
            b.w = tok
            b.r = {}
        for b in reads:
            b.r[eng] = tok[1]
        return tok

    def dma(self, q, out, in_, reads=(), writes=(), **kw):
        i = self.dma_i[q]
        self.dma_i[q] = i + 1
        slot, gen = i % DMA_K, i // DMA_K
        key = ("d", q, slot)
        extra = [(key, 16 * gen)] if gen > 0 else []
        waits = self._collect(q, reads, writes, extra)
        sem = self.sems[key]
        tok = (key, 16 * (gen + 1))
        self.n_wait += len(waits)
        self.n_ins += 1

        def run(e, waits=waits, sem=sem, out=out, in_=in_, kw=kw):
            for s, v in waits:
                e.wait_ge(s, v)
            e.dma_start(out=out, in_=in_, **kw).then_inc(sem, 16)

        self.ops[q].append(run)
        for b in writes:
            b.w = tok
            b.r = {}
        for b in reads:
            b.r[key] = tok[1]
        return tok

    def _all_tokens(self, dma_queues=("sp", "act", "pool")):
        toks = []
        for q in dma_queues:
            n = self.dma_i[q]
            for slot in range(min(n, DMA_K)):
                gens = (n - 1 - slot) // DMA_K + 1
                toks.append((("d", q, slot), 16 * gens))
        for e in ("pe", "act", "dve", "pool"):
            if self.cnt[e] > 0:
                toks.append((e, self.cnt[e]))
        return toks

    def barrier(self, engines=("pe", "act", "dve", "pool"), dma_queues=("act", "pool")):
        toks = self._all_tokens(dma_queues)
        for eng in engines:
            seen = self.seen[eng]
            waits = []
            for k, v in toks:
                if k == eng:
                    continue
                if seen.get(k, 0) < v:
                    seen[k] = v
                    waits.append((self.sems[k], v))

            def run(e, waits=waits):
                for s, v in waits:
                    e.wait_ge(s, v)

            self.ops[eng].append(run)

    def finish(self):
        fin = [(self.sems[k], v) for k, v in self._all_tokens()]

        def run_fin(e, fin=fin):
            for s, v in fin:
                e.wait_ge(s, v)

        self.ops["sp"].append(run_fin)
        nc, ops = self.nc, self.ops
        with nc.Block() as block:
            @block.tensor
            def _(e):
                for f in ops["pe"]:
                    f(e)

            @block.scalar
            def _(e):
                for f in ops["act"]:
                    f(e)

            @block.vector
            def _(e):
                for f in ops["dve"]:
                    f(e)

            @block.gpsimd
            def _(e):
                for f in ops["pool"]:
                    f(e)

            @block.sync
            def _(e):
                for f in ops["sp"]:
                    f(e)
        self.st.close()


class Slot:
    __slots__ = ("t", "b", "ring", "i")

    def __init__(self, t, b, ring, i):
        self.t, self.b, self.ring, self.i = t, b, ring, i

    def free(self):
        self.ring.freelist.append(self.i)


class Ring:
    def __init__(self, tensors, name, excl=False):
        self.name = name
        self.slots = [Slot(t, Buf(f"{name}{i}", excl), self, i) for i, t in enumerate(tensors)]
        self.freelist = list(range(len(tensors)))

    def alloc(self):
        assert self.freelist, f"ring {self.name} exhausted"
        return self.slots[self.freelist.pop(0)]

    def extend(self, tensors):
        for t in tensors:
            i = len(self.slots)
            self.slots.append(Slot(t, Buf(f"{self.name}{i}", self.slots[0].b.excl), self, i))
            self.freelist.append(i)

    def shrink(self, n):
        for _ in range(n):
            i = len(self.slots) - 1
            assert i in self.freelist, f"ring {self.name}: slot {i} still in use"
            self.freelist.remove(i)
            self.slots.pop()


C_ID, C_ONES, C_NTRI, C_NONES, C_HMP, C_HMS, C_RMS, C_SCS, C_GMP, C_SCP, C_AM, C_AMS = (
    0, 128, 256, 384, 512, 640, 768, 772, 900, 1028, 1540, 3588)
C_RES = 900
NCONST = 3588 + 128


def make_consts():
    c = np.zeros((128, NCONST), np.float32)
    i = np.arange(128)
    c[:, C_ID:C_ID + 128] = np.eye(128)
    c[:, C_ONES:C_ONES + 128] = 1.0
    c[:, C_NTRI:C_NTRI + 128] = -1.0 * (i[:, None] >= i[None, :])
    c[:, C_NONES:C_NONES + 128] = -1.0
    c[:, C_HMP:C_HMP + 128] = (i[:, None] // 64 == i[None, :] // 64) & (i[:, None] <= i[None, :])
    c[:, C_HMS:C_HMS + 128] = (i[:, None] // 32 == i[None, :] // 32) & (i[:, None] <= i[None, :])
    c[:, C_GMP:C_GMP + 128] = (i[None, :] // 64) <= (i[:, None] // 64)
    c[:, C_RMS:C_RMS + 4] = (i[:, None] // 32 == np.arange(4)[None, :])
    t = np.arange(512)
    c[:, C_SCP:C_SCP + 512] = (t % 64 != 0)[None, :]
    c[:, C_SCS:C_SCS + 128] = (np.arange(128) % 32 != 0)[None, :]
    for kd in range(4):
        c[:, C_AM + kd * 512:C_AM + (kd + 1) * 512] = (t[None, :] - 128 * kd) > i[:, None]
    for b in range(4):
        c[:, C_AMS + b * 32:C_AMS + (b + 1) * 32] = (i[:, None] // 32 == b) & ((i[:, None] % 32) < np.arange(32)[None, :])
    return c


class Builder:
    def __init__(self, NP, SEQ, NS, DS, PAST, n_wbf=6):
        self.NP, self.SEQ, self.NS, self.DS, self.PAST = NP, SEQ, NS, DS, PAST
        assert NS * DS == 128 and DS == 32
        nc = self.nc = bass.Bass("TRN2", target_bir_lowering=False)
        P = self.P = Prog(nc)

        def din(name, shape):
            return nc.dram_tensor(name, list(shape), F32, kind="ExternalInput").ap()

        def dout(name, shape):
            return nc.dram_tensor(name, list(shape), F32, kind="ExternalOutput").ap()

        self.i = dict(
            xp=din("xp", [NP, SEQ, D]), xs=din("xs", [128, D]),
            ck=din("ck", [NS, PAST, H, DH]), cv=din("cv", [NS, PAST, H, DH]), sh=din("sh", [NS, H, 128, 128]),
            consts=din("consts", [128, NCONST]),
            ffn1_norm=din("ffn1_norm", [2, D]), mix_norm=din("mix_norm", [2, D]), ffn2_norm=din("ffn2_norm", [2, D]),
            ffn1_w_gate=din("ffn1_w_gate", [2, D, DFF]), ffn1_w_up=din("ffn1_w_up", [2, D, DFF]),
            ffn1_w_down=din("ffn1_w_down", [2, DFF, D]),
            ffn2_w_gate=din("ffn2_w_gate", [2, D, DFF]), ffn2_w_up=din("ffn2_w_up", [2, D, DFF]),
            ffn2_w_down=din("ffn2_w_down", [2, DFF, D]),
            ab_w_in=din("ab_w_in", [1, D, 7 * D]), ab_lb=din("ab_lb", [2, D]), ab_g_out=din("ab_g_out", [1, H, 128]),
            ab_g_q=din("ab_g_q", [1, 128]), ab_g_k=din("ab_g_k", [1, 128]), ab_w_out=din("ab_w_out", [1, 2 * D, D]),
            c_w_in=din("c_w_in", [1, D, 2 * DC]), c_ln_g=din("c_ln_g", [1, DC]), c_ln_b=din("c_ln_b", [1, DC]),
            c_w_s=din("c_w_s", [1, 8, 128, 128]), c_b_s=din("c_b_s", [1, 8, 128]), c_w_out=din("c_w_out", [1, DC, D]),
        )
        self.o = dict(
            yp=dout("yp", [NP, SEQ, D]), ys=dout("ys", [128, D]),
            kp=dout("kp", [NP, SEQ, H, DH]), vp=dout("vp", [NP, SEQ, H, DH]), hp=dout("hp", [NP, H, 128, 128]),
            ks=dout("ks", [128, H, DH]), vs=dout("vs", [128, H, DH]), hs=dout("hs", [NS, H, 128, 128]),
            gv=dout("gv", [128, DC]),
        )
        self.banks = Ring([P.ps(f"bk{i}", [128, 512]) for i in range(8)], "bk", excl=True)
        self.wst = Ring([P.sb(f"wst{i}", [128, 2048]) for i in range(2)], "wst")
        self.wst_parts = [[Buf() for _ in range(4)] for _ in range(2)]
        self.wbf = Ring([P.sb(f"wbf{i}", [128, 2048], BF16) for i in range(n_wbf)], "wbf")
        self.F = Ring([P.sb(f"F{i}", [128, 512]) for i in range(6)], "F")
        self.Hh = Ring([P.sb(f"H{i}", [128, 512], BF16) for i in range(12)], "H")
        self.cA = P.sb("cA", [128, C_RES])
        self.scanb = P.sb("scanb", [128, 512], BF16)
        self.ntri = P.sb("ntri", [128, 256], F32R)
        self.FR = Ring([P.sb(f"FR{i}", [128, 512], F32R) for i in range(3)], "FR")
        self.amask = P.sb("amask", [128, 4, 512], BF16)
        self.amask_s = P.sb("amask_s", [128, 4, 32], BF16)
        self.identb = P.sb("identb", [128, 128], BF16)
        self.onesb = P.sb("onesb", [128, 128], BF16)
        self.prm = P.sb("prm", [128, 104])
        self.lbp = P.sb("lbp", [128, 3, 8])
        self.gqk = P.sb("gqk", [128, 2, 128])
        self.cb = Buf("consts")
        self.setup_consts()

    def mm(self, out, lhsT, rhs, start, stop, rd, wr, **kw):
        self.P.op("pe", lambda e: e.matmul(out, lhsT=lhsT, rhs=rhs, start=start, stop=stop, **kw), reads=rd, writes=wr)

    def tp(self, out, in_, bf, rd, wr):
        ident = self.identb[:] if bf else self.cA[:, C_ID:C_ID + 128]
        pin = in_.partition_size()
        ident = ident[0:pin, 0:pin]
        self.P.op("pe", lambda e: e.transpose(out=out, in_=in_, identity=ident), reads=list(rd) + [self.cb], writes=wr)

    def act(self, out, in_, func, rd, wr, **kw):
        self.P.op("act", lambda e: e.activation(out=out, in_=in_, func=func, **kw), reads=rd, writes=wr)

    def tt(self, eng, out, in0, in1, op, rd, wr):
        self.P.op(eng, lambda e: e.tensor_tensor(out=out, in0=in0, in1=in1, op=op), reads=rd, writes=wr)

    def ts(self, eng, out, in0, s1, s2, op0, op1, rd, wr):
        self.P.op(eng, lambda e: e.tensor_scalar(out=out, in0=in0, scalar1=s1, scalar2=s2, op0=op0, op1=op1), reads=rd, writes=wr)

    def stt(self, eng, out, in0, scalar, in1, op0, op1, rd, wr):
        self.P.op(eng, lambda e: e.scalar_tensor_tensor(out=out, in0=in0, scalar=scalar, in1=in1, op0=op0, op1=op1),
                  reads=rd, writes=wr)

    def cp(self, eng, out, in_, rd, wr):
        if eng == "act":
            self.P.op(eng, lambda e: e.activation(out=out, in_=in_, func=AF.Copy), reads=rd, writes=wr)
        else:
            self.P.op(eng, lambda e: e.tensor_copy(out=out, in_=in_), reads=rd, writes=wr)

    def ts1(self, eng, out, in_, scalar, op, rd, wr):
        self.P.op(eng, lambda e: e.tensor_single_scalar(out=out, in_=in_, scalar=scalar, op=op), reads=rd, writes=wr)

    def const(self, off, n=128, rows=128):
        return self.cA[0:rows, off:off + n]

    def wload(self, srcs, total, cast_eng="act"):
        if cast_eng == "alt":
            cast_eng = "act"
        st = self.wst.alloc()
        parts = self.wst_parts[st.i]
        for pi, (off, n, shape_fn, src) in enumerate(srcs):
            dst = st.t[:, off:off + n]
            if shape_fn is not None:
                dst = shape_fn(dst)
            self.P.dma("sp", dst, src, writes=[parts[pi]])
        wb = self.wbf.alloc()
        self.cp(cast_eng, wb.t[:, 0:total], st.t[:, 0:total], rd=parts, wr=[wb.b])
        st.free()
        return wb

    def setup_consts(self):
        P, I = self.P, self.i
        cb = self.cb
        P.dma("sp", self.cA[:], I["consts"][:, 0:C_RES], writes=[cb])
        st = self.wst.alloc()
        tb = Buf()
        P.dma("sp", st.t[:, 0:2048], I["consts"][:, C_AM:C_AM + 2048], writes=[tb])
        self.ts("dve", self.amask[:].rearrange("p a b -> p (a b)"), st.t[:, 0:2048], 30000.0, -30000.0, ALU.mult, ALU.add,
                rd=[tb], wr=[cb])
        st2 = self.wst.alloc()
        tb2 = Buf()
        P.dma("sp", st2.t[:, 0:128], I["consts"][:, C_AMS:C_AMS + 128], writes=[tb2])
        P.dma("sp", st2.t[:, 128:640], I["consts"][:, C_SCP:C_SCP + 512], writes=[tb2])
        self.cp("dve", self.scanb[:], st2.t[:, 128:640], rd=[tb2], wr=[cb])
        self.ts("dve", self.amask_s[:].rearrange("p a b -> p (a b)"), st2.t[:, 0:128], 30000.0, -30000.0, ALU.mult, ALU.add,
                rd=[tb2], wr=[cb])
        self.cp("dve", self.ntri[:], self.cA[:, C_NTRI:C_NTRI + 256], rd=[cb], wr=[cb])
        self.cp("dve", self.identb[:], self.cA[:, C_ID:C_ID + 128], rd=[cb], wr=[cb])
        self.cp("dve", self.onesb[:], self.cA[:, C_ONES:C_ONES + 128], rd=[cb], wr=[cb])
        tmpstk = contextlib.ExitStack()
        if "nos2" in DBG:
            st.free(); st2.free(); return
        rowt = P.sb("prmrows", [128, 128], stack=tmpstk)
        rows = rowt[0:104, 0:128]
        tb3 = Buf()

        def ld(r0, n, src):
            P.dma("sp", rowt[r0:r0 + n, 0:128], src, writes=[tb3])

        ld(0, 16, I["ffn1_norm"].rearrange("l (c k) -> (l c) k", k=128))
        ld(16, 16, I["mix_norm"].rearrange("l (c k) -> (l c) k", k=128))
        ld(32, 16, I["ffn2_norm"].rearrange("l (c k) -> (l c) k", k=128))
        ld(48, 16, I["ab_lb"].rearrange("l (c k) -> (l c) k", k=128))
        ld(64, 8, I["ab_g_out"][0])
        ld(72, 16, I["c_ln_g"].rearrange("l (c k) -> (l c) k", k=128))
        ld(88, 16, I["c_ln_b"].rearrange("l (c k) -> (l c) k", k=128))
        bk = self.banks.alloc()
        self.tp(bk.t[:, 0:104], rows, False, rd=[tb3], wr=[bk.b])
        self.cp("dve", self.prm[:], bk.t[:, 0:104], rd=[bk.b], wr=[cb])
        bk.free()
        self.tt("dve", self.lbp[:, 0, :], self.prm[:, 48:56], self.prm[:, 56:64], ALU.subtract, rd=[cb], wr=[cb])
        self.act(self.lbp[:, 0, :], self.lbp[:, 0, :], AF.Sigmoid, rd=[cb], wr=[cb])
        self.ts("dve", self.lbp[:, 1, :], self.lbp[:, 0, :], -1.0, 1.0, ALU.mult, ALU.add, rd=[cb], wr=[cb])
        self.ts("dve", self.lbp[:, 2, :], self.lbp[:, 0, :], 1.0, -1.0, ALU.mult, ALU.add, rd=[cb], wr=[cb])
        if "nos3" in DBG:
            st.free(); st2.free(); return
        P.dma("sp", self.gqk[:, 0, :], I["ab_g_q"][0:1, :].partition_broadcast(128), reads=[cb], writes=[cb])
        P.dma("sp", self.gqk[:, 1, :], I["ab_g_k"][0:1, :].partition_broadcast(128), reads=[cb], writes=[cb])
        self.ts1("dve", self.gqk[:, 0, :], self.gqk[:, 0, :], float(DH ** -0.5), ALU.mult, rd=[cb], wr=[cb])
        st.free()
        st2.free()
        P.barrier(engines=ENGS, dma_queues=("sp", "act", "pool"))
        tmpstk.close()

    def gmlp_consts(self, sample, wsT, BT, stack, tmps=None):
        P, I = self.P, self.i
        cb = self.cb
        sfx = "s" if sample else "p"
        if tmps is not None:
            wsf, wsTf, bsb = tmps
        else:
            wsf = P.sb("gc_wsf" + sfx, [128, 8, 128], stack=stack)
            wsTf = P.sb("gc_wsTf" + sfx, [128, 8, 128], stack=stack)
            bsb = P.sb("gc_bsb" + sfx, [128, 8, 128], stack=stack)
        b1, b2, b3 = Buf(), Buf(), Buf()
        if not sample:
            P.dma("sp", wsf[:], I["c_w_s"][0].rearrange("g i j -> i g j"), writes=[b1])
            gm = bsb[:, 0, :]
            P.dma("sp", gm, I["consts"][:, C_GMP:C_GMP + 128], writes=[b3])
            for g in range(8):
                self.tt("dve", wsf[:, g, :], wsf[:, g, :], gm, ALU.mult, rd=[b1, b3], wr=[b1, b3])
            P.dma("sp", bsb[:].rearrange("p g i -> p (g i)"),
                  I["c_b_s"][0:1].rearrange("o g i -> o (g i)").partition_broadcast(128), writes=[b3])
        else:
            P.op("pool", lambda e: e.memset(wsf[:], 0.0), writes=[b1])
            for b in range(4):
                P.dma("sp", wsf[b * 32:(b + 1) * 32, :, b * 32:(b + 1) * 32],
                      I["c_w_s"][0, :, 0:32, 0:32].rearrange("g i j -> i g j"), reads=[b1], writes=[b1])
                P.dma("sp", bsb[:, :, b * 32:(b + 1) * 32],
                      I["c_b_s"][0:1, :, 0:32].partition_broadcast(128), writes=[b3])
        if "g1" in DBG:
            return
        for g in range(8):
            bk = self.banks.alloc()
            self.tp(bk.t[:, 0:128], wsf[:, g, :], False, rd=[b1], wr=[bk.b])
            self.cp("dve", wsTf[:, g, :], bk.t[:, 0:128], rd=[bk.b], wr=[b2])
            self.cp("act", wsT[:, g, :], bk.t[:, 0:128], rd=[bk.b], wr=[cb])
            bk.free()
        if "g2" in DBG:
            return
        for g in range(8):
            bk = self.banks.alloc()
            self.mm(bk.t[:, 0:128], self.const(C_ONES), wsTf[:, g, :], True, True, rd=[b2, cb], wr=[bk.b])
            for cc in (2 * g, 2 * g + 1):
                self.stt("dve", BT[:, cc, :], bk.t[:, 0:128], self.prm[:, 88 + cc:89 + cc], bsb[:, g, :],
                         ALU.mult, ALU.add, rd=[bk.b, b3, cb], wr=[cb])
            bk.free()

    def load_x(self, ctx, src, xin_ring):
        P = self.P
        T = ctx["T"]
        for ti in range(T // 128):
            g = (ti * 128) // ctx["GN"]
            xin = xin_ring.alloc()
            P.dma("sp", xin.t[:, :], src[ti * 128:(ti + 1) * 128, :], writes=[xin.b])
            for half in range(2):
                bk = self.banks.alloc()
                for j in range(4):
                    c = half * 4 + j
                    self.tp(bk.t[:, j * 128:(j + 1) * 128], xin.t[:, c * 128:(c + 1) * 128], False, rd=[xin.b], wr=[bk.b])
                self.cp("act" if half == 0 else "dve",
                        ctx["xT"][:, half * 4:half * 4 + 4, ti * 128:(ti + 1) * 128],
                        bk.t[:, 0:512].rearrange("p (c t) -> p c t", c=4),
                        rd=[bk.b], wr=[ctx["xb"][g][half * 4 + j] for j in range(4)])
                bk.free()
            xin.free()

    def store_y(self, ctx, dst, ybuf_ring):
        P = self.P
        T = ctx["T"]
        for ti in range(T // 128):
            g = (ti * 128) // ctx["GN"]
            yb = ybuf_ring.alloc()
            for half in range(2):
                bk = self.banks.alloc()
                for j in range(4):
                    c = half * 4 + j
                    self.tp(bk.t[:, j * 128:(j + 1) * 128], ctx["xT"][:, c, ti * 128:(ti + 1) * 128], False,
                            rd=[ctx["xb"][g][c]], wr=[bk.b])
                self.cp("act" if half == 0 else "dve", yb.t[:, half * 512:(half + 1) * 512], bk.t[:, 0:512],
                        rd=[bk.b], wr=[yb.b])
                bk.free()
            P.dma("act", dst[ti * 128:(ti + 1) * 128, :], yb.t[:, :], reads=[yb.b])
            yb.free()

    def rstd_bc(self, src_bank, n, dim, explog=False):
        f = self.F.alloc()
        if explog:
            self.act(f.t[:, 0:n], src_bank.t[:, 0:n], AF.Ln, rd=[src_bank.b], wr=[f.b], scale=1.0 / dim, bias=EPS)
            self.act(f.t[:, 0:n], f.t[:, 0:n], AF.Exp, rd=[f.b], wr=[f.b], scale=-0.5)
        else:
            self.act(f.t[:, 0:n], src_bank.t[:, 0:n], AF.Sqrt, rd=[src_bank.b], wr=[f.b], scale=1.0 / dim, bias=EPS)
            self.P.op("dve", lambda e: e.reciprocal(out=f.t[:, 0:n], in_=f.t[:, 0:n]), reads=[f.b], writes=[f.b])
        return f

    def rmsnorm(self, ctx, gcol0):
        xT, hT = ctx["xT"], ctx["hT"]
        for g, (t0, n) in enumerate(ctx["groups"]):
            bk = self.banks.alloc()
            for c in range(KC):
                sq = self.Hh.alloc()
                self.act(sq.t[:, 0:n], xT[:, c, t0:t0 + n], AF.Square, rd=[ctx["xb"][g][c]], wr=[sq.b])
                self.mm(bk.t[:, 0:n], self.onesb[:], sq.t[:, 0:n], c == 0, c == KC - 1, rd=[sq.b, self.cb], wr=[bk.b])
                sq.free()
            rs = self.rstd_bc(bk, n, D)
            bk.free()
            for c in range(KC):
                self.stt("dve", hT[:, c, t0:t0 + n], xT[:, c, t0:t0 + n],
                         self.prm[:, gcol0 + c:gcol0 + c + 1], rs.t[:, 0:n], ALU.mult, ALU.mult,
                         rd=[ctx["xb"][g][c], rs.b, self.cb], wr=[ctx["hb"][g]])
            rs.free()

    def ffn(self, ctx, Wg, Wu, Wd):
        xT, hT = ctx["xT"], ctx["hT"]
        pending = None

        def down(wd, a, g, t0, n, free_wd):
            for d in range(KC):
                bk = self.banks.alloc()
                for j in range(2):
                    self.mm(bk.t[:, 0:n], wd.t[:, j * 1024 + d * 128: j * 1024 + (d + 1) * 128], a[j].t[:, 0:n],
                            j == 0, j == 1, rd=[wd.b, a[j].b], wr=[bk.b])
                self.stt("dve", xT[:, d, t0:t0 + n], bk.t[:, 0:n], 0.5, xT[:, d, t0:t0 + n], ALU.mult, ALU.add,
                         rd=[bk.b, ctx["xb"][g][d]], wr=[ctx["xb"][g][d]])
                bk.free()
            for j in range(2):
                a[j].free()
            if free_wd:
                wd.free()

        npieces = DFF // 256
        for p in range(npieces):
            f0 = p * 256
            view = lambda ap: ap.rearrange("p (c f) -> p c f", c=KC)
            wg = self.wload([(0, 2048, view, Wg.rearrange("(c k) f -> k c f", k=128)[:, :, f0:f0 + 256])], 2048, "alt")
            wu = self.wload([(0, 2048, view, Wu.rearrange("(c k) f -> k c f", k=128)[:, :, f0:f0 + 256])], 2048, "alt")
            viewd = lambda ap: ap.rearrange("p (j d) -> p j d", j=2)
            wd = self.wload([(0, 2048, viewd, Wd[f0:f0 + 256, :].rearrange("(j f) d -> f j d", f=128))], 2048, "alt")
            for g, (t0, n) in enumerate(ctx["groups"]):
                a = []
                for j in range(2):
                    bg = self.banks.alloc()
                    bu = self.banks.alloc()
                    for c in range(KC):
                        self.mm(bg.t[:, 0:n], wg.t[:, c * 256 + j * 128:c * 256 + (j + 1) * 128], hT[:, c, t0:t0 + n],
                                c == 0, c == KC - 1, rd=[wg.b, ctx["hb"][g]], wr=[bg.b])
                    for c in range(KC):
                        self.mm(bu.t[:, 0:n], wu.t[:, c * 256 + j * 128:c * 256 + (j + 1) * 128], hT[:, c, t0:t0 + n],
                                c == 0, c == KC - 1, rd=[wu.b, ctx["hb"][g]], wr=[bu.b])
                    sg = self.F.alloc()
                    self.act(sg.t[:, 0:n], bg.t[:, 0:n], AF.Silu, rd=[bg.b], wr=[sg.b])
                    bg.free()
                    aj = self.Hh.alloc()
                    self.tt("dve", aj.t[:, 0:n], sg.t[:, 0:n], bu.t[:, 0:n], ALU.mult, rd=[sg.b, bu.b], wr=[aj.b])
                    sg.free()
                    bu.free()
                    a.append(aj)
                if pending is not None:
                    down(*pending)
                last_g = g == len(ctx["groups"]) - 1
                pending = (wd, a, g, t0, n, last_g)
            wg.free()
            wu.free()
        down(*pending)

    def mixer_c(self, ctx, sample, wsT, BT, ar):
        P, I = self.P, self.i
        xT, hT = ctx["xT"], ctx["hT"]
        Win, Wout = I["c_w_in"][0], I["c_w_out"][0]
        view = lambda ap: ap.rearrange("p (c f) -> p c f", c=KC)
        viewd = lambda ap: ap.rearrange("p (j d) -> p j d", j=2)
        vn, vnb, stats, stb = ar["vn"], ar["vnb"], ar["stats"], ar["stb"]
        for g, (t0, n) in enumerate(ctx["groups"]):
            TG = n // 128
            P.op("pool", lambda e: e.memset(stats[:], 0.0), writes=[stb] + list(vnb))
            for u in range(8):
                wv = self.wload([(0, 2048, view, Win.rearrange("(c k) f -> k c f", k=128)[:, :, DC + u * 256:DC + (u + 1) * 256])], 2048, "alt")
                for t in range(TG):
                    bk = self.banks.alloc()
                    for c in range(KC):
                        self.mm(bk.t[:, 0:256], hT[:, c, t0 + t * 128:t0 + (t + 1) * 128], wv.t[:, c * 256:(c + 1) * 256],
                                c == 0, c == KC - 1, rd=[wv.b, ctx["hb"][g]], wr=[bk.b])
                    self.act(vn[:, t, u * 256:(u + 1) * 256], bk.t[:, 0:256], AF.Gelu_apprx_tanh, rd=[bk.b], wr=[vnb[t]],
                             accum_out=stats[:, t, 0, u:u + 1])
                    bk.free()
                    junk = self.F.alloc()
                    self.act(junk.t[:, 0:256], vn[:, t, u * 256:(u + 1) * 256], AF.Square, rd=[vnb[t]], wr=[junk.b, stb],
                             accum_out=stats[:, t, 1, u:u + 1])
                    junk.free()
                wv.free()
            mv = ar["mv"]
            for t in range(TG):
                P.op("dve", lambda e, t=t: e.reduce_sum(out=mv[:, t, 0:2], in_=stats[:, t, :, :], axis=mybir.AxisListType.X),
                     reads=[stb, vnb[t]], writes=[stb])
                self.ts1("dve", mv[:, t, 0:2], mv[:, t, 0:2], 1.0 / DC, ALU.mult, rd=[stb], wr=[stb])
                self.tt("dve", mv[:, t, 2:3], mv[:, t, 0:1], mv[:, t, 0:1], ALU.mult, rd=[stb], wr=[stb])
                self.tt("dve", mv[:, t, 2:3], mv[:, t, 1:2], mv[:, t, 2:3], ALU.subtract, rd=[stb], wr=[stb])
                self.act(mv[:, t, 2:3], mv[:, t, 2:3], AF.Sqrt, rd=[stb], wr=[stb], bias=EPS)
                P.op("dve", lambda e, t=t: e.reciprocal(out=mv[:, t, 3:4], in_=mv[:, t, 2:3]), reads=[stb], writes=[stb])
                self.ts("dve", vn[:, t, :], vn[:, t, :], mv[:, t, 0:1], mv[:, t, 3:4], ALU.subtract, ALU.mult,
                        rd=[stb, vnb[t]], wr=[vnb[t]])
                if sample:
                    vo = ar["vout"]
                    self.tt("dve", vo[:, :], vn[:, t, :], ar["lng"][:, :], ALU.mult, rd=[vnb[t], self.cb], wr=[ar["voutb"]])
                    self.tt("pool", vo[:, :], vo[:, :], ar["lnb"][:, :], ALU.add, rd=[self.cb], wr=[ar["voutb"]])
                    P.dma("act", self.o["gv"][:, :], vo[:, :], reads=[ar["voutb"]])
                    self.cp("pool", ar["vhb"][:, t, :], vn[:, t, :], rd=[vnb[t]], wr=[ar["vhbb"]])
            vh = ar["vhb"] if sample else vn
            vhrd = [ar["vhbb"]] if sample else [vnb[t] for t in range(TG)]
            pending = None

            def down(wo, a, t0=t0, n=n, g=g):
                for d in range(KC):
                    bk = self.banks.alloc()
                    for j in range(2):
                        self.mm(bk.t[:, 0:n], wo.t[:, j * 1024 + d * 128:j * 1024 + (d + 1) * 128], a[j].t[:, 0:n],
                                j == 0, j == 1, rd=[wo.b, a[j].b], wr=[bk.b])
                    self.stt("dve", xT[:, d, t0:t0 + n], bk.t[:, 0:n], 1.0, xT[:, d, t0:t0 + n], ALU.mult, ALU.add,
                             rd=[bk.b, ctx["xb"][g][d]], wr=[ctx["xb"][g][d]])
                    bk.free()
                for j in range(2):
                    a[j].free()
                wo.free()

            for p in range(8):
                wu = self.wload([(0, 2048, view, Win.rearrange("(c k) f -> k c f", k=128)[:, :, p * 256:(p + 1) * 256])], 2048, "alt")
                wo = self.wload([(0, 2048, viewd, Wout[p * 256:(p + 1) * 256, :].rearrange("(j f) d -> f j d", f=128))], 2048, "alt")
                a = []
                for j in range(2):
                    cc = 2 * p + j
                    bu = self.banks.alloc()
                    for c in range(KC):
                        self.mm(bu.t[:, 0:n], wu.t[:, c * 256 + j * 128:c * 256 + (j + 1) * 128], hT[:, c, t0:t0 + n],
                                c == 0, c == KC - 1, rd=[wu.b, ctx["hb"][g]], wr=[bu.b])
                    ug = self.F.alloc()
                    self.act(ug.t[:, 0:n], bu.t[:, 0:n], AF.Gelu_apprx_tanh, rd=[bu.b], wr=[ug.b])
                    bu.free()
                    bm = self.banks.alloc()
                    for t in range(TG):
                        self.mm(bm.t[:, t * 128:(t + 1) * 128], vh[:, t, cc * 128:(cc + 1) * 128], wsT[:, p, :],
                                True, True, rd=vhrd + [self.cb], wr=[bm.b])
                    t1 = self.F.alloc()
                    for t in range(TG):
                        self.stt("dve", t1.t[:, t * 128:(t + 1) * 128], bm.t[:, t * 128:(t + 1) * 128],
                                 self.prm[:, 72 + cc:73 + cc], BT[:, cc, :], ALU.mult, ALU.add,
                                 rd=[bm.b, self.cb], wr=[t1.b])
                    bm.free()
                    aj = self.Hh.alloc()
                    self.tt("dve", aj.t[:, 0:n], t1.t[:, 0:n], ug.t[:, 0:n], ALU.mult, rd=[t1.b, ug.b], wr=[aj.b])
                    t1.free()
                    ug.free()
                    a.append(aj)
                wu.free()
                if pending is not None:
                    down(*pending)
                pending = (wo, a)
            down(*pending)

    def attn_pairs(self, pairs, R, O, filler=None):
        cb = self.cb
        K = len(pairs)
        st = [dict() for _ in pairs]

        def scores(bank, p, last_stop):
            zs, QTv, qb_, avs, S, N, mask = p
            for j, (c0, c1, KTv, kb_) in enumerate(zs):
                self.mm(bank.t[0:S, c0:c1], KTv, QTv[:, c0:c1], j == 0, last_stop and mask is None and j == len(zs) - 1,
                        rd=[kb_, qb_], wr=[bank.b])
            if mask is not None:
                self.mm(bank.t[0:S, 0:N], self.identb[0:S, 0:S], mask, False, last_stop, rd=[cb], wr=[bank.b])

        def A(p, s):
            zs, QTv, qb_, avs, S, N, mask = p
            z = self.banks.alloc()
            scores(z, p, True)
            X = self.F.alloc()
            self.act(X.t[0:S, 0:N], z.t[0:S, 0:N], AF.Exp, rd=[z.b], wr=[X.b])
            z.free()
            E = self.FR.alloc()
            self.act(E.t[0:S, 0:N], X.t[0:S, 0:N], AF.Ln, rd=[X.b], wr=[E.b], bias=1.0)
            X.free()
            s["E"] = E

        def B(p, s, first, last):
            zs, QTv, qb_, avs, S, N, mask = p
            E = s["E"]
            L = self.banks.alloc()
            scores(L, p, False)
            self.mm(L.t[0:S, 0:N], self.ntri[0:S, 0:S], E.t[0:S, 0:N], False, first, rd=[E.b, cb], wr=[L.b])
            if not first:
                self.mm(L.t[0:S, 0:N], self.ntri[:, 128:128 + S], R.t[:, 0:N], False, True, rd=[R.b, cb], wr=[L.b])
            W = self.Hh.alloc()
            self.act(W.t[0:S, 0:N], L.t[0:S, 0:N], AF.Exp, rd=[L.b], wr=[W.b])
            L.free()
            s["W"] = W
            if not last:
                Rv, Ev = R.t[0:S, 0:N], E.t[0:S, 0:N]
                if first:
                    self.cp("dve", Rv, Ev, rd=[E.b], wr=[R.b])
                else:
                    self.tt("dve", Rv, Rv, Ev, ALU.add, rd=[E.b, R.b], wr=[R.b])
            E.free()

        def C(p, s, first, last):
            zs, QTv, qb_, avs, S, N, mask = p
            W = s["W"]
            for j, (c0, c1, Vv, vb_) in enumerate(avs):
                self.mm(O.t[:, c0:c1], Vv, W.t[0:S, c0:c1], first and j == 0, last and j == len(avs) - 1,
                        rd=[vb_, W.b], wr=[O.b])
            W.free()

        for i in range(K + 2):
            if i < K:
                A(pairs[i], st[i])
            if 0 <= i - 1 < K:
                B(pairs[i - 1], st[i - 1], i - 1 == 0, i - 1 == K - 1)
            if 0 <= i - 2 < K:
                C(pairs[i - 2], st[i - 2], i - 2 == 0, i - 2 == K - 1)
            if filler is not None:
                next(filler, None)

    def mixer_ab(self, ctx, sample, ar, seq_out):
        P, I, O_ = self.P, self.i, self.o
        cb = self.cb
        xT, hT = ctx["xT"], ctx["hT"]
        Win, Wout = I["ab_w_in"][0], I["ab_w_out"][0]
        Wv = Win.rearrange("(c k) (t h n) -> k c t h n", k=128, t=7, h=H)
        CH = 32 if sample else 64
        cpt = 128 // CH
        hmask = self.const(C_HMS if sample else C_HMP)
        scanm = self.const(C_SCS, 128) if sample else self.scanb[:, :]
        KT, Vb, kvb = ar["KT"], ar["Vb"], ar["kvb"]
        groups = ctx["groups"]
        NGp = len(groups)
        iters = [(h, g) for h in range(H) for g in range(NGp)]
        NI = len(iters)
        st = [dict() for _ in iters]
        wts = {}
        v2 = lambda ap: ap.rearrange("p (c t n) -> p c t n", c=KC, t=2)[:, :, 0, :]
        v2b = lambda ap: ap.rearrange("p (c t n) -> p c t n", c=KC, t=2)[:, :, 1, :]
        v1 = lambda ap: ap.rearrange("p (c n) -> p c n", c=KC)

        def load_head(h):
            blk = lambda t: Wv[:, :, t, h, :]
            w1 = self.wload([(0, 2048, v2, blk(0)), (0, 2048, v2b, blk(1))], 2048)
            w2 = self.wload([(0, 1024, v1, blk(3))], 1024)
            w3 = self.wload([(0, 2048, v2, blk(2)), (0, 2048, v2b, blk(4))], 2048)
            w4 = self.wload([(0, 2048, v2, blk(5)), (0, 2048, v2b, blk(6))], 2048)
            wo = self.wload([(0, 1024, None, Wout[h * 128:(h + 1) * 128, :]),
                             (1024, 1024, None, Wout[D + h * 128:D + (h + 1) * 128, :])], 2048)
            wts[h] = (w1, w2, w3, w4, wo)

        def P1a(n):
            h, g = iters[n]
            s = st[n]
            t0, nn = groups[g]
            TG = nn // 128
            par = n % 2
            if h not in wts:
                load_head(h)
            w1, w2, w3, w4, wo = wts[h]
            hb = ctx["hb"][g]
            bq, bf_, bg_ = self.banks.alloc(), self.banks.alloc(), self.banks.alloc()
            for c in range(KC):
                self.mm(bq.t[:, 0:nn], w1.t[:, c * 256:c * 256 + 128], hT[:, c, t0:t0 + nn], c == 0, c == KC - 1,
                        rd=[w1.b, hb], wr=[bq.b])
            yield
            for c in range(KC):
                self.mm(bf_.t[:, 0:nn], w1.t[:, c * 256 + 128:c * 256 + 256], hT[:, c, t0:t0 + nn], c == 0, c == KC - 1,
                        rd=[w1.b, hb], wr=[bf_.b])
            for c in range(KC):
                self.mm(bg_.t[:, 0:nn], w2.t[:, c * 128:(c + 1) * 128], hT[:, c, t0:t0 + nn], c == 0, c == KC - 1,
                        rd=[w2.b, hb], wr=[bg_.b])
            yield
            T1, T2, T3, T4, T5 = [self.F.alloc() for _ in range(5)]
            def sigm(T):
                self.ts1("dve", T.t[:, 0:nn], T.t[:, 0:nn], 1.0, ALU.add, rd=[T.b], wr=[T.b])
                P.op("dve", lambda e, T=T: e.reciprocal(out=T.t[:, 0:nn], in_=T.t[:, 0:nn]), reads=[T.b], writes=[T.b])

            self.act(T1.t[:, 0:nn], bf_.t[:, 0:nn], AF.Exp, rd=[bf_.b], wr=[T1.b], scale=-1.0)
            bf_.free()
            self.act(T5.t[:, 0:nn], bg_.t[:, 0:nn], AF.Exp, rd=[bg_.b], wr=[T5.b], scale=-1.0)
            bg_.free()
            sigm(T1)
            self.act(T2.t[:, 0:nn], T1.t[:, 0:nn], AF.Ln, rd=[T1.b, cb], wr=[T2.b],
                     scale=self.lbp[:, 1, h:h + 1], bias=self.lbp[:, 0, h:h + 1])
            P.op("dve", lambda e, T3=T3, T2=T2, nn=nn: e.tensor_tensor_scan(
                out=T3.t[:, 0:nn], data0=scanm[:, 0:nn], data1=T2.t[:, 0:nn], initial=0.0, op0=ALU.mult, op1=ALU.add),
                reads=[T2.b, cb], writes=[T3.b])
            self.act(T4.t[:, 0:nn], T3.t[:, 0:nn], AF.Exp, rd=[T3.b], wr=[T4.b])
            self.act(T2.t[:, 0:nn], T3.t[:, 0:nn], AF.Exp, rd=[T3.b, T2.b], wr=[T2.b], scale=-1.0)
            self.act(T3.t[:, 0:nn], bq.t[:, 0:nn], AF.Exp, rd=[bq.b, T3.b], wr=[T3.b], scale=-1.0)
            sigm(T5)
            self.ts("dve", T1.t[:, 0:nn], T1.t[:, 0:nn], self.lbp[:, 2, h:h + 1], self.lbp[:, 1, h:h + 1],
                    ALU.mult, ALU.add, rd=[T1.b, cb], wr=[T1.b])
            kt = self.Hh.alloc()
            self.tt("dve", kt.t[:, 0:nn], T1.t[:, 0:nn], T2.t[:, 0:nn], ALU.mult, rd=[T1.b, T2.b], wr=[kt.b])
            sigm(T3)
            self.tt("dve", T3.t[:, 0:nn], T3.t[:, 0:nn], bq.t[:, 0:nn], ALU.mult, rd=[T3.b, bq.b], wr=[T3.b])
            bq.free()
            qt = self.Hh.alloc()
            self.tt("dve", qt.t[:, 0:nn], T3.t[:, 0:nn], T4.t[:, 0:nn], ALU.mult, rd=[T3.b, T4.b], wr=[qt.b])
            nch = nn // CH
            ecl = ar["ecl"][:, par, :]
            eclb = ar["eclb"][par]
            self.cp("pool", ecl[:, 0:nch], T4.t[:, CH - 1:nn:CH], rd=[T4.b], wr=[eclb])
            T1.free(); T2.free(); T3.free(); T4.free()
            va, Kst, Vst = ar["va"][par], ar["Kst"], ar["Vst"]
            vab = ar["vab"][par]
            qk = []
            for t in range(TG):
                yield
                bt = self.banks.alloc()
                cols = slice(t0 + t * 128, t0 + (t + 1) * 128)
                for c in range(KC):
                    self.mm(bt.t[:, 0:256], hT[:, c, cols], w3.t[:, c * 256:(c + 1) * 256], c == 0, c == KC - 1,
                            rd=[w3.b, hb], wr=[bt.b])
                for c in range(KC):
                    self.mm(bt.t[:, 256:512], hT[:, c, cols], w4.t[:, c * 256:(c + 1) * 256], c == 0, c == KC - 1,
                            rd=[w4.b, hb], wr=[bt.b])
                self.cp("act", va[:, t, :], bt.t[:, 0:128], rd=[bt.b], wr=[vab[t]])
                ss = ar["ss"][:, t, :]
                ssb = ar["ssb"][t]
                junk = self.Hh.alloc()
                self.act(junk.t[:, 0:128], bt.t[:, 128:256], AF.Square, rd=[bt.b], wr=[junk.b, ssb], accum_out=ss[:, 0:1])
                self.act(junk.t[:, 128:256], bt.t[:, 256:384], AF.Square, rd=[bt.b], wr=[junk.b, ssb], accum_out=ss[:, 1:2])
                junk.free()
                self.cp("act", Vst[:, t, :], bt.t[:, 384:512], rd=[bt.b], wr=[ar["Vstb"][t]])
                yield
                self.act(ss[:, 2:4], ss[:, 0:2], AF.Ln, rd=[ssb], wr=[ssb], scale=1.0 / DH, bias=EPS)
                self.act(ss[:, 2:4], ss[:, 2:4], AF.Exp, rd=[ssb], wr=[ssb], scale=-0.5)
                if t % 2 == 0:
                    qk.append(self.Hh.alloc())
                qs = qk[-1]
                o0 = (t % 2) * 256
                self.stt("dve", qs.t[:, o0:o0 + 128], bt.t[:, 128:256], ss[:, 2:3], self.gqk[:, 0, :], ALU.mult, ALU.mult,
                         rd=[bt.b, ssb, cb], wr=[qs.b])
                self.stt("dve", Kst[:, t, :], bt.t[:, 256:384], ss[:, 3:4], self.gqk[:, 1, :], ALU.mult, ALU.mult,
                         rd=[bt.b, ssb, cb], wr=[ar["Kstb"][t]])
                bt.free()
                self.cp("pool", qs.t[:, o0 + 128:o0 + 256], Kst[:, t, :], rd=[ar["Kstb"][t], qs.b], wr=[qs.b])
            s.update(kt=kt, qt=qt, T5=T5, qk=qk, par=par)
            if g == NGp - 1:
                for w in (w1, w2, w3, w4):
                    w.free()
                if NGp > 2 and h + 1 < H and "nopf" not in DBG:
                    load_head(h + 1)

        def P1b(n):
            h, g = iters[n]
            s = st[n]
            t0, nn = groups[g]
            TG = nn // 128
            kt = s["kt"]
            ktm, Kst, Vst = ar["ktm"], ar["Kst"], ar["Vst"]
            QT = self.Hh.alloc()
            for t in range(TG):
                gt = (t0 // 128) + t
                qs = s["qk"][t // 2]
                o0 = (t % 2) * 256
                self.cp("pool", Vb[:, gt, :], Vst[:, t, :], rd=[ar["Vstb"][t]], wr=[kvb[gt]])
                bx = self.banks.alloc()
                bxv = bx.t[:].bitcast(BF16)
                self.tp(bxv[:, 0:128], qs.t[:, o0:o0 + 128], True, rd=[qs.b], wr=[bx.b])
                self.tp(bxv[:, 128:256], qs.t[:, o0 + 128:o0 + 256], True, rd=[qs.b], wr=[bx.b])
                self.tp(bxv[:, 256:384], kt.t[:, t * 128:(t + 1) * 128], True, rd=[kt.b], wr=[bx.b])
                self.cp("act", QT.t[:, t * 128:(t + 1) * 128], bxv[:, 0:128], rd=[bx.b], wr=[QT.b])
                self.cp("dve", KT[:, gt * 128:(gt + 1) * 128], bxv[:, 128:256], rd=[bx.b], wr=[kvb[gt]])
                self.cp("dve", ktm[:, t, :], bxv[:, 256:384], rd=[bx.b], wr=[ar["ktmb"][t]])
                bx.free()
            for qs in s["qk"]:
                qs.free()
            s["QT"] = QT
            kdst = (O_["ks"] if sample else O_["kp"][seq_out])
            vdst = (O_["vs"] if sample else O_["vp"][seq_out])
            P.dma("act", kdst[t0:t0 + nn, h, :].rearrange("(t p) d -> p t d", p=128), Kst[:, 0:TG, :], reads=ar["Kstb"][0:TG])
            P.dma("act", vdst[t0:t0 + nn, h, :].rearrange("(t p) d -> p t d", p=128), Vst[:, 0:TG, :], reads=ar["Vstb"][0:TG])

        def P2a(n):
            h, g = iters[n]
            s = st[n]
            t0, nn = groups[g]
            par = s["par"]
            nch = nn // CH
            va, vab, ktm = ar["va"][par], ar["vab"][par], ar["ktm"]
            ecl, eclb = ar["ecl"][:, par, :], ar["eclb"][par]
            Sbf, Sbfb, S, Sb = ar["Sbf"], ar["Sbfb"], ar["S"], ar["Sb"]
            if sample:
                S0, S0b, S0bf, S0bfb = ar["S0"], ar["S0b"], ar["S0bf"], ar["S0bfb"]
                ktmm = ar["ktmm"]
                for b in range(4):
                    P.dma("sp", S0[:, b, :], I["sh"][b, h], writes=[S0b[b]])
                    self.cp("pool", S0bf[:, b, :], S0[:, b, :], rd=[S0b[b]], wr=[S0bfb[b]])
                    self.ts1("dve", ktmm[:, b, :], ktm[:, 0, :], self.cA[:, C_RMS + b:C_RMS + b + 1], ALU.mult,
                             rd=[ar["ktmb"][0], cb], wr=[ar["ktmmb"]])
            bDs = []
            if sample:
                dmap = lambda i: (i // 4, i % 4)
                nbD = (nch + 3) // 4
            else:
                dmap = lambda i: ((i % cpt) + cpt * ((i // cpt) // 4), (i // cpt) % 4)
                nbD = cpt * ((nch // cpt + 3) // 4)

            def scan_step(i):
                bD = bDs[dmap(i)[0]]
                dcols = slice(dmap(i)[1] * 128, (dmap(i)[1] + 1) * 128)
                if sample:
                    Sn = ar["Sn"]
                    self.tt("dve", Sn[:, i, :], bD.t[:, dcols], S0[:, i, :], ALU.add, rd=[bD.b, S0b[i]], wr=[ar["Snb"][i]])
                    self.ts1("dve", Sn[:, i, :], Sn[:, i, :], ecl[:, i:i + 1], ALU.mult, rd=[eclb, ar["Snb"][i]], wr=[ar["Snb"][i]])
                    P.dma("act", O_["hs"][i, h], Sn[:, i, :], reads=[ar["Snb"][i]])
                else:
                    gi = (t0 // CH) + i
                    cur, prv = gi % 2, (gi + 1) % 2
                    if gi == 0:
                        self.ts1("dve", S[:, cur, :], bD.t[:, dcols], ecl[:, i:i + 1], ALU.mult, rd=[bD.b, eclb], wr=[Sb[cur]])
                    else:
                        self.ts1("dve", S[:, cur, :], S[:, prv, :], ecl[:, i:i + 1], ALU.mult, rd=[Sb[prv], eclb], wr=[Sb[cur]])
                        self.stt("dve", S[:, cur, :], bD.t[:, dcols], ecl[:, i:i + 1], S[:, cur, :], ALU.mult, ALU.add,
                                 rd=[bD.b, eclb, Sb[cur]], wr=[Sb[cur]])
                    self.cp("dve", Sbf[:, i + 1, :], S[:, cur, :], rd=[Sb[cur]], wr=[Sbfb[i + 1]])
                    if gi == self.SEQ // CH - 1:
                        P.dma("act", O_["hp"][seq_out, h], S[:, cur, :], reads=[Sb[cur]])
            for _ in range(nbD):
                bDs.append(self.banks.alloc())
            for i in range(nch):
                bD = bDs[dmap(i)[0]]
                t, pb = i // cpt, (i % cpt) * CH
                dcols = slice(dmap(i)[1] * 128, (dmap(i)[1] + 1) * 128)
                if sample:
                    self.mm(bD.t[:, dcols], ktmm[:, i, :], va[:, 0, :], True, True, rd=[ar["ktmmb"], vab[0]], wr=[bD.b])
                else:
                    self.mm(bD.t[:, dcols], ktm[pb:pb + CH, t, :], va[pb:pb + CH, t, :], True, True,
                            rd=[ar["ktmb"][t], vab[t]], wr=[bD.b])
                if "nodf" in DBG:
                    scan_step(i)
            if "nodf" not in DBG:
                for i in range(nch):
                    scan_step(i)
            for bD in bDs:
                bD.free()

        def PA(n):
            h, g = iters[n]
            s = st[n]
            t0, nn = groups[g]
            TG = nn // 128
            kt, qt = s["kt"], s["qt"]
            bA = self.banks.alloc()
            ATm = self.Hh.alloc()
            for t in range(TG):
                cs = slice(t * 128, (t + 1) * 128)
                self.mm(bA.t[:, cs], kt.t[:, cs], qt.t[:, cs], True, True, rd=[kt.b, qt.b], wr=[bA.b])
            self.tt("dve", ATm.t[:, 0:nn].rearrange("p (t c) -> p t c", c=128), bA.t[:, 0:nn].rearrange("p (t c) -> p t c", c=128),
                    hmask.unsqueeze(1).to_broadcast([128, TG, 128]), ALU.mult, rd=[bA.b, cb], wr=[ATm.b])
            bA.free()
            kt.free()
            s["ATm"] = ATm

        def PO(n):
            h, g = iters[n]
            s = st[n]
            t0, nn = groups[g]
            TG = nn // 128
            par = s["par"]
            nch = nn // CH
            qt, T5, ATm = s["qt"], s["T5"], s["ATm"]
            va, vab = ar["va"][par], ar["vab"][par]
            Sbf, Sbfb = ar["Sbf"], ar["Sbfb"]
            bO = self.banks.alloc()
            for t in range(TG):
                cs = slice(t * 128, (t + 1) * 128)
                self.mm(bO.t[:, cs], va[:, t, :], ATm.t[:, cs], True, False, rd=[vab[t], ATm.b], wr=[bO.b])
                for ci in range(cpt):
                    i = t * cpt + ci
                    ccs = slice(i * CH, (i + 1) * CH)
                    lastmm = ci == cpt - 1
                    if sample:
                        self.mm(bO.t[:, ccs], ar["S0bf"][:, i, :], qt.t[:, ccs], False, lastmm,
                                rd=[ar["S0bfb"][i], qt.b], wr=[bO.b])
                    else:
                        gi = (t0 // CH) + i
                        if gi == 0:
                            continue
                        self.mm(bO.t[:, ccs], Sbf[:, i, :], qt.t[:, ccs], False, lastmm, rd=[Sbfb[i], qt.b], wr=[bO.b])
            ATm.free()
            qt.free()
            if not sample and g < NGp - 1:
                self.cp("dve", Sbf[:, 0, :], Sbf[:, nch, :], rd=[Sbfb[nch]], wr=[Sbfb[0]])
            sq = self.Hh.alloc()
            self.act(sq.t[:, 0:nn], bO.t[:, 0:nn], AF.Square, rd=[bO.b], wr=[sq.b])
            bS = self.banks.alloc()
            self.mm(bS.t[:, 0:nn], self.onesb[:], sq.t[:, 0:nn], True, True, rd=[sq.b, cb], wr=[bS.b])
            sq.free()
            rs = self.rstd_bc(bS, nn, DH, explog=True)
            bS.free()
            self.tt("dve", rs.t[:, 0:nn], bO.t[:, 0:nn], rs.t[:, 0:nn], ALU.mult, rd=[bO.b, rs.b], wr=[rs.b])
            bO.free()
            mixa = self.Hh.alloc()
            self.stt("dve", mixa.t[:, 0:nn], rs.t[:, 0:nn], self.prm[:, 64 + h:65 + h], T5.t[:, 0:nn], ALU.mult, ALU.mult,
                     rd=[rs.b, T5.b, cb], wr=[mixa.b])
            rs.free()
            T5.free()
            s["mixa"] = mixa

        def P3(n, filler=None):
            h, g = iters[n]
            s = st[n]
            t0, nn = groups[g]
            QT = s["QT"]
            mixb = self.Hh.alloc()
            if not sample:
                R = self.FR.alloc()
                Ob = self.banks.alloc()
                kts = list(range((t0 + nn) // 128 - 1, -1, -1))
                g0 = t0 // 128
                pairs = []
                for kbi in kts:
                    kd = kbi - g0
                    mask = self.amask[:, kd, 0:nn] if kd >= 0 else None
                    pairs.append(([(0, nn, KT[:, kbi * 128:(kbi + 1) * 128], kvb[kbi])], QT.t[:, 0:nn], QT.b,
                                  [(0, nn, Vb[:, kbi, :], kvb[kbi])], 128, nn, mask))
                self.attn_pairs(pairs, R, Ob, filler)
                self.cp("act", mixb.t[:, 0:nn], Ob.t[:, 0:nn], rd=[Ob.b], wr=[mixb.b])
                Ob.free()
                R.free()
            else:
                npt = self.PAST // 128
                vk = lambda ap: ap.rearrange("p (t d) -> p t d", d=128)
                vcs = []
                for b in range(4):
                    kc = self.wload([(0, npt * 128, vk, I["ck"][b, :, h, :].rearrange("(t p) d -> p t d", p=128))], npt * 128, "dve")
                    vcs.append(self.wload([(0, npt * 128, vk, I["cv"][b, :, h, :].rearrange("(t p) d -> p t d", p=128))], npt * 128, "dve"))
                    KTp, KTpb = ar["KTp"][b], ar["KTpb"][b]
                    for t8 in range(0, npt, 8):
                        bx = self.banks.alloc()
                        bxv = bx.t[:].bitcast(BF16)
                        m = min(8, npt - t8)
                        for j in range(m):
                            self.tp(bxv[:, j * 128:(j + 1) * 128], kc.t[:, (t8 + j) * 128:(t8 + j + 1) * 128], True,
                                    rd=[kc.b], wr=[bx.b])
                        self.cp("act" if (t8 // 8) % 2 == 0 else "dve", KTp[:, t8 * 128:(t8 + m) * 128], bxv[:, 0:m * 128],
                                rd=[bx.b], wr=[KTpb])
                        bx.free()
                    kc.free()
                R = self.FR.alloc()
                Ob = self.banks.alloc()
                pairs = [([(0, 128, KT[:, 0:128], kvb[0])], QT.t[:, 0:128], QT.b, [(0, 128, Vb[:, 0, :], kvb[0])], 128, 128,
                          self.amask_s[:].rearrange("p a b -> p (a b)"))]
                for kbi in range(npt - 1, -1, -1):
                    zs = [(b * 32, (b + 1) * 32, ar["KTp"][b][:, kbi * 128:(kbi + 1) * 128], ar["KTpb"][b]) for b in range(4)]
                    avs = [(b * 32, (b + 1) * 32, vcs[b].t[:, kbi * 128:(kbi + 1) * 128], vcs[b].b) for b in range(4)]
                    pairs.append((zs, QT.t[:, 0:128], QT.b, avs, 128, 128, None))
                self.attn_pairs(pairs, R, Ob, filler)
                self.cp("act", mixb.t[:, 0:128], Ob.t[:, 0:128], rd=[Ob.b], wr=[mixb.b])
                Ob.free()
                R.free()
                for vc in vcs:
                    vc.free()
            QT.free()
            s["mixb"] = mixb

        def P4(n):
            h, g = iters[n]
            s = st[n]
            t0, nn = groups[g]
            wo = wts[h][4]
            mixa, mixb = s["mixa"], s["mixb"]
            for d in range(KC):
                bk = self.banks.alloc()
                self.mm(bk.t[:, 0:nn], wo.t[:, d * 128:(d + 1) * 128], mixa.t[:, 0:nn], True, False, rd=[wo.b, mixa.b], wr=[bk.b])
                self.mm(bk.t[:, 0:nn], wo.t[:, 1024 + d * 128:1024 + (d + 1) * 128], mixb.t[:, 0:nn], False, True,
                        rd=[wo.b, mixb.b], wr=[bk.b])
                self.stt("dve", xT[:, d, t0:t0 + nn], bk.t[:, 0:nn], 1.0, xT[:, d, t0:t0 + nn], ALU.mult, ALU.add,
                         rd=[bk.b, ctx["xb"][g][d]], wr=[ctx["xb"][g][d]])
                bk.free()
            mixa.free()
            mixb.free()
            if g == NGp - 1:
                wo.free()

        for _ in P1a(0):
            pass
        P1b(0)
        for n in range(NI):
            nxt = n + 1 < NI
            gen = P1a(n + 1) if nxt else iter(())
            if "tmlate" in DBG:
                for _ in range(3):
                    next(gen, None)
            elif "fill" not in DBG:
                for _ in gen:
                    pass
            else:
                next(gen, None)
            if "v3" in DBG:
                PA(n)
                P2a(n)
            else:
                P2a(n)
                PA(n)
            if n > 0 and "p4late" not in DBG:
                P4(n - 1)
            P3(n, gen if "fill" in DBG else None)
            for _ in gen:
                pass
            if "v2" in DBG:
                if nxt:
                    P1b(n + 1)
                PO(n)
            else:
                PO(n)
                if nxt:
                    P1b(n + 1)
            if n > 0 and "p4late" in DBG:
                P4(n - 1)
        P4(NI - 1)

    def run_pass(self, ctx, src, ydst, sample, seq_out, ar, wsT, BT, io_ring):
        I = self.i
        dbg = DBG
        mk = self.P.mark
        mk("load_x")
        self.load_x(ctx, src, io_ring)
        for l in range(2):
            if "ffn1" in dbg:
                mk(f"L{l}.norm1")
                self.rmsnorm(ctx, 0 + l * 8)
                mk(f"L{l}.ffn1")
                self.ffn(ctx, I["ffn1_w_gate"][l], I["ffn1_w_up"][l], I["ffn1_w_down"][l])
            if "norm" in dbg:
                mk(f"L{l}.normM")
                self.rmsnorm(ctx, 16 + l * 8)
            self.P.barrier()
            mk(f"L{l}.mixer")
            if l == 0:
                if "ab" in dbg:
                    self.mixer_ab(ctx, sample, ar, seq_out)
            else:
                if "c" in dbg:
                    self.mixer_c(ctx, sample, wsT, BT, ar)
            self.P.barrier()
            if "ffn2" in dbg:
                mk(f"L{l}.norm2")
                self.rmsnorm(ctx, 32 + l * 8)
                mk(f"L{l}.ffn2")
                self.ffn(ctx, I["ffn2_w_gate"][l], I["ffn2_w_up"][l], I["ffn2_w_down"][l])
        mk("store_y")
        self.store_y(ctx, ydst, io_ring)
        mk("end")

    def build(self):
        P = self.P
        SEQ, NP = self.SEQ, self.NP
        with contextlib.ExitStack() as stk:
            GN = min(512, SEQ)
            NG = SEQ // GN
            xT = P.sb("xT", [128, KC, SEQ], stack=stk)
            hT = P.sb("hT", [128, KC, SEQ], BF16, stack=stk)
            arena = P.sb("arena", [128, 4800], stack=stk)
            wsT = P.sb("wsT", [128, 8, 128], BF16, stack=stk)
            BT = P.sb("BT", [128, 16, 128], stack=stk)
            with contextlib.ExitStack() as tmp:
                if "nog" not in DBG:
                    tmps = [arena[:, k * 1024:(k + 1) * 1024].rearrange("p (g j) -> p g j", g=8) for k in range(3)]
                    self.gmlp_consts(False, wsT, BT, tmp, tmps)
                P.barrier()
            off = [0]

            def carve(nf32, dt=F32, shape=None):
                ap = arena[:, off[0]:off[0] + nf32]
                off[0] += nf32
                if dt == BF16:
                    ap = ap.bitcast(BF16)
                return ap

            TGm = GN // 128
            nchm = GN // 64
            ar = {}
            ar["KT"] = carve(SEQ // 2, BF16)
            ar["Vb"] = carve(SEQ // 2, BF16).rearrange("p (t d) -> p t d", d=128)
            ar["kvb"] = [Buf() for _ in range(SEQ // 128)]
            ar["va"] = [carve(TGm * 64, BF16).rearrange("p (t d) -> p t d", d=128) for _ in range(2)]
            ar["vab"] = [[Buf() for _ in range(TGm)] for _ in range(2)]
            ar["ktm"] = carve(TGm * 64, BF16).rearrange("p (t d) -> p t d", d=128); ar["ktmb"] = [Buf() for _ in range(TGm)]
            ar["Sbf"] = carve((nchm + 1) * 64, BF16).rearrange("p (t d) -> p t d", d=128)
            ar["Sbfb"] = [Buf() for _ in range(nchm + 1)]
            ar["Kst"] = carve(TGm * 128).rearrange("p (t d) -> p t d", d=128); ar["Kstb"] = [Buf() for _ in range(TGm)]
            ar["Vst"] = carve(TGm * 128).rearrange("p (t d) -> p t d", d=128); ar["Vstb"] = [Buf() for _ in range(TGm)]
            ar["S"] = carve(256).rearrange("p (t d) -> p t d", d=128); ar["Sb"] = [Buf(), Buf()]
            ar["ecl"] = carve(32).rearrange("p (a c) -> p a c", a=2); ar["eclb"] = [Buf(), Buf()]
            ar["ss"] = carve(4 * TGm).rearrange("p (t a) -> p t a", a=4); ar["ssb"] = [Buf() for _ in range(TGm)]
            ab_end = off[0]
            off[0] = 0
            ar["vn"] = carve(TGm * 1024, BF16).rearrange("p (t f) -> p t f", f=DC)
            ar["vnb"] = [Buf() for _ in range(TGm)]
            ar["stats"] = carve(TGm * 16).rearrange("p (t a u) -> p t a u", a=2, u=8); ar["stb"] = Buf()
            ar["mv"] = carve(TGm * 4).rearrange("p (t a) -> p t a", a=4)
            c_end = off[0]
            io_ring = Ring([arena[:, k * 1024:(k + 1) * 1024] for k in range(4)], "io")
            assert max(ab_end, c_end, 4096) <= 4800, (ab_end, c_end)
            for s in range(NP if "prompt" in DBG else 0):
                ctx = dict(xT=xT, hT=hT, T=SEQ, GN=GN, groups=[(g * GN, GN) for g in range(NG)],
                           xb=[[Buf() for _ in range(KC)] for _ in range(NG)], hb=[Buf() for _ in range(NG)])
                P.barrier(engines=ENGS, dma_queues=("sp", "act", "pool"))
                self.run_pass(ctx, self.i["xp"][s], self.o["yp"][s], False, s, ar, wsT, BT, io_ring)
        P.barrier(engines=ENGS, dma_queues=("sp", "act", "pool"))
        with contextlib.ExitStack() as stk:
            xT = P.sb("xTs", [128, KC, 128], stack=stk)
            hT = P.sb("hTs", [128, KC, 128], BF16, stack=stk)
            wsT = P.sb("wsTs", [128, 8, 128], BF16, stack=stk)
            BT = P.sb("BTs", [128, 16, 128], stack=stk)
            with contextlib.ExitStack() as tmp:
                if "nogs" not in DBG:
                    self.gmlp_consts(True, wsT, BT, tmp)
                P.barrier()
            ar = {}
            sbt = lambda name, shape, dt=F32: P.sb("s_" + name, shape, dt, stack=stk)
            self.wbf.extend([sbt(f"xwbf{i}", [128, 2048], BF16) for i in range(6)])
            self.wst.extend([sbt(f"xwst{i}", [128, 2048]) for i in range(2)])
            self.wst_parts += [[Buf() for _ in range(4)] for _ in range(2)]
            self.Hh.extend([sbt(f"xH{i}", [128, 512], BF16) for i in range(1)])
            ar["KT"] = sbt("KT", [128, 128], BF16)
            ar["Vb"] = sbt("Vb", [128, 1, 128], BF16); ar["kvb"] = [Buf()]
            ar["va"] = [sbt("va0", [128, 1, 128], BF16), sbt("va1", [128, 1, 128], BF16)]; ar["vab"] = [[Buf()], [Buf()]]
            ar["ktm"] = sbt("ktm", [128, 1, 128], BF16); ar["ktmb"] = [Buf()]
            ar["ktmm"] = sbt("ktmm", [128, 4, 128], BF16); ar["ktmmb"] = Buf()
            ar["Sbf"] = None; ar["Sbfb"] = None; ar["S"] = None; ar["Sb"] = None
            ar["Kst"] = sbt("Kst", [128, 1, 128]); ar["Kstb"] = [Buf()]
            ar["Vst"] = sbt("Vst", [128, 1, 128]); ar["Vstb"] = [Buf()]
            ar["ecl"] = sbt("ecl", [128, 2, 16]); ar["eclb"] = [Buf(), Buf()]
            ar["ss"] = sbt("ss", [128, 1, 4]); ar["ssb"] = [Buf()]
            ar["S0"] = sbt("S0", [128, 4, 128]); ar["S0b"] = [Buf() for _ in range(4)]
            ar["S0bf"] = sbt("S0bf", [128, 4, 128], BF16); ar["S0bfb"] = [Buf() for _ in range(4)]
            ar["Sn"] = sbt("Sn", [128, 4, 128]); ar["Snb"] = [Buf() for _ in range(4)]
            ar["KTp"] = [sbt(f"KTp{b}", [128, self.PAST], BF16) for b in range(4)]; ar["KTpb"] = [Buf() for _ in range(4)]
            ar["vn"] = sbt("vn", [128, 1, DC]); ar["vnb"] = [Buf()]
            ar["stats"] = sbt("stats", [128, 1, 2, 8]); ar["stb"] = Buf()
            ar["mv"] = sbt("mv", [128, 1, 4])
            ar["vout"] = sbt("vout", [128, DC]); ar["voutb"] = Buf()
            ar["vhb"] = sbt("vhb", [128, 1, DC], BF16); ar["vhbb"] = Buf()
            ar["lng"] = sbt("lng", [128, DC]); ar["lnb"] = sbt("lnb", [128, DC])
            P.dma("sp", ar["lng"][:, :], self.i["c_ln_g"][0:1, :].partition_broadcast(128), reads=[self.cb], writes=[self.cb])
            P.dma("sp", ar["lnb"][:, :], self.i["c_ln_b"][0:1, :].partition_broadcast(128), reads=[self.cb], writes=[self.cb])
            io_ring = Ring([sbt("io0", [128, 1024]), sbt("io1", [128, 1024])], "ios")
            ctx = dict(xT=xT, hT=hT, T=128, GN=128, groups=[(0, 128)],
                       xb=[[Buf() for _ in range(KC)]], hb=[Buf()])
            if "sample" in DBG:
                self.run_pass(ctx, self.i["xs"], self.o["ys"], True, None, ar, wsT, BT, io_ring)
            P.barrier()
        P.finish()
        if _os.environ.get("KMARKS"):
            import json
            json.dump(P.marks, open(_os.environ["KMARKS"], "w"))
        return self.nc


_CACHE = {}


def _get_nc(key):
    if key not in _CACHE:
        _CACHE[key] = Builder(*key).build()
    return _CACHE[key]


WEIGHT_KEYS = ["ffn1_norm", "ffn1_w_gate", "ffn1_w_up", "ffn1_w_down", "mix_norm", "ffn2_norm", "ffn2_w_gate",
               "ffn2_w_up", "ffn2_w_down", "ab_w_in", "ab_lb", "ab_g_out", "ab_g_q", "ab_g_k", "ab_w_out", "c_w_in",
               "c_ln_g", "c_ln_b", "c_w_s", "c_b_s", "c_w_out"]


def kernel(x_prompt, x_sample, cache_sb_k, cache_sb_v, state_hgrn, n_cores=8, **w):
    x_prompt = np.asarray(x_prompt, np.float32)
    x_sample = np.asarray(x_sample, np.float32)
    B, SEQ, _ = x_prompt.shape
    BS, DS, _ = x_sample.shape
    PAST = cache_sb_k.shape[2]
    NP, NS = B // n_cores, BS // n_cores
    nc = _get_nc((NP, SEQ, NS, DS, PAST))
    consts = make_consts()
    wts = {k: np.ascontiguousarray(np.asarray(w[k], np.float32)) for k in WEIGHT_KEYS}
    in_maps = []
    for i in range(n_cores):
        m = dict(wts)
        m["consts"] = consts
        m["xp"] = np.ascontiguousarray(x_prompt[i * NP:(i + 1) * NP])
        m["xs"] = np.ascontiguousarray(x_sample[i * NS:(i + 1) * NS]).reshape(NS * DS, D)
        m["ck"] = np.ascontiguousarray(np.asarray(cache_sb_k, np.float32)[0, i * NS:(i + 1) * NS])
        m["cv"] = np.ascontiguousarray(np.asarray(cache_sb_v, np.float32)[0, i * NS:(i + 1) * NS])
        m["sh"] = np.ascontiguousarray(np.asarray(state_hgrn, np.float32)[0, i * NS:(i + 1) * NS])
        in_maps.append(m)
    res = run_bass_kernel_spmd(nc, in_maps, core_ids=list(range(n_cores)))
    R = res.results
    cat = lambda k: np.concatenate([np.asarray(r[k]) for r in R], axis=0)
    y_prompt = cat("yp")
    y_sample = cat("ys").reshape(BS, DS, D)
    kp = cat("kp")[None]
    vp = cat("vp")[None]
    hp = cat("hp")[None]
    ks = cat("ks").reshape(BS, DS, H, DH)[None]
    vs = cat("vs").reshape(BS, DS, H, DH)[None]
    hs = cat("hs")[None]
    gv = cat("gv").reshape(BS, DS, DC)[None]
    return (y_prompt, y_sample, kp, vp, hp, ks, vs, hs, gv)
```

```python
import contextlib
import numpy as np
import concourse.bass as bass
import concourse.mybir as mybir
from concourse.bass_utils import run_bass_kernel_spmd

import os as _os
DBG = set(_os.environ.get("KDBG", "ffn1,norm,ab,c,ffn2,prompt,sample").split(","))
F32 = mybir.dt.float32
F32R = mybir.dt.float32r
BF16 = mybir.dt.bfloat16
AF = mybir.ActivationFunctionType
ALU = mybir.AluOpType

D = 1024
KC = 8
DFF = 2816
H = 8
DH = 128
DC = 2048
EPS = 1e-6
ENGS = ("pe", "act", "dve", "pool", "sp")
DMA_K = 8


class Buf:
    __slots__ = ("name", "w", "r", "excl")

    def __init__(self, name="", excl=False):
        self.name = name
        self.w = None
        self.r = {}
        self.excl = excl


class Prog:
    def __init__(self, nc):
        self.nc = nc
        self.st = contextlib.ExitStack()
        self.ops = {e: [] for e in ENGS}
        self.cnt = {e: 0 for e in ENGS}
        self.seen = {e: {} for e in ENGS}
        self.sems = {}
        for e in ENGS:
            self.sems[e] = self.st.enter_context(nc.semaphore("s_" + e))
        self.dma_i = {}
        for q in ("sp", "act", "pool"):
            self.dma_i[q] = 0
            for k in range(DMA_K):
                self.sems[("d", q, k)] = self.st.enter_context(nc.semaphore(f"d_{q}_{k}"))
        self.n_ins = 0
        self.n_wait = 0
        self.marks = []

    def mark(self, label):
        self.marks.append((label, dict(self.cnt)))

    def sb(self, name, shape, dt=F32, stack=None):
        return (stack or self.st).enter_context(self.nc.sbuf_tensor(name, list(shape), dt))

    def ps(self, name, shape, dt=F32):
        return self.st.enter_context(self.nc.psum_tensor(name, list(shape), dt))

    def _collect(self, eng, reads, writes, extra=()):
        waits = {}
        seen = self.seen[eng]

        def need(k, v):
            if k == eng and eng == "pe":
                return
            if seen.get(k, 0) < v and waits.get(k, 0) < v:
                waits[k] = v

        for b in reads:
            if b.w is not None:
                need(*b.w)
            if b.excl:
                for k, v in b.r.items():
                    if k != eng:
                        need(k, v)
        for b in writes:
            if b.w is not None:
                need(*b.w)
            for k, v in b.r.items():
                need(k, v)
        for k, v in extra:
            need(k, v)
        for k, v in waits.items():
            seen[k] = v
        return [(self.sems[k], v) for k, v in waits.items()]

    def op(self, eng, fn, reads=(), writes=()):
        waits = self._collect(eng, reads, writes)
        self.cnt[eng] += 1
        tok = (eng, self.cnt[eng])
        sem = self.sems[eng]
        self.n_wait += len(waits)
        self.n_ins += 1

        def run(e, waits=waits, fn=fn, sem=sem):
            for s, v in waits:
                e.wait_ge(s, v)
            fn(e).then_inc(sem, 1)

        self.ops[eng].append(run)
        for b in writes:
            b.w = tok
            b.r = {}
        for b in reads:
            b.r[eng] = tok[1]
        return tok

    def dma(self, q, out, in_, reads=(), writes=(), **kw):
        i = self.dma_i[q]
        self.dma_i[q] = i + 1
        slot, gen = i % DMA_K, i // DMA_K
        key = ("d", q, slot)
        extra = [(key, 16 * gen)] if gen > 0 else []
        waits = self._collect(q, reads, writes, extra)
        sem = self.sems[key]
        tok = (key, 16 * (gen + 1))
        self.n_wait += len(waits)
        self.n_ins += 1

        def run(e, waits=waits, sem=sem, out=out, in_=in_, kw=kw):
            for s, v in waits:
                e.wait_ge(s, v)
            e.dma_start(out=out, in_=in_, **kw).then_inc(sem, 16)

        self.ops[q].append(run)
        for b in writes:
            b.w = tok
            b.r = {}
        for b in reads:
            b.r[key] = tok[1]
        return tok

    def _all_tokens(self, dma_queues=("sp", "act", "pool")):
        toks = []
        for q in dma_queues:
            n = self.dma_i[q]
            for slot in range(min(n, DMA_K)):
                gens = (n - 1 - slot) // DMA_K + 1
                toks.append((("d", q, slot), 16 * gens))
        for e in ("pe", "act", "dve", "pool"):
            if self.cnt[e] > 0:
                toks.append((e, self.cnt[e]))
        return toks

    def barrier(self, engines=("pe", "act", "dve", "pool"), dma_queues=("act", "pool")):
        toks = self._all_tokens(dma_queues)
        for eng in engines:
            seen = self.seen[eng]
            waits = []
            for k, v in toks:
                if k == eng:
                    continue
                if seen.get(k, 0) < v:
                    seen[k] = v
                    waits.append((self.sems[k], v))

            def run(e, waits=waits):
                for s, v in waits:
                    e.wait_ge(s, v)

            self.ops[eng].append(run)

    def finish(self):
        fin = [(self.sems[k], v) for k, v in self._all_tokens()]

        def run_fin(e, fin=fin):
            for s, v in fin:
                e.wait_ge(s, v)

        self.ops["sp"].append(run_fin)
        nc, ops = self.nc, self.ops
        with nc.Block() as block:
            @block.tensor
            def _(e):
                for f in ops["pe"]:
                    f(e)

            @block.scalar
            def _(e):
                for f in ops["act"]:
                    f(e)

            @block.vector
            def _(e):
                for f in ops["dve"]:
                    f(e)

            @block.gpsimd
            def _(e):
                for f in ops["pool"]:
                    f(e)

            @block.sync
            def _(e):
                for f in ops["sp"]:
                    f(e)
        self.st.close()


class Slot:
    __slots__ = ("t", "b", "ring", "i")

    def __init__(self, t, b, ring, i):
        self.t, self.b, self.ring, self.i = t, b, ring, i

    def free(self):
        self.ring.freelist.append(self.i)


class Ring:
    def __init__(self, tensors, name, excl=False):
        self.name = name
        self.slots = [Slot(t, Buf(f"{name}{i}", excl), self, i) for i, t in enumerate(tensors)]
        self.freelist = list(range(len(tensors)))

    def alloc(self):
        assert self.freelist, f"ring {self.name} exhausted"
        return self.slots[self.freelist.pop(0)]

    def extend(self, tensors):
        for t in tensors:
            i = len(self.slots)
            self.slots.append(Slot(t, Buf(f"{self.name}{i}", self.slots[0].b.excl), self, i))
            self.freelist.append(i)

    def shrink(self, n):
        for _ in range(n):
            i = len(self.slots) - 1
            assert i in self.freelist, f"ring {self.name}: slot {i} still in use"
            self.freelist.remove(i)
            self.slots.pop()


C_ID, C_ONES, C_NTRI, C_NONES, C_HMP, C_HMS, C_RMS, C_SCS, C_GMP, C_SCP, C_AM, C_AMS = (
    0, 128, 256, 384, 512, 640, 768, 772, 900, 1028, 1540, 3588)
C_RES = 900
NCONST = 3588 + 128


def make_consts():
    c = np.zeros((128, NCONST), np.float32)
    i = np.arange(128)
    c[:, C_ID:C_ID + 128] = np.eye(128)
    c[:, C_ONES:C_ONES + 128] = 1.0
    c[:, C_NTRI:C_NTRI + 128] = -1.0 * (i[:, None] >= i[None, :])
    c[:, C_NONES:C_NONES + 128] = -1.0
    c[:, C_HMP:C_HMP + 128] = (i[:, None] // 64 == i[None, :] // 64) & (i[:, None] <= i[None, :])
    c[:, C_HMS:C_HMS + 128] = (i[:, None] // 32 == i[None, :] // 32) & (i[:, None] <= i[None, :])
    c[:, C_GMP:C_GMP + 128] = (i[None, :] // 64) <= (i[:, None] // 64)
    c[:, C_RMS:C_RMS + 4] = (i[:, None] // 32 == np.arange(4)[None, :])
    t = np.arange(512)
    c[:, C_SCP:C_SCP + 512] = (t % 64 != 0)[None, :]
    c[:, C_SCS:C_SCS + 128] = (np.arange(128) % 32 != 0)[None, :]
    for kd in range(4):
        c[:, C_AM + kd * 512:C_AM + (kd + 1) * 512] = (t[None, :] - 128 * kd) > i[:, None]
    for b in range(4):
        c[:, C_AMS + b * 32:C_AMS + (b + 1) * 32] = (i[:, None] // 32 == b) & ((i[:, None] % 32) < np.arange(32)[None, :])
    return c


class Builder:
    def __init__(self, NP, SEQ, NS, DS, PAST, n_wbf=6):
        self.NP, self.SEQ, self.NS, self.DS, self.PAST = NP, SEQ, NS, DS, PAST
        assert NS * DS == 128 and DS == 32
        nc = self.nc = bass.Bass("TRN2", target_bir_lowering=False)
        P = self.P = Prog(nc)

        def din(name, shape):
            return nc.dram_tensor(name, list(shape), F32, kind="ExternalInput").ap()

        def dout(name, shape):
            return nc.dram_tensor(name, list(shape), F32, kind="ExternalOutput").ap()

        self.i = dict(
            xp=din("xp", [NP, SEQ, D]), xs=din("xs", [128, D]),
            ck=din("ck", [NS, PAST, H, DH]), cv=din("cv", [NS, PAST, H, DH]), sh=din("sh", [NS, H, 128, 128]),
            consts=din("consts", [128, NCONST]),
            ffn1_norm=din("ffn1_norm", [2, D]), mix_norm=din("mix_norm", [2, D]), ffn2_norm=din("ffn2_norm", [2, D]),
            ffn1_w_gate=din("ffn1_w_gate", [2, D, DFF]), ffn1_w_up=din("ffn1_w_up", [2, D, DFF]),
            ffn1_w_down=din("ffn1_w_down", [2, DFF, D]),
            ffn2_w_gate=din("ffn2_w_gate", [2, D, DFF]), ffn2_w_up=din("ffn2_w_up", [2, D, DFF]),
            ffn2_w_down=din("ffn2_w_down", [2, DFF, D]),
            ab_w_in=din("ab_w_in", [1, D, 7 * D]), ab_lb=din("ab_lb", [2, D]), ab_g_out=din("ab_g_out", [1, H, 128]),
            ab_g_q=din("ab_g_q", [1, 128]), ab_g_k=din("ab_g_k", [1, 128]), ab_w_out=din("ab_w_out", [1, 2 * D, D]),
            c_w_in=din("c_w_in", [1, D, 2 * DC]), c_ln_g=din("c_ln_g", [1, DC]), c_ln_b=din("c_ln_b", [1, DC]),
            c_w_s=din("c_w_s", [1, 8, 128, 128]), c_b_s=din("c_b_s", [1, 8, 128]), c_w_out=din("c_w_out", [1, DC, D]),
        )
        self.o = dict(
            yp=dout("yp", [NP, SEQ, D]), ys=dout("ys", [128, D]),
            kp=dout("kp", [NP, SEQ, H, DH]), vp=dout("vp", [NP, SEQ, H, DH]), hp=dout("hp", [NP, H, 128, 128]),
            ks=dout("ks", [128, H, DH]), vs=dout("vs", [128, H, DH]), hs=dout("hs", [NS, H, 128, 128]),
            gv=dout("gv", [128, DC]),
        )
        self.banks = Ring([P.ps(f"bk{i}", [128, 512]) for i in range(8)], "bk", excl=True)
        self.wst = Ring([P.sb(f"wst{i}", [128, 2048]) for i in range(2)], "wst")
        self.wst_parts = [[Buf() for _ in range(4)] for _ in range(2)]
        self.wbf = Ring([P.sb(f"wbf{i}", [128, 2048], BF16) for i in range(n_wbf)], "wbf")
        self.F = Ring([P.sb(f"F{i}", [128, 512]) for i in range(6)], "F")
        self.Hh = Ring([P.sb(f"H{i}", [128, 512], BF16) for i in range(12)], "H")
        self.cA = P.sb("cA", [128, C_RES])
        self.scanb = P.sb("scanb", [128, 512], BF16)
        self.ntri = P.sb("ntri", [128, 256], F32R)
        self.FR = Ring([P.sb(f"FR{i}", [128, 512], F32R) for i in range(3)], "FR")
        self.amask = P.sb("amask", [128, 4, 512], BF16)
        self.amask_s = P.sb("amask_s", [128, 4, 32], BF16)
        self.identb = P.sb("identb", [128, 128], BF16)
        self.onesb = P.sb("onesb", [128, 128], BF16)
        self.prm = P.sb("prm", [128, 104])
        self.lbp = P.sb("lbp", [128, 3, 8])
        self.gqk = P.sb("gqk", [128, 2, 128])
        self.cb = Buf("consts")
        self.setup_consts()

    def mm(self, out, lhsT, rhs, start, stop, rd, wr, **kw):
        self.P.op("pe", lambda e: e.matmul(out, lhsT=lhsT, rhs=rhs, start=start, stop=stop, **kw), reads=rd, writes=wr)

    def tp(self, out, in_, bf, rd, wr):
        ident = self.identb[:] if bf else self.cA[:, C_ID:C_ID + 128]
        pin = in_.partition_size()
        ident = ident[0:pin, 0:pin]
        self.P.op("pe", lambda e: e.transpose(out=out, in_=in_, identity=ident), reads=list(rd) + [self.cb], writes=wr)

    def act(self, out, in_, func, rd, wr, **kw):
        self.P.op("act", lambda e: e.activation(out=out, in_=in_, func=func, **kw), reads=rd, writes=wr)

    def tt(self, eng, out, in0, in1, op, rd, wr):
        self.P.op(eng, lambda e: e.tensor_tensor(out=out, in0=in0, in1=in1, op=op), reads=rd, writes=wr)

    def ts(self, eng, out, in0, s1, s2, op0, op1, rd, wr):
        self.P.op(eng, lambda e: e.tensor_scalar(out=out, in0=in0, scalar1=s1, scalar2=s2, op0=op0, op1=op1), reads=rd, writes=wr)

    def stt(self, eng, out, in0, scalar, in1, op0, op1, rd, wr):
        self.P.op(eng, lambda e: e.scalar_tensor_tensor(out=out, in0=in0, scalar=scalar, in1=in1, op0=op0, op1=op1),
                  reads=rd, writes=wr)

    def cp(self, eng, out, in_, rd, wr):
        if eng == "act":
            self.P.op(eng, lambda e: e.activation(out=out, in_=in_, func=AF.Copy), reads=rd, writes=wr)
        else:
            self.P.op(eng, lambda e: e.tensor_copy(out=out, in_=in_), reads=rd, writes=wr)

    def ts1(self, eng, out, in_, scalar, op, rd, wr):
        self.P.op(eng, lambda e: e.tensor_single_scalar(out=out, in_=in_, scalar=scalar, op=op), reads=rd, writes=wr)

    def const(self, off, n=128, rows=128):
        return self.cA[0:rows, off:off + n]

    def wload(self, srcs, total, cast_eng="act"):
        if cast_eng == "alt":
            self.cast_i = getattr(self, "cast_i", 0) + 1
            cast_eng = "dve" if (self.cast_i % 2 == 1 and "altdve" in DBG) else "act"
        st = self.wst.alloc()
        parts = self.wst_parts[st.i]
        for pi, (off, n, shape_fn, src) in enumerate(srcs):
            dst = st.t[:, off:off + n]
            if shape_fn is not None:
                dst = shape_fn(dst)
            self.P.dma("sp", dst, src, writes=[parts[pi]])
        wb = self.wbf.alloc()
        self.cp(cast_eng, wb.t[:, 0:total], st.t[:, 0:total], rd=parts, wr=[wb.b])
        st.free()
        return wb

    def setup_consts(self):
        P, I = self.P, self.i
        cb = self.cb
        P.dma("sp", self.cA[:], I["consts"][:, 0:C_RES], writes=[cb])
        st = self.wst.alloc()
        tb = Buf()
        P.dma("sp", st.t[:, 0:2048], I["consts"][:, C_AM:C_AM + 2048], writes=[tb])
        self.ts("dve", self.amask[:].rearrange("p a b -> p (a b)"), st.t[:, 0:2048], 30000.0, -30000.0, ALU.mult, ALU.add,
                rd=[tb], wr=[cb])
        st2 = self.wst.alloc()
        tb2 = Buf()
        P.dma("sp", st2.t[:, 0:128], I["consts"][:, C_AMS:C_AMS + 128], writes=[tb2])
        P.dma("sp", st2.t[:, 128:640], I["consts"][:, C_SCP:C_SCP + 512], writes=[tb2])
        self.cp("dve", self.scanb[:], st2.t[:, 128:640], rd=[tb2], wr=[cb])
        self.ts("dve", self.amask_s[:].rearrange("p a b -> p (a b)"), st2.t[:, 0:128], 30000.0, -30000.0, ALU.mult, ALU.add,
                rd=[tb2], wr=[cb])
        self.cp("dve", self.ntri[:], self.cA[:, C_NTRI:C_NTRI + 256], rd=[cb], wr=[cb])
        self.cp("dve", self.identb[:], self.cA[:, C_ID:C_ID + 128], rd=[cb], wr=[cb])
        self.cp("dve", self.onesb[:], self.cA[:, C_ONES:C_ONES + 128], rd=[cb], wr=[cb])
        tmpstk = contextlib.ExitStack()
        if "nos2" in DBG:
            st.free(); st2.free(); return
        rowt = P.sb("prmrows", [128, 128], stack=tmpstk)
        rows = rowt[0:104, 0:128]
        tb3 = Buf()

        def ld(r0, n, src):
            P.dma("sp", rowt[r0:r0 + n, 0:128], src, writes=[tb3])

        ld(0, 16, I["ffn1_norm"].rearrange("l (c k) -> (l c) k", k=128))
        ld(16, 16, I["mix_norm"].rearrange("l (c k) -> (l c) k", k=128))
        ld(32, 16, I["ffn2_norm"].rearrange("l (c k) -> (l c) k", k=128))
        ld(48, 16, I["ab_lb"].rearrange("l (c k) -> (l c) k", k=128))
        ld(64, 8, I["ab_g_out"][0])
        ld(72, 16, I["c_ln_g"].rearrange("l (c k) -> (l c) k", k=128))
        ld(88, 16, I["c_ln_b"].rearrange("l (c k) -> (l c) k", k=128))
        bk = self.banks.alloc()
        self.tp(bk.t[:, 0:104], rows, False, rd=[tb3], wr=[bk.b])
        self.cp("dve", self.prm[:], bk.t[:, 0:104], rd=[bk.b], wr=[cb])
        bk.free()
        self.tt("dve", self.lbp[:, 0, :], self.prm[:, 48:56], self.prm[:, 56:64], ALU.subtract, rd=[cb], wr=[cb])
        self.act(self.lbp[:, 0, :], self.lbp[:, 0, :], AF.Sigmoid, rd=[cb], wr=[cb])
        self.ts("dve", self.lbp[:, 1, :], self.lbp[:, 0, :], -1.0, 1.0, ALU.mult, ALU.add, rd=[cb], wr=[cb])
        self.ts("dve", self.lbp[:, 2, :], self.lbp[:, 0, :], 1.0, -1.0, ALU.mult, ALU.add, rd=[cb], wr=[cb])
        if "nos3" in DBG:
            st.free(); st2.free(); return
        P.dma("sp", self.gqk[:, 0, :], I["ab_g_q"][0:1, :].partition_broadcast(128), reads=[cb], writes=[cb])
        P.dma("sp", self.gqk[:, 1, :], I["ab_g_k"][0:1, :].partition_broadcast(128), reads=[cb], writes=[cb])
        self.ts1("dve", self.gqk[:, 0, :], self.gqk[:, 0, :], float(DH ** -0.5), ALU.mult, rd=[cb], wr=[cb])
        st.free()
        st2.free()
        P.barrier(engines=ENGS, dma_queues=("sp", "act", "pool"))
        tmpstk.close()

    def gmlp_consts(self, sample, wsT, BT, stack, tmps=None):
        P, I = self.P, self.i
        cb = self.cb
        sfx = "s" if sample else "p"
        if tmps is not None:
            wsf, wsTf, bsb = tmps
        else:
            wsf = P.sb("gc_wsf" + sfx, [128, 8, 128], stack=stack)
            wsTf = P.sb("gc_wsTf" + sfx, [128, 8, 128], stack=stack)
            bsb = P.sb("gc_bsb" + sfx, [128, 8, 128], stack=stack)
        b1, b2, b3 = Buf(), Buf(), Buf()
        if not sample:
            P.dma("sp", wsf[:], I["c_w_s"][0].rearrange("g i j -> i g j"), writes=[b1])
            gm = bsb[:, 0, :]
            P.dma("sp", gm, I["consts"][:, C_GMP:C_GMP + 128], writes=[b3])
            for g in range(8):
                self.tt("dve", wsf[:, g, :], wsf[:, g, :], gm, ALU.mult, rd=[b1, b3], wr=[b1, b3])
            P.dma("sp", bsb[:].rearrange("p g i -> p (g i)"),
                  I["c_b_s"][0:1].rearrange("o g i -> o (g i)").partition_broadcast(128), writes=[b3])
        else:
            P.op("pool", lambda e: e.memset(wsf[:], 0.0), writes=[b1])
            for b in range(4):
                P.dma("sp", wsf[b * 32:(b + 1) * 32, :, b * 32:(b + 1) * 32],
                      I["c_w_s"][0, :, 0:32, 0:32].rearrange("g i j -> i g j"), reads=[b1], writes=[b1])
                P.dma("sp", bsb[:, :, b * 32:(b + 1) * 32],
                      I["c_b_s"][0:1, :, 0:32].partition_broadcast(128), writes=[b3])
        if "g1" in DBG:
            return
        for g in range(8):
            bk = self.banks.alloc()
            self.tp(bk.t[:, 0:128], wsf[:, g, :], False, rd=[b1], wr=[bk.b])
            self.cp("dve", wsTf[:, g, :], bk.t[:, 0:128], rd=[bk.b], wr=[b2])
            self.cp("act", wsT[:, g, :], bk.t[:, 0:128], rd=[bk.b], wr=[cb])
            bk.free()
        if "g2" in DBG:
            return
        for g in range(8):
            bk = self.banks.alloc()
            self.mm(bk.t[:, 0:128], self.const(C_ONES), wsTf[:, g, :], True, True, rd=[b2, cb], wr=[bk.b])
            for cc in (2 * g, 2 * g + 1):
                self.stt("dve", BT[:, cc, :], bk.t[:, 0:128], self.prm[:, 88 + cc:89 + cc], bsb[:, g, :],
                         ALU.mult, ALU.add, rd=[bk.b, b3, cb], wr=[cb])
            bk.free()

    def load_x(self, ctx, src, xin_ring):
        P = self.P
        T = ctx["T"]
        for ti in range(T // 128):
            g = (ti * 128) // ctx["GN"]
            xin = xin_ring.alloc()
            P.dma("sp", xin.t[:, :], src[ti * 128:(ti + 1) * 128, :], writes=[xin.b])
            for half in range(2):
                bk = self.banks.alloc()
                for j in range(4):
                    c = half * 4 + j
                    self.tp(bk.t[:, j * 128:(j + 1) * 128], xin.t[:, c * 128:(c + 1) * 128], False, rd=[xin.b], wr=[bk.b])
                self.cp("act" if half == 0 else "dve",
                        ctx["xT"][:, half * 4:half * 4 + 4, ti * 128:(ti + 1) * 128],
                        bk.t[:, 0:512].rearrange("p (c t) -> p c t", c=4),
                        rd=[bk.b], wr=[ctx["xb"][g][half * 4 + j] for j in range(4)])
                bk.free()
            xin.free()

    def store_y(self, ctx, dst, ybuf_ring):
        P = self.P
        T = ctx["T"]
        for ti in range(T // 128):
            g = (ti * 128) // ctx["GN"]
            yb = ybuf_ring.alloc()
            for half in range(2):
                bk = self.banks.alloc()
                for j in range(4):
                    c = half * 4 + j
                    self.tp(bk.t[:, j * 128:(j + 1) * 128], ctx["xT"][:, c, ti * 128:(ti + 1) * 128], False,
                            rd=[ctx["xb"][g][c]], wr=[bk.b])
                self.cp("act" if half == 0 else "dve", yb.t[:, half * 512:(half + 1) * 512], bk.t[:, 0:512],
                        rd=[bk.b], wr=[yb.b])
                bk.free()
            P.dma("act", dst[ti * 128:(ti + 1) * 128, :], yb.t[:, :], reads=[yb.b])
            yb.free()

    def rstd_bc(self, src_bank, n, dim, explog=False):
        f = self.F.alloc()
        if explog:
            self.act(f.t[:, 0:n], src_bank.t[:, 0:n], AF.Ln, rd=[src_bank.b], wr=[f.b], scale=1.0 / dim, bias=EPS)
            self.act(f.t[:, 0:n], f.t[:, 0:n], AF.Exp, rd=[f.b], wr=[f.b], scale=-0.5)
        else:
            self.act(f.t[:, 0:n], src_bank.t[:, 0:n], AF.Sqrt, rd=[src_bank.b], wr=[f.b], scale=1.0 / dim, bias=EPS)
            self.P.op("dve", lambda e: e.reciprocal(out=f.t[:, 0:n], in_=f.t[:, 0:n]), reads=[f.b], writes=[f.b])
        return f

    def rmsnorm(self, ctx, gcol0):
        xT, hT = ctx["xT"], ctx["hT"]
        for g, (t0, n) in enumerate(ctx["groups"]):
            bk = self.banks.alloc()
            for c in range(KC):
                sq = self.Hh.alloc()
                self.act(sq.t[:, 0:n], xT[:, c, t0:t0 + n], AF.Square, rd=[ctx["xb"][g][c]], wr=[sq.b])
                self.mm(bk.t[:, 0:n], self.onesb[:], sq.t[:, 0:n], c == 0, c == KC - 1, rd=[sq.b, self.cb], wr=[bk.b])
                sq.free()
            rs = self.rstd_bc(bk, n, D)
            bk.free()
            for c in range(KC):
                self.stt("dve", hT[:, c, t0:t0 + n], xT[:, c, t0:t0 + n],
                         self.prm[:, gcol0 + c:gcol0 + c + 1], rs.t[:, 0:n], ALU.mult, ALU.mult,
                         rd=[ctx["xb"][g][c], rs.b, self.cb], wr=[ctx["hb"][g]])
            rs.free()

    def ffn(self, ctx, Wg, Wu, Wd):
        xT, hT = ctx["xT"], ctx["hT"]
        pending = None

        def down(wd, a, g, t0, n, free_wd):
            for d in range(KC):
                bk = self.banks.alloc()
                for j in range(2):
                    self.mm(bk.t[:, 0:n], wd.t[:, j * 1024 + d * 128: j * 1024 + (d + 1) * 128], a[j].t[:, 0:n],
                            j == 0, j == 1, rd=[wd.b, a[j].b], wr=[bk.b])
                self.stt("dve", xT[:, d, t0:t0 + n], bk.t[:, 0:n], 0.5, xT[:, d, t0:t0 + n], ALU.mult, ALU.add,
                         rd=[bk.b, ctx["xb"][g][d]], wr=[ctx["xb"][g][d]])
                bk.free()
            for j in range(2):
                a[j].free()
            if free_wd:
                wd.free()

        npieces = DFF // 256
        view = lambda ap: ap.rearrange("p (c f) -> p c f", c=KC)
        viewd = lambda ap: ap.rearrange("p (j d) -> p j d", j=2)

        def load_piece(p):
            f0 = p * 256
            wg_ = self.wload([(0, 2048, view, Wg.rearrange("(c k) f -> k c f", k=128)[:, :, f0:f0 + 256])], 2048, "act")
            wu_ = self.wload([(0, 2048, view, Wu.rearrange("(c k) f -> k c f", k=128)[:, :, f0:f0 + 256])], 2048, "act")
            wd_ = self.wload([(0, 2048, viewd, Wd[f0:f0 + 256, :].rearrange("(j f) d -> f j d", f=128))], 2048, "act")
            return wg_, wu_, wd_

        NGf = len(ctx["groups"])
        pre_g = 1 if NGf >= 2 else 0
        cur_w = load_piece(0)
        for p in range(npieces):
            wg, wu, wd = cur_w
            nxt_w = None
            for g, (t0, n) in enumerate(ctx["groups"]):
                a = []
                for j in range(2):
                    bg = self.banks.alloc()
                    bu = self.banks.alloc()
                    for c in range(KC):
                        self.mm(bg.t[:, 0:n], wg.t[:, c * 256 + j * 128:c * 256 + (j + 1) * 128], hT[:, c, t0:t0 + n],
                                c == 0, c == KC - 1, rd=[wg.b, ctx["hb"][g]], wr=[bg.b])
                    for c in range(KC):
                        self.mm(bu.t[:, 0:n], wu.t[:, c * 256 + j * 128:c * 256 + (j + 1) * 128], hT[:, c, t0:t0 + n],
                                c == 0, c == KC - 1, rd=[wu.b, ctx["hb"][g]], wr=[bu.b])
                    sg = self.F.alloc()
                    self.act(sg.t[:, 0:n], bg.t[:, 0:n], AF.Silu, rd=[bg.b], wr=[sg.b])
                    bg.free()
                    aj = self.Hh.alloc()
                    self.tt("dve", aj.t[:, 0:n], sg.t[:, 0:n], bu.t[:, 0:n], ALU.mult, rd=[sg.b, bu.b], wr=[aj.b])
                    sg.free()
                    bu.free()
                    a.append(aj)
                if pending is not None:
                    down(*pending)
                last_g = g == len(ctx["groups"]) - 1
                pending = (wd, a, g, t0, n, last_g)
                if g == pre_g and p + 1 < npieces:
                    nxt_w = load_piece(p + 1)
            wg.free()
            wu.free()
            cur_w = nxt_w
        down(*pending)

    def mixer_c(self, ctx, sample, wsT, BT, ar):
        P, I = self.P, self.i
        xT, hT = ctx["xT"], ctx["hT"]
        Win, Wout = I["c_w_in"][0], I["c_w_out"][0]
        view = lambda ap: ap.rearrange("p (c f) -> p c f", c=KC)
        viewd = lambda ap: ap.rearrange("p (j d) -> p j d", j=2)
        vn, vnb, stats, stb = ar["vn"], ar["vnb"], ar["stats"], ar["stb"]
        for g, (t0, n) in enumerate(ctx["groups"]):
            TG = n // 128
            P.op("pool", lambda e: e.memset(stats[:], 0.0), writes=[stb] + list(vnb))
            load_v = lambda u: self.wload(
                [(0, 2048, view, Win.rearrange("(c k) f -> k c f", k=128)[:, :, DC + u * 256:DC + (u + 1) * 256])], 2048, "act")
            wv_next = load_v(0)
            for u in range(8):
                wv = wv_next
                wv_next = load_v(u + 1) if u + 1 < 8 else None
                for t in range(TG):
                    bk = self.banks.alloc()
                    for c in range(KC):
                        self.mm(bk.t[:, 0:256], hT[:, c, t0 + t * 128:t0 + (t + 1) * 128], wv.t[:, c * 256:(c + 1) * 256],
                                c == 0, c == KC - 1, rd=[wv.b, ctx["hb"][g]], wr=[bk.b])
                    self.act(vn[:, t, u * 256:(u + 1) * 256], bk.t[:, 0:256], AF.Gelu_apprx_tanh, rd=[bk.b], wr=[vnb[t]],
                             accum_out=stats[:, t, 0, u:u + 1])
                    bk.free()
                    junk = self.F.alloc()
                    self.act(junk.t[:, 0:256], vn[:, t, u * 256:(u + 1) * 256], AF.Square, rd=[vnb[t]], wr=[junk.b, stb],
                             accum_out=stats[:, t, 1, u:u + 1])
                    junk.free()
                wv.free()
            mv = ar["mv"]
            for t in range(TG):
                P.op("dve", lambda e, t=t: e.reduce_sum(out=mv[:, t, 0:2], in_=stats[:, t, :, :], axis=mybir.AxisListType.X),
                     reads=[stb, vnb[t]], writes=[stb])
                self.ts1("dve", mv[:, t, 0:2], mv[:, t, 0:2], 1.0 / DC, ALU.mult, rd=[stb], wr=[stb])
                self.tt("dve", mv[:, t, 2:3], mv[:, t, 0:1], mv[:, t, 0:1], ALU.mult, rd=[stb], wr=[stb])
                self.tt("dve", mv[:, t, 2:3], mv[:, t, 1:2], mv[:, t, 2:3], ALU.subtract, rd=[stb], wr=[stb])
                self.act(mv[:, t, 2:3], mv[:, t, 2:3], AF.Sqrt, rd=[stb], wr=[stb], bias=EPS)
                P.op("dve", lambda e, t=t: e.reciprocal(out=mv[:, t, 3:4], in_=mv[:, t, 2:3]), reads=[stb], writes=[stb])
                self.ts("dve", vn[:, t, :], vn[:, t, :], mv[:, t, 0:1], mv[:, t, 3:4], ALU.subtract, ALU.mult,
                        rd=[stb, vnb[t]], wr=[vnb[t]])
                if sample:
                    vo = ar["vout"]
                    self.tt("dve", vo[:, :], vn[:, t, :], ar["lng"][:, :], ALU.mult, rd=[vnb[t], self.cb], wr=[ar["voutb"]])
                    self.tt("pool", vo[:, :], vo[:, :], ar["lnb"][:, :], ALU.add, rd=[self.cb], wr=[ar["voutb"]])
                    P.dma("act", self.o["gv"][:, :], vo[:, :], reads=[ar["voutb"]])
                    self.cp("pool", ar["vhb"][:, t, :], vn[:, t, :], rd=[vnb[t]], wr=[ar["vhbb"]])
            vh = ar["vhb"] if sample else vn
            vhrd = [ar["vhbb"]] if sample else [vnb[t] for t in range(TG)]
            pending = None

            def down(wo, a, t0=t0, n=n, g=g):
                for d in range(KC):
                    bk = self.banks.alloc()
                    for j in range(2):
                        self.mm(bk.t[:, 0:n], wo.t[:, j * 1024 + d * 128:j * 1024 + (d + 1) * 128], a[j].t[:, 0:n],
                                j == 0, j == 1, rd=[wo.b, a[j].b], wr=[bk.b])
                    self.stt("dve", xT[:, d, t0:t0 + n], bk.t[:, 0:n], 1.0, xT[:, d, t0:t0 + n], ALU.mult, ALU.add,
                             rd=[bk.b, ctx["xb"][g][d]], wr=[ctx["xb"][g][d]])
                    bk.free()
                for j in range(2):
                    a[j].free()
                wo.free()

            def load_p(p):
                wu_ = self.wload([(0, 2048, view, Win.rearrange("(c k) f -> k c f", k=128)[:, :, p * 256:(p + 1) * 256])], 2048, "act")
                wo_ = self.wload([(0, 2048, viewd, Wout[p * 256:(p + 1) * 256, :].rearrange("(j f) d -> f j d", f=128))], 2048, "act")
                return wu_, wo_

            nxt_p = load_p(0)
            for p in range(8):
                wu, wo = nxt_p
                nxt_p = load_p(p + 1) if p + 1 < 8 else None
                a = []
                for j in range(2):
                    cc = 2 * p + j
                    bu = self.banks.alloc()
                    for c in range(KC):
                        self.mm(bu.t[:, 0:n], wu.t[:, c * 256 + j * 128:c * 256 + (j + 1) * 128], hT[:, c, t0:t0 + n],
                                c == 0, c == KC - 1, rd=[wu.b, ctx["hb"][g]], wr=[bu.b])
                    ug = self.F.alloc()
                    self.act(ug.t[:, 0:n], bu.t[:, 0:n], AF.Gelu_apprx_tanh, rd=[bu.b], wr=[ug.b])
                    bu.free()
                    bm = self.banks.alloc()
                    for t in range(TG):
                        self.mm(bm.t[:, t * 128:(t + 1) * 128], vh[:, t, cc * 128:(cc + 1) * 128], wsT[:, p, :],
                                True, True, rd=vhrd + [self.cb], wr=[bm.b])
                    t1 = self.F.alloc()
                    for t in range(TG):
                        self.stt("dve", t1.t[:, t * 128:(t + 1) * 128], bm.t[:, t * 128:(t + 1) * 128],
                                 self.prm[:, 72 + cc:73 + cc], BT[:, cc, :], ALU.mult, ALU.add,
                                 rd=[bm.b, self.cb], wr=[t1.b])
                    bm.free()
                    aj = self.Hh.alloc()
                    self.tt("dve", aj.t[:, 0:n], t1.t[:, 0:n], ug.t[:, 0:n], ALU.mult, rd=[t1.b, ug.b], wr=[aj.b])
                    t1.free()
                    ug.free()
                    a.append(aj)
                wu.free()
                if pending is not None:
                    down(*pending)
                pending = (wo, a)
            down(*pending)

    def attn_pairs(self, pairs, R, O, filler=None):
        cb = self.cb
        K = len(pairs)
        st = [dict() for _ in pairs]

        def scores(bank, p, last_stop):
            zs, QTv, qb_, avs, S, N, mask, c0, cn = p
            nm = len(zs)
            for j, (a_, b_, KTv, kb_) in enumerate(zs):
                self.mm(bank.t[0:S, a_:b_], KTv, QTv[:, a_:b_], j == 0, last_stop and mask is None and j == nm - 1,
                        rd=[kb_, qb_], wr=[bank.b], skip_group_check=True)
            if mask is not None:
                self.mm(bank.t[0:S, c0:cn], self.identb[0:S, 0:S], mask, False, last_stop, rd=[cb], wr=[bank.b],
                        skip_group_check=True)

        def A(p, s):
            zs, QTv, qb_, avs, S, N, mask, c0, cn = p
            z = self.banks.alloc()
            scores(z, p, True)
            X = self.F.alloc()
            self.act(X.t[0:S, c0:N], z.t[0:S, c0:N], AF.Exp, rd=[z.b], wr=[X.b])
            z.free()
            E = self.FR.alloc()
            self.act(E.t[0:S, c0:N], X.t[0:S, c0:N], AF.Ln, rd=[X.b], wr=[E.b], bias=1.0)
            X.free()
            s["E"] = E

        def B(p, s, first, last):
            zs, QTv, qb_, avs, S, N, mask, c0, cn = p
            E = s["E"]
            L = self.banks.alloc()
            scores(L, p, False)
            has_old = (not first) and cn < N
            self.mm(L.t[0:S, c0:N], self.ntri[0:S, 0:S], E.t[0:S, c0:N], False, not has_old, rd=[E.b, cb], wr=[L.b],
                    skip_group_check=True)
            if has_old:
                self.mm(L.t[0:S, cn:N], self.ntri[:, 128:128 + S], R.t[:, cn:N], False, True, rd=[R.b, cb], wr=[L.b],
                        skip_group_check=True)
            W = self.Hh.alloc()
            self.act(W.t[0:S, c0:N], L.t[0:S, c0:N], AF.Exp, rd=[L.b], wr=[W.b])
            L.free()
            s["W"] = W
            if not last:
                if cn > c0:
                    self.cp("dve", R.t[0:S, c0:cn], E.t[0:S, c0:cn], rd=[E.b], wr=[R.b])
                if has_old:
                    self.tt("dve", R.t[0:S, cn:N], R.t[0:S, cn:N], E.t[0:S, cn:N], ALU.add, rd=[E.b, R.b], wr=[R.b])
            E.free()

        def C(p, s, first, last):
            zs, QTv, qb_, avs, S, N, mask, c0, cn = p
            W = s["W"]
            na = len(avs)
            for j, (a_, b_, Vv, vb_) in enumerate(avs):
                self.mm(O.t[:, a_:b_], Vv, W.t[0:S, a_:b_], first and j == 0, last and j == na - 1,
                        rd=[vb_, W.b], wr=[O.b], skip_group_check=True)
            W.free()

        for i in range(K + 2):
            if i < K:
                A(pairs[i], st[i])
            if 0 <= i - 1 < K:
                B(pairs[i - 1], st[i - 1], i - 1 == 0, i - 1 == K - 1)
            if 0 <= i - 2 < K:
                C(pairs[i - 2], st[i - 2], i - 2 == 0, i - 2 == K - 1)
            if filler is not None:
                next(filler, None)

    def mixer_ab(self, ctx, sample, ar, seq_out):
        P, I, O_ = self.P, self.i, self.o
        cb = self.cb
        xT, hT = ctx["xT"], ctx["hT"]
        Win, Wout = I["ab_w_in"][0], I["ab_w_out"][0]
        Wv = Win.rearrange("(c k) (t h n) -> k c t h n", k=128, t=7, h=H)
        CH = 32 if sample else 64
        cpt = 128 // CH
        hmask = self.const(C_HMS if sample else C_HMP)
        scanm = self.const(C_SCS, 128) if sample else self.scanb[:, :]
        KT, Vb, kvb = ar["KT"], ar["Vb"], ar["kvb"]
        groups = ctx["groups"]
        NGp = len(groups)
        iters = [(h, g) for h in range(H) for g in range(NGp)]
        NI = len(iters)
        st = [dict() for _ in iters]
        wts = {}
        v2 = lambda ap: ap.rearrange("p (c t n) -> p c t n", c=KC, t=2)[:, :, 0, :]
        v2b = lambda ap: ap.rearrange("p (c t n) -> p c t n", c=KC, t=2)[:, :, 1, :]
        v1 = lambda ap: ap.rearrange("p (c n) -> p c n", c=KC)

        def load_head(h):
            blk = lambda t: Wv[:, :, t, h, :]
            w1 = self.wload([(0, 2048, v2, blk(0)), (0, 2048, v2b, blk(1))], 2048)
            w2 = self.wload([(0, 1024, v1, blk(3))], 1024)
            w3 = self.wload([(0, 2048, v2, blk(2)), (0, 2048, v2b, blk(4))], 2048)
            w4 = self.wload([(0, 2048, v2, blk(5)), (0, 2048, v2b, blk(6))], 2048)
            wo = self.wload([(0, 1024, None, Wout[h * 128:(h + 1) * 128, :]),
                             (1024, 1024, None, Wout[D + h * 128:D + (h + 1) * 128, :])], 2048)
            wts[h] = (w1, w2, w3, w4, wo)

        def P1a(n):
            h, g = iters[n]
            s = st[n]
            t0, nn = groups[g]
            TG = nn // 128
            par = n % 2
            if h not in wts:
                load_head(h)
            w1, w2, w3, w4, wo = wts[h]
            hb = ctx["hb"][g]
            bq, bf_, bg_ = self.banks.alloc(), self.banks.alloc(), self.banks.alloc()
            for c in range(KC):
                self.mm(bq.t[:, 0:nn], w1.t[:, c * 256:c * 256 + 128], hT[:, c, t0:t0 + nn], c == 0, c == KC - 1,
                        rd=[w1.b, hb], wr=[bq.b])
            yield
            for c in range(KC):
                self.mm(bf_.t[:, 0:nn], w1.t[:, c * 256 + 128:c * 256 + 256], hT[:, c, t0:t0 + nn], c == 0, c == KC - 1,
                        rd=[w1.b, hb], wr=[bf_.b])
            for c in range(KC):
                self.mm(bg_.t[:, 0:nn], w2.t[:, c * 128:(c + 1) * 128], hT[:, c, t0:t0 + nn], c == 0, c == KC - 1,
                        rd=[w2.b, hb], wr=[bg_.b])
            yield
            T1, T2, T3, T4, T5 = [self.F.alloc() for _ in range(5)]
            def sigm(T):
                self.ts1("dve", T.t[:, 0:nn], T.t[:, 0:nn], 1.0, ALU.add, rd=[T.b], wr=[T.b])
                P.op("dve", lambda e, T=T: e.reciprocal(out=T.t[:, 0:nn], in_=T.t[:, 0:nn]), reads=[T.b], writes=[T.b])

            self.act(T1.t[:, 0:nn], bf_.t[:, 0:nn], AF.Exp, rd=[bf_.b], wr=[T1.b], scale=-1.0)
            bf_.free()
            self.act(T5.t[:, 0:nn], bg_.t[:, 0:nn], AF.Exp, rd=[bg_.b], wr=[T5.b], scale=-1.0)
            bg_.free()
            sigm(T1)
            self.act(T2.t[:, 0:nn], T1.t[:, 0:nn], AF.Ln, rd=[T1.b, cb], wr=[T2.b],
                     scale=self.lbp[:, 1, h:h + 1], bias=self.lbp[:, 0, h:h + 1])
            P.op("dve", lambda e, T3=T3, T2=T2, nn=nn: e.tensor_tensor_scan(
                out=T3.t[:, 0:nn], data0=scanm[:, 0:nn], data1=T2.t[:, 0:nn], initial=0.0, op0=ALU.mult, op1=ALU.add),
                reads=[T2.b, cb], writes=[T3.b])
            self.act(T4.t[:, 0:nn], T3.t[:, 0:nn], AF.Exp, rd=[T3.b], wr=[T4.b])
            self.act(T2.t[:, 0:nn], T3.t[:, 0:nn], AF.Exp, rd=[T3.b, T2.b], wr=[T2.b], scale=-1.0)
            self.act(T3.t[:, 0:nn], bq.t[:, 0:nn], AF.Exp, rd=[bq.b, T3.b], wr=[T3.b], scale=-1.0)
            sigm(T5)
            self.ts("dve", T1.t[:, 0:nn], T1.t[:, 0:nn], self.lbp[:, 2, h:h + 1], self.lbp[:, 1, h:h + 1],
                    ALU.mult, ALU.add, rd=[T1.b, cb], wr=[T1.b])
            kt = self.Hh.alloc()
            self.tt("dve", kt.t[:, 0:nn], T1.t[:, 0:nn], T2.t[:, 0:nn], ALU.mult, rd=[T1.b, T2.b], wr=[kt.b])
            sigm(T3)
            self.tt("dve", T3.t[:, 0:nn], T3.t[:, 0:nn], bq.t[:, 0:nn], ALU.mult, rd=[T3.b, bq.b], wr=[T3.b])
            bq.free()
            qt = self.Hh.alloc()
            self.tt("dve", qt.t[:, 0:nn], T3.t[:, 0:nn], T4.t[:, 0:nn], ALU.mult, rd=[T3.b, T4.b], wr=[qt.b])
            nch = nn // CH
            ecl = ar["ecl"][:, par, :]
            eclb = ar["eclb"][par]
            self.cp("pool", ecl[:, 0:nch], T4.t[:, CH - 1:nn:CH], rd=[T4.b], wr=[eclb])
            T1.free(); T2.free(); T3.free(); T4.free()
            va, Kst, Vst = ar["va"][par], ar["Kst"], ar["Vst"]
            vab = ar["vab"][par]
            qk = []
            for t in range(TG):
                yield
                bt = self.banks.alloc()
                cols = slice(t0 + t * 128, t0 + (t + 1) * 128)
                for c in range(KC):
                    self.mm(bt.t[:, 0:256], hT[:, c, cols], w3.t[:, c * 256:(c + 1) * 256], c == 0, c == KC - 1,
                            rd=[w3.b, hb], wr=[bt.b])
                for c in range(KC):
                    self.mm(bt.t[:, 256:512], hT[:, c, cols], w4.t[:, c * 256:(c + 1) * 256], c == 0, c == KC - 1,
                            rd=[w4.b, hb], wr=[bt.b])
                self.cp("act", va[:, t, :], bt.t[:, 0:128], rd=[bt.b], wr=[vab[t]])
                ss = ar["ss"][:, t, :]
                ssb = ar["ssb"][t]
                junk = self.Hh.alloc()
                self.act(junk.t[:, 0:128], bt.t[:, 128:256], AF.Square, rd=[bt.b], wr=[junk.b, ssb], accum_out=ss[:, 0:1])
                self.act(junk.t[:, 128:256], bt.t[:, 256:384], AF.Square, rd=[bt.b], wr=[junk.b, ssb], accum_out=ss[:, 1:2])
                junk.free()
                self.cp("act", Vst[:, t, :], bt.t[:, 384:512], rd=[bt.b], wr=[ar["Vstb"][t]])
                yield
                self.act(ss[:, 2:4], ss[:, 0:2], AF.Ln, rd=[ssb], wr=[ssb], scale=1.0 / DH, bias=EPS)
                self.act(ss[:, 2:4], ss[:, 2:4], AF.Exp, rd=[ssb], wr=[ssb], scale=-0.5)
                if t % 2 == 0:
                    qk.append(self.Hh.alloc())
                qs = qk[-1]
                o0 = (t % 2) * 256
                self.stt("dve", qs.t[:, o0:o0 + 128], bt.t[:, 128:256], ss[:, 2:3], self.gqk[:, 0, :], ALU.mult, ALU.mult,
                         rd=[bt.b, ssb, cb], wr=[qs.b])
                self.stt("dve", Kst[:, t, :], bt.t[:, 256:384], ss[:, 3:4], self.gqk[:, 1, :], ALU.mult, ALU.mult,
                         rd=[bt.b, ssb, cb], wr=[ar["Kstb"][t]])
                bt.free()
                self.cp("pool", qs.t[:, o0 + 128:o0 + 256], Kst[:, t, :], rd=[ar["Kstb"][t], qs.b], wr=[qs.b])
            s.update(kt=kt, qt=qt, T5=T5, qk=qk, par=par)
            if g == NGp - 1:
                for w in (w1, w2, w3, w4):
                    w.free()
                if NGp > 2 and h + 1 < H and "nopf" not in DBG:
                    load_head(h + 1)

        def P1b(n):
            h, g = iters[n]
            s = st[n]
            t0, nn = groups[g]
            TG = nn // 128
            kt = s["kt"]
            ktm, Kst, Vst = ar["ktm"], ar["Kst"], ar["Vst"]
            QT = self.Hh.alloc()
            for t in range(TG):
                gt = (t0 // 128) + t
                qs = s["qk"][t // 2]
                o0 = (t % 2) * 256
                self.cp("pool", Vb[:, gt, :], Vst[:, t, :], rd=[ar["Vstb"][t]], wr=[kvb[gt]])
                bx = self.banks.alloc()
                bxv = bx.t[:].bitcast(BF16)
                self.tp(bxv[:, 0:128], qs.t[:, o0:o0 + 128], True, rd=[qs.b], wr=[bx.b])
                self.tp(bxv[:, 128:256], qs.t[:, o0 + 128:o0 + 256], True, rd=[qs.b], wr=[bx.b])
                self.tp(bxv[:, 256:384], kt.t[:, t * 128:(t + 1) * 128], True, rd=[kt.b], wr=[bx.b])
                self.cp("act", QT.t[:, t * 128:(t + 1) * 128], bxv[:, 0:128], rd=[bx.b], wr=[QT.b])
                self.cp("dve", KT[:, gt * 128:(gt + 1) * 128], bxv[:, 128:256], rd=[bx.b], wr=[kvb[gt]])
                self.cp("dve", ktm[:, t, :], bxv[:, 256:384], rd=[bx.b], wr=[ar["ktmb"][t]])
                bx.free()
            for qs in s["qk"]:
                qs.free()
            s["QT"] = QT
            kdst = (O_["ks"] if sample else O_["kp"][seq_out])
            vdst = (O_["vs"] if sample else O_["vp"][seq_out])
            P.dma("act", kdst[t0:t0 + nn, h, :].rearrange("(t p) d -> p t d", p=128), Kst[:, 0:TG, :], reads=ar["Kstb"][0:TG])
            P.dma("act", vdst[t0:t0 + nn, h, :].rearrange("(t p) d -> p t d", p=128), Vst[:, 0:TG, :], reads=ar["Vstb"][0:TG])

        def P2a(n):
            h, g = iters[n]
            s = st[n]
            t0, nn = groups[g]
            par = s["par"]
            nch = nn // CH
            va, vab, ktm = ar["va"][par], ar["vab"][par], ar["ktm"]
            ecl, eclb = ar["ecl"][:, par, :], ar["eclb"][par]
            Sbf, Sbfb, S, Sb = ar["Sbf"], ar["Sbfb"], ar["S"], ar["Sb"]
            if sample:
                S0, S0b, S0bf, S0bfb = ar["S0"], ar["S0b"], ar["S0bf"], ar["S0bfb"]
                ktmm = ar["ktmm"]
                for b in range(4):
                    P.dma("sp", S0[:, b, :], I["sh"][b, h], writes=[S0b[b]])
                    self.cp("pool", S0bf[:, b, :], S0[:, b, :], rd=[S0b[b]], wr=[S0bfb[b]])
                    self.ts1("dve", ktmm[:, b, :], ktm[:, 0, :], self.cA[:, C_RMS + b:C_RMS + b + 1], ALU.mult,
                             rd=[ar["ktmb"][0], cb], wr=[ar["ktmmb"]])
            bDs = []
            if sample:
                dmap = lambda i: (i // 4, i % 4)
                nbD = (nch + 3) // 4
            else:
                dmap = lambda i: ((i % cpt) + cpt * ((i // cpt) // 4), (i // cpt) % 4)
                nbD = cpt * ((nch // cpt + 3) // 4)

            def scan_step(i):
                bD = bDs[dmap(i)[0]]
                dcols = slice(dmap(i)[1] * 128, (dmap(i)[1] + 1) * 128)
                if sample:
                    Sn = ar["Sn"]
                    self.tt("dve", Sn[:, i, :], bD.t[:, dcols], S0[:, i, :], ALU.add, rd=[bD.b, S0b[i]], wr=[ar["Snb"][i]])
                    self.ts1("dve", Sn[:, i, :], Sn[:, i, :], ecl[:, i:i + 1], ALU.mult, rd=[eclb, ar["Snb"][i]], wr=[ar["Snb"][i]])
                    P.dma("act", O_["hs"][i, h], Sn[:, i, :], reads=[ar["Snb"][i]])
                else:
                    gi = (t0 // CH) + i
                    cur, prv = gi % 2, (gi + 1) % 2
                    if gi == 0:
                        self.ts1("dve", S[:, cur, :], bD.t[:, dcols], ecl[:, i:i + 1], ALU.mult, rd=[bD.b, eclb], wr=[Sb[cur]])
                    else:
                        self.ts1("dve", S[:, cur, :], S[:, prv, :], ecl[:, i:i + 1], ALU.mult, rd=[Sb[prv], eclb], wr=[Sb[cur]])
                        self.stt("dve", S[:, cur, :], bD.t[:, dcols], ecl[:, i:i + 1], S[:, cur, :], ALU.mult, ALU.add,
                                 rd=[bD.b, eclb, Sb[cur]], wr=[Sb[cur]])
                    self.cp("dve", Sbf[:, i + 1, :], S[:, cur, :], rd=[Sb[cur]], wr=[Sbfb[i + 1]])
                    if gi == self.SEQ // CH - 1:
                        P.dma("act", O_["hp"][seq_out, h], S[:, cur, :], reads=[Sb[cur]])
            for _ in range(nbD):
                bDs.append(self.banks.alloc())
            for i in range(nch):
                bD = bDs[dmap(i)[0]]
                t, pb = i // cpt, (i % cpt) * CH
                dcols = slice(dmap(i)[1] * 128, (dmap(i)[1] + 1) * 128)
                if sample:
                    self.mm(bD.t[:, dcols], ktmm[:, i, :], va[:, 0, :], True, True, rd=[ar["ktmmb"], vab[0]], wr=[bD.b])
                else:
                    self.mm(bD.t[:, dcols], ktm[pb:pb + CH, t, :], va[pb:pb + CH, t, :], True, True,
                            rd=[ar["ktmb"][t], vab[t]], wr=[bD.b])
                if "nodf" in DBG:
                    scan_step(i)
            if "nodf" not in DBG:
                for i in range(nch):
                    scan_step(i)
            for bD in bDs:
                bD.free()

        def PA(n):
            h, g = iters[n]
            s = st[n]
            t0, nn = groups[g]
            TG = nn // 128
            kt, qt = s["kt"], s["qt"]
            bA = self.banks.alloc()
            ATm = self.Hh.alloc()
            for t in range(TG):
                cs = slice(t * 128, (t + 1) * 128)
                self.mm(bA.t[:, cs], kt.t[:, cs], qt.t[:, cs], True, True, rd=[kt.b, qt.b], wr=[bA.b])
            self.tt("dve", ATm.t[:, 0:nn].rearrange("p (t c) -> p t c", c=128), bA.t[:, 0:nn].rearrange("p (t c) -> p t c", c=128),
                    hmask.unsqueeze(1).to_broadcast([128, TG, 128]), ALU.mult, rd=[bA.b, cb], wr=[ATm.b])
            bA.free()
            kt.free()
            s["ATm"] = ATm

        def PO(n):
            h, g = iters[n]
            s = st[n]
            t0, nn = groups[g]
            TG = nn // 128
            par = s["par"]
            nch = nn // CH
            qt, T5, ATm = s["qt"], s["T5"], s["ATm"]
            va, vab = ar["va"][par], ar["vab"][par]
            Sbf, Sbfb = ar["Sbf"], ar["Sbfb"]
            bO = self.banks.alloc()
            for t in range(TG):
                cs = slice(t * 128, (t + 1) * 128)
                self.mm(bO.t[:, cs], va[:, t, :], ATm.t[:, cs], True, False, rd=[vab[t], ATm.b], wr=[bO.b])
                for ci in range(cpt):
                    i = t * cpt + ci
                    ccs = slice(i * CH, (i + 1) * CH)
                    lastmm = ci == cpt - 1
                    if sample:
                        self.mm(bO.t[:, ccs], ar["S0bf"][:, i, :], qt.t[:, ccs], False, lastmm,
                                rd=[ar["S0bfb"][i], qt.b], wr=[bO.b])
                    else:
                        gi = (t0 // CH) + i
                        if gi == 0:
                            continue
                        self.mm(bO.t[:, ccs], Sbf[:, i, :], qt.t[:, ccs], False, lastmm, rd=[Sbfb[i], qt.b], wr=[bO.b])
            ATm.free()
            qt.free()
            if not sample and g < NGp - 1:
                self.cp("dve", Sbf[:, 0, :], Sbf[:, nch, :], rd=[Sbfb[nch]], wr=[Sbfb[0]])
            sq = self.Hh.alloc()
            self.act(sq.t[:, 0:nn], bO.t[:, 0:nn], AF.Square, rd=[bO.b], wr=[sq.b])
            bS = self.banks.alloc()
            self.mm(bS.t[:, 0:nn], self.onesb[:], sq.t[:, 0:nn], True, True, rd=[sq.b, cb], wr=[bS.b])
            sq.free()
            rs = self.rstd_bc(bS, nn, DH, explog=True)
            bS.free()
            self.tt("dve", rs.t[:, 0:nn], bO.t[:, 0:nn], rs.t[:, 0:nn], ALU.mult, rd=[bO.b, rs.b], wr=[rs.b])
            bO.free()
            mixa = self.Hh.alloc()
            self.stt("dve", mixa.t[:, 0:nn], rs.t[:, 0:nn], self.prm[:, 64 + h:65 + h], T5.t[:, 0:nn], ALU.mult, ALU.mult,
                     rd=[rs.b, T5.b, cb], wr=[mixa.b])
            rs.free()
            T5.free()
            s["mixa"] = mixa

        def P3(n, filler=None):
            h, g = iters[n]
            s = st[n]
            t0, nn = groups[g]
            QT = s["QT"]
            mixb = self.Hh.alloc()
            if not sample:
                R = self.FR.alloc()
                Ob = self.banks.alloc()
                kts = list(range((t0 + nn) // 128 - 1, -1, -1))
                g0 = t0 // 128
                pairs = []
                for kbi in kts:
                    kd = kbi - g0
                    if kd >= 0:
                        c0, cn, mask = kd * 128, (kd + 1) * 128, self.amask[:, 0, 0:128]
                    else:
                        c0, cn, mask = 0, 0, None
                    pairs.append(([(c0, nn, KT[:, kbi * 128:(kbi + 1) * 128], kvb[kbi])], QT.t[:, 0:nn], QT.b,
                                  [(c0, nn, Vb[:, kbi, :], kvb[kbi])], 128, nn, mask, c0, cn))
                self.attn_pairs(pairs, R, Ob, filler)
                self.cp("act", mixb.t[:, 0:nn], Ob.t[:, 0:nn], rd=[Ob.b], wr=[mixb.b])
                Ob.free()
                R.free()
            else:
                npt = self.PAST // 128
                vk = lambda ap: ap.rearrange("p (t d) -> p t d", d=128)
                vcs = []
                for b in range(4):
                    kc = self.wload([(0, npt * 128, vk, I["ck"][b, :, h, :].rearrange("(t p) d -> p t d", p=128))], npt * 128, "dve")
                    vcs.append(self.wload([(0, npt * 128, vk, I["cv"][b, :, h, :].rearrange("(t p) d -> p t d", p=128))], npt * 128, "dve"))
                    KTp, KTpb = ar["KTp"][b], ar["KTpb"][b]
                    for t8 in range(0, npt, 8):
                        bx = self.banks.alloc()
                        bxv = bx.t[:].bitcast(BF16)
                        m = min(8, npt - t8)
                        for j in range(m):
                            self.tp(bxv[:, j * 128:(j + 1) * 128], kc.t[:, (t8 + j) * 128:(t8 + j + 1) * 128], True,
                                    rd=[kc.b], wr=[bx.b])
                        self.cp("act" if (t8 // 8) % 2 == 0 else "dve", KTp[:, t8 * 128:(t8 + m) * 128], bxv[:, 0:m * 128],
                                rd=[bx.b], wr=[KTpb])
                        bx.free()
                    kc.free()
                R = self.FR.alloc()
                Ob = self.banks.alloc()
                pairs = [([(0, 128, KT[:, 0:128], kvb[0])], QT.t[:, 0:128], QT.b, [(0, 128, Vb[:, 0, :], kvb[0])], 128, 128,
                          self.amask_s[:].rearrange("p a b -> p (a b)"), 0, 128)]
                for kbi in range(npt - 1, -1, -1):
                    zs = [(b * 32, (b + 1) * 32, ar["KTp"][b][:, kbi * 128:(kbi + 1) * 128], ar["KTpb"][b]) for b in range(4)]
                    avs = [(b * 32, (b + 1) * 32, vcs[b].t[:, kbi * 128:(kbi + 1) * 128], vcs[b].b) for b in range(4)]
                    pairs.append((zs, QT.t[:, 0:128], QT.b, avs, 128, 128, None, 0, 0))
                self.attn_pairs(pairs, R, Ob, filler)
                self.cp("act", mixb.t[:, 0:128], Ob.t[:, 0:128], rd=[Ob.b], wr=[mixb.b])
                Ob.free()
                R.free()
                for vc in vcs:
                    vc.free()
            QT.free()
            s["mixb"] = mixb

        def P4(n):
            h, g = iters[n]
            s = st[n]
            t0, nn = groups[g]
            wo = wts[h][4]
            mixa, mixb = s["mixa"], s["mixb"]
            for d in range(KC):
                bk = self.banks.alloc()
                self.mm(bk.t[:, 0:nn], wo.t[:, d * 128:(d + 1) * 128], mixa.t[:, 0:nn], True, False, rd=[wo.b, mixa.b], wr=[bk.b])
                self.mm(bk.t[:, 0:nn], wo.t[:, 1024 + d * 128:1024 + (d + 1) * 128], mixb.t[:, 0:nn], False, True,
                        rd=[wo.b, mixb.b], wr=[bk.b])
                self.stt("dve", xT[:, d, t0:t0 + nn], bk.t[:, 0:nn], 1.0, xT[:, d, t0:t0 + nn], ALU.mult, ALU.add,
                         rd=[bk.b, ctx["xb"][g][d]], wr=[ctx["xb"][g][d]])
                bk.free()
            mixa.free()
            mixb.free()
            if g == NGp - 1:
                wo.free()

        for _ in P1a(0):
            pass
        P1b(0)
        for n in range(NI):
            nxt = n + 1 < NI
            gen = P1a(n + 1) if nxt else iter(())
            if "tmlate" in DBG:
                for _ in range(3):
                    next(gen, None)
            elif "fill" not in DBG:
                for _ in gen:
                    pass
            else:
                next(gen, None)
            if "v3" in DBG:
                PA(n)
                P2a(n)
            else:
                P2a(n)
                PA(n)
            if n > 0 and "p4late" not in DBG:
                P4(n - 1)
            P3(n, gen if "fill" in DBG else None)
            for _ in gen:
                pass
            if "v2" in DBG:
                if nxt:
                    P1b(n + 1)
                PO(n)
            else:
                PO(n)
                if nxt:
                    P1b(n + 1)
            if n > 0 and "p4late" in DBG:
                P4(n - 1)
        P4(NI - 1)

    def run_pass(self, ctx, src, ydst, sample, seq_out, ar, wsT, BT, io_ring):
        I = self.i
        dbg = DBG
        mk = self.P.mark
        mk("load_x")
        self.load_x(ctx, src, io_ring)
        for l in range(2):
            if "ffn1" in dbg:
                mk(f"L{l}.norm1")
                self.rmsnorm(ctx, 0 + l * 8)
                mk(f"L{l}.ffn1")
                self.ffn(ctx, I["ffn1_w_gate"][l], I["ffn1_w_up"][l], I["ffn1_w_down"][l])
            if "norm" in dbg:
                mk(f"L{l}.normM")
                self.rmsnorm(ctx, 16 + l * 8)
            self.P.barrier()
            mk(f"L{l}.mixer")
            if l == 0:
                if "ab" in dbg:
                    self.mixer_ab(ctx, sample, ar, seq_out)
            else:
                if "c" in dbg:
                    self.mixer_c(ctx, sample, wsT, BT, ar)
            self.P.barrier()
            if "ffn2" in dbg:
                mk(f"L{l}.norm2")
                self.rmsnorm(ctx, 32 + l * 8)
                mk(f"L{l}.ffn2")
                self.ffn(ctx, I["ffn2_w_gate"][l], I["ffn2_w_up"][l], I["ffn2_w_down"][l])
        mk("store_y")
        self.store_y(ctx, ydst, io_ring)
        mk("end")

    def build(self):
        P = self.P
        SEQ, NP = self.SEQ, self.NP
        with contextlib.ExitStack() as stk:
            GN = min(512, SEQ)
            NG = SEQ // GN
            xT = P.sb("xT", [128, KC, SEQ], stack=stk)
            hT = P.sb("hT", [128, KC, SEQ], BF16, stack=stk)
            arena = P.sb("arena", [128, 4800], stack=stk)
            wsT = P.sb("wsT", [128, 8, 128], BF16, stack=stk)
            BT = P.sb("BT", [128, 16, 128], stack=stk)
            with contextlib.ExitStack() as tmp:
                if "nog" not in DBG:
                    tmps = [arena[:, k * 1024:(k + 1) * 1024].rearrange("p (g j) -> p g j", g=8) for k in range(3)]
                    self.gmlp_consts(False, wsT, BT, tmp, tmps)
                P.barrier()
            off = [0]

            def carve(nf32, dt=F32, shape=None):
                ap = arena[:, off[0]:off[0] + nf32]
                off[0] += nf32
                if dt == BF16:
                    ap = ap.bitcast(BF16)
                return ap

            TGm = GN // 128
            nchm = GN // 64
            ar = {}
            ar["KT"] = carve(SEQ // 2, BF16)
            ar["Vb"] = carve(SEQ // 2, BF16).rearrange("p (t d) -> p t d", d=128)
            ar["kvb"] = [Buf() for _ in range(SEQ // 128)]
            ar["va"] = [carve(TGm * 64, BF16).rearrange("p (t d) -> p t d", d=128) for _ in range(2)]
            ar["vab"] = [[Buf() for _ in range(TGm)] for _ in range(2)]
            ar["ktm"] = carve(TGm * 64, BF16).rearrange("p (t d) -> p t d", d=128); ar["ktmb"] = [Buf() for _ in range(TGm)]
            ar["Sbf"] = carve((nchm + 1) * 64, BF16).rearrange("p (t d) -> p t d", d=128)
            ar["Sbfb"] = [Buf() for _ in range(nchm + 1)]
            ar["Kst"] = carve(TGm * 128).rearrange("p (t d) -> p t d", d=128); ar["Kstb"] = [Buf() for _ in range(TGm)]
            ar["Vst"] = carve(TGm * 128).rearrange("p (t d) -> p t d", d=128); ar["Vstb"] = [Buf() for _ in range(TGm)]
            ar["S"] = carve(256).rearrange("p (t d) -> p t d", d=128); ar["Sb"] = [Buf(), Buf()]
            ar["ecl"] = carve(32).rearrange("p (a c) -> p a c", a=2); ar["eclb"] = [Buf(), Buf()]
            ar["ss"] = carve(4 * TGm).rearrange("p (t a) -> p t a", a=4); ar["ssb"] = [Buf() for _ in range(TGm)]
            ab_end = off[0]
            off[0] = 0
            ar["vn"] = carve(TGm * 1024, BF16).rearrange("p (t f) -> p t f", f=DC)
            ar["vnb"] = [Buf() for _ in range(TGm)]
            ar["stats"] = carve(TGm * 16).rearrange("p (t a u) -> p t a u", a=2, u=8); ar["stb"] = Buf()
            ar["mv"] = carve(TGm * 4).rearrange("p (t a) -> p t a", a=4)
            c_end = off[0]
            io_ring = Ring([arena[:, k * 1024:(k + 1) * 1024] for k in range(4)], "io")
            assert max(ab_end, c_end, 4096) <= 4800, (ab_end, c_end)
            for s in range(NP if "prompt" in DBG else 0):
                ctx = dict(xT=xT, hT=hT, T=SEQ, GN=GN, groups=[(g * GN, GN) for g in range(NG)],
                           xb=[[Buf() for _ in range(KC)] for _ in range(NG)], hb=[Buf() for _ in range(NG)])
                P.barrier(engines=ENGS, dma_queues=("sp", "act", "pool"))
                self.run_pass(ctx, self.i["xp"][s], self.o["yp"][s], False, s, ar, wsT, BT, io_ring)
        P.barrier(engines=ENGS, dma_queues=("sp", "act", "pool"))
        with contextlib.ExitStack() as stk:
            xT = P.sb("xTs", [128, KC, 128], stack=stk)
            hT = P.sb("hTs", [128, KC, 128], BF16, stack=stk)
            wsT = P.sb("wsTs", [128, 8, 128], BF16, stack=stk)
            BT = P.sb("BTs", [128, 16, 128], stack=stk)
            with contextlib.ExitStack() as tmp:
                if "nogs" not in DBG:
                    self.gmlp_consts(True, wsT, BT, tmp)
                P.barrier()
            ar = {}
            sbt = lambda name, shape, dt=F32: P.sb("s_" + name, shape, dt, stack=stk)
            self.wbf.extend([sbt(f"xwbf{i}", [128, 2048], BF16) for i in range(6)])
            self.wst.extend([sbt(f"xwst{i}", [128, 2048]) for i in range(2)])
            self.wst_parts += [[Buf() for _ in range(4)] for _ in range(2)]
            self.Hh.extend([sbt(f"xH{i}", [128, 512], BF16) for i in range(1)])
            ar["KT"] = sbt("KT", [128, 128], BF16)
            ar["Vb"] = sbt("Vb", [128, 1, 128], BF16); ar["kvb"] = [Buf()]
            ar["va"] = [sbt("va0", [128, 1, 128], BF16), sbt("va1", [128, 1, 128], BF16)]; ar["vab"] = [[Buf()], [Buf()]]
            ar["ktm"] = sbt("ktm", [128, 1, 128], BF16); ar["ktmb"] = [Buf()]
            ar["ktmm"] = sbt("ktmm", [128, 4, 128], BF16); ar["ktmmb"] = Buf()
            ar["Sbf"] = None; ar["Sbfb"] = None; ar["S"] = None; ar["Sb"] = None
            ar["Kst"] = sbt("Kst", [128, 1, 128]); ar["Kstb"] = [Buf()]
            ar["Vst"] = sbt("Vst", [128, 1, 128]); ar["Vstb"] = [Buf()]
            ar["ecl"] = sbt("ecl", [128, 2, 16]); ar["eclb"] = [Buf(), Buf()]
            ar["ss"] = sbt("ss", [128, 1, 4]); ar["ssb"] = [Buf()]
            ar["S0"] = sbt("S0", [128, 4, 128]); ar["S0b"] = [Buf() for _ in range(4)]
            ar["S0bf"] = sbt("S0bf", [128, 4, 128], BF16); ar["S0bfb"] = [Buf() for _ in range(4)]
            ar["Sn"] = sbt("Sn", [128, 4, 128]); ar["Snb"] = [Buf() for _ in range(4)]
            ar["KTp"] = [sbt(f"KTp{b}", [128, self.PAST], BF16) for b in range(4)]; ar["KTpb"] = [Buf() for _ in range(4)]
            ar["vn"] = sbt("vn", [128, 1, DC]); ar["vnb"] = [Buf()]
            ar["stats"] = sbt("stats", [128, 1, 2, 8]); ar["stb"] = Buf()
            ar["mv"] = sbt("mv", [128, 1, 4])
            ar["vout"] = sbt("vout", [128, DC]); ar["voutb"] = Buf()
            ar["vhb"] = sbt("vhb", [128, 1, DC], BF16); ar["vhbb"] = Buf()
            ar["lng"] = sbt("lng", [128, DC]); ar["lnb"] = sbt("lnb", [128, DC])
            P.dma("sp", ar["lng"][:, :], self.i["c_ln_g"][0:1, :].partition_broadcast(128), reads=[self.cb], writes=[self.cb])
            P.dma("sp", ar["lnb"][:, :], self.i["c_ln_b"][0:1, :].partition_broadcast(128), reads=[self.cb], writes=[self.cb])
            io_ring = Ring([sbt("io0", [128, 1024]), sbt("io1", [128, 1024])], "ios")
            ctx = dict(xT=xT, hT=hT, T=128, GN=128, groups=[(0, 128)],
                       xb=[[Buf() for _ in range(KC)]], hb=[Buf()])
            if "sample" in DBG:
                self.run_pass(ctx, self.i["xs"], self.o["ys"], True, None, ar, wsT, BT, io_ring)
            P.barrier()
        P.finish()
        if _os.environ.get("KMARKS"):
            import json
            json.dump(P.marks, open(_os.environ["KMARKS"], "w"))
        return self.nc


_CACHE = {}


def _get_nc(key):
    if key not in _CACHE:
        _CACHE[key] = Builder(*key).build()
    return _CACHE[key]


WEIGHT_KEYS = ["ffn1_norm", "ffn1_w_gate", "ffn1_w_up", "ffn1_w_down", "mix_norm", "ffn2_norm", "ffn2_w_gate",
               "ffn2_w_up", "ffn2_w_down", "ab_w_in", "ab_lb", "ab_g_out", "ab_g_q", "ab_g_k", "ab_w_out", "c_w_in",
               "c_ln_g", "c_ln_b", "c_w_s", "c_b_s", "c_w_out"]


def kernel(x_prompt, x_sample, cache_sb_k, cache_sb_v, state_hgrn, n_cores=8, **w):
    x_prompt = np.asarray(x_prompt, np.float32)
    x_sample = np.asarray(x_sample, np.float32)
    B, SEQ, _ = x_prompt.shape
    BS, DS, _ = x_sample.shape
    PAST = cache_sb_k.shape[2]
    NP, NS = B // n_cores, BS // n_cores
    nc = _get_nc((NP, SEQ, NS, DS, PAST))
    consts = make_consts()
    wts = {k: np.ascontiguousarray(np.asarray(w[k], np.float32)) for k in WEIGHT_KEYS}
    in_maps = []
    for i in range(n_cores):
        m = dict(wts)
        m["consts"] = consts
        m["xp"] = np.ascontiguousarray(x_prompt[i * NP:(i + 1) * NP])
        m["xs"] = np.ascontiguousarray(x_sample[i * NS:(i + 1) * NS]).reshape(NS * DS, D)
        m["ck"] = np.ascontiguousarray(np.asarray(cache_sb_k, np.float32)[0, i * NS:(i + 1) * NS])
        m["cv"] = np.ascontiguousarray(np.asarray(cache_sb_v, np.float32)[0, i * NS:(i + 1) * NS])
        m["sh"] = np.ascontiguousarray(np.asarray(state_hgrn, np.float32)[0, i * NS:(i + 1) * NS])
        in_maps.append(m)
    res = run_bass_kernel_spmd(nc, in_maps, core_ids=list(range(n_cores)))
    R = res.results
    cat = lambda k: np.concatenate([np.asarray(r[k]) for r in R], axis=0)
    y_prompt = cat("yp")
    y_sample = cat("ys").reshape(BS, DS, D)
    kp = cat("kp")[None]
    vp = cat("vp")[None]
    hp = cat("hp")[None]
    ks = cat("ks").reshape(BS, DS, H, DH)[None]
    vs = cat("vs").reshape(BS, DS, H, DH)[None]
    hs = cat("hs")[None]
    gv = cat("gv").reshape(BS, DS, DC)[None]
    return (y_prompt, y_sample, kp, vp, hp, ks, vs, hs, gv)
```
